# Optimizing a Trainium2 kernel written in Bass

```python
import math
import jax, jax.numpy as jnp
from jax import lax
import numpy as np

D_MODEL = 1024
BATCH = 8
SEQ = 2048
DEPTH = 2

PLE_DIM = 256
MIX_WIDTH = D_MODEL
GROUP_WIDTH = MIX_WIDTH // 4
CHUNK = 64
EPS = 1e-6

GLA_HEADS = 4
GLA_DK = GROUP_WIDTH // (2 * GLA_HEADS)
GLA_DV = GROUP_WIDTH // GLA_HEADS
GLA_RANK = 16
GLA_GATE_NORM = 16.0

S5_CH = 16
S5_GROUPS = GROUP_WIDTH // S5_CH
S5_STATE = 64
S5_DT_MIN = 1e-3
S5_DT_MAX = 1e-1

RET_HEADS = 4
RET_DK = GROUP_WIDTH // RET_HEADS
RET_DV = GROUP_WIDTH // RET_HEADS
ROPE_BASE = 10000.0

GDN_HEADS = 4
GDN_DK = GROUP_WIDTH // GDN_HEADS
GDN_DV = GROUP_WIDTH // GDN_HEADS
GDN_CONV = 4
GDN_DT_MIN = 1e-3
GDN_DT_MAX = 1e-1

FFN_HIDDEN = -(-8 * D_MODEL // (3 * 256)) * 256

IN_SPLITS = (
    GLA_HEADS * GLA_DK, GLA_HEADS * GLA_DK, GLA_HEADS * GLA_DV, GLA_RANK, GROUP_WIDTH,
    GROUP_WIDTH,
    RET_HEADS * RET_DK, RET_HEADS * RET_DK, RET_HEADS * RET_DV, GROUP_WIDTH,
    GDN_HEADS * GDN_DK, GDN_HEADS * GDN_DK, GDN_HEADS * GDN_DV, GDN_HEADS, GDN_HEADS, GROUP_WIDTH,
)
IN_WIDTH = sum(IN_SPLITS)

kernel_name = "hybrid_parallel_gla_s5_retnet_gdn"

F32 = jnp.float32


def rms_norm(x, g):
    xf = x.astype(F32)
    y = xf * lax.rsqrt(jnp.mean(xf * xf, axis=-1, keepdims=True) + EPS)
    return (y * g.astype(F32)).astype(x.dtype)


def head_rms_norm(x, g):
    return x * lax.rsqrt(jnp.mean(x * x, axis=-1, keepdims=True) + EPS) * g.astype(F32)


def l2_norm(x):
    return x * lax.rsqrt(jnp.sum(x * x, axis=-1, keepdims=True) + EPS)


def to_chunks(x):
    B, L, H, d = x.shape
    return x.reshape(B, L // CHUNK, CHUNK, H, d).transpose(0, 3, 1, 2, 4)


def from_chunks(x):
    B, H, nc, C, d = x.shape
    return x.transpose(0, 2, 3, 1, 4).reshape(B, nc * C, H, d)


def split_columns(z):
    offs = np.cumsum(IN_SPLITS)[:-1].tolist()
    return jnp.split(z, offs, axis=-1)


def scan_chunk_states(d_state, decay):
    ds = jnp.moveaxis(d_state, 2, 0)
    dc = jnp.moveaxis(decay, 2, 0)

    def step(S, inp):
        inc, a = inp
        return S * a + inc, S

    _, s_prev = lax.scan(step, jnp.zeros_like(ds[0]), (ds, dc))
    return jnp.moveaxis(s_prev, 0, 2)


def rope_tables(positions):
    inv_freq = ROPE_BASE ** (-jnp.linspace(0.0, 1.0, RET_DK // 2, dtype=F32))
    ang = positions.astype(F32)[..., None] * inv_freq
    return jnp.cos(ang)[:, :, None, :], jnp.sin(ang)[:, :, None, :]


def apply_rope(x, cos, sin):
    x1, x2 = jnp.split(x, 2, axis=-1)
    return jnp.concatenate([x1 * cos - x2 * sin, x1 * sin + x2 * cos], axis=-1)


def causal_depthwise_conv(x, w):
    K, C = w.shape
    return lax.conv_general_dilated(x, w[:, None, :], window_strides=(1,), padding=[(K - 1, 0)],
                                    dimension_numbers=("NWC", "WIO", "NWC"), feature_group_count=C)


def gla_mixer(q, k, v, a_low, r, w_a2, b_a, norm_g):
    B, L = q.shape[:2]
    q = q.astype(F32).reshape(B, L, GLA_HEADS, GLA_DK) * GLA_DK ** -0.5
    k = k.astype(F32).reshape(B, L, GLA_HEADS, GLA_DK)
    v = v.astype(F32).reshape(B, L, GLA_HEADS, GLA_DV)
    gk = jax.nn.log_sigmoid(a_low.astype(F32) @ w_a2.astype(F32) + b_a.astype(F32)) / GLA_GATE_NORM
    gk = gk.reshape(B, L, GLA_HEADS, GLA_DK)
    qc, kc, vc, gc = to_chunks(q), to_chunks(k), to_chunks(v), to_chunks(gk)
    b = jnp.cumsum(gc, axis=3)
    b_last = b[:, :, :, -1:, :]
    q_t = qc * jnp.exp(b)
    k_t = kc * jnp.exp(-b)
    k_end = kc * jnp.exp(b_last - b)
    causal = jnp.tril(jnp.ones((CHUNK, CHUNK), dtype=bool))
    attn = jnp.where(causal, jnp.einsum('bhncd,bhnsd->bhncs', q_t, k_t), 0.0)
    o = jnp.einsum('bhncs,bhnsv->bhncv', attn, vc)
    d_state = jnp.einsum('bhnsd,bhnsv->bhndv', k_end, vc)
    s_prev = scan_chunk_states(d_state, jnp.swapaxes(jnp.exp(b_last), -1, -2))
    o = o + jnp.einsum('bhncd,bhndv->bhncv', q_t, s_prev)
    o = head_rms_norm(from_chunks(o), norm_g)
    return o.reshape(B, L, GLA_HEADS * GLA_DV) * jax.nn.silu(r.astype(F32))


def s5_mixer(u, lam_re, lam_im, log_dt, b_re, b_im, c_re, c_im, d_skip, w_glu, b_glu):
    B, L, _ = u.shape
    uf = u.astype(F32).reshape(B, L, S5_GROUPS, S5_CH)
    lam = lax.complex(jnp.minimum(lam_re.astype(F32), -1e-4), lam_im.astype(F32))
    dt = jnp.exp(log_dt.astype(F32))[:, None]
    a_bar = jnp.exp(lam * dt)
    b_bar = ((a_bar - 1.0) / lam)[..., None] * lax.complex(b_re.astype(F32), b_im.astype(F32))
    bu = jnp.einsum('blgh,gph->blgp', uf, b_bar)
    a_elems = jnp.broadcast_to(a_bar, (L,) + a_bar.shape)[None]

    def combine(e1, e2):
        a1, s1 = e1
        a2, s2 = e2
        return a2 * a1, a2 * s1 + s2

    _, states = lax.associative_scan(combine, (a_elems, bu), axis=1)
    c = lax.complex(c_re.astype(F32), c_im.astype(F32))
    y = jnp.real(jnp.einsum('blgp,ghp->blgh', states, c)) + d_skip.astype(F32).reshape(S5_GROUPS, S5_CH) * uf
    y = jax.nn.gelu(y.reshape(B, L, GROUP_WIDTH))
    return y * jax.nn.sigmoid(y @ w_glu.astype(F32) + b_glu.astype(F32))


def retention_mixer(q, k, v, g, cos, sin, norm_g):
    B, L = q.shape[:2]
    q = apply_rope(q.astype(F32).reshape(B, L, RET_HEADS, RET_DK), cos, sin)
    k = apply_rope(k.astype(F32).reshape(B, L, RET_HEADS, RET_DK), cos, sin) * RET_DK ** -0.5
    v = v.astype(F32).reshape(B, L, RET_HEADS, RET_DV)
    log_gamma = jnp.log1p(-(2.0 ** (-5.0 - jnp.arange(RET_HEADS, dtype=F32))))
    qc, kc, vc = to_chunks(q), to_chunks(k), to_chunks(v)
    nc = qc.shape[2]
    idx = jnp.arange(CHUNK, dtype=F32)
    rel = idx[:, None] - idx[None, :]
    dmask = jnp.where(rel >= 0, jnp.exp(jnp.maximum(rel, 0.0) * log_gamma[:, None, None]), 0.0)
    inner = jnp.einsum('bhncs,bhnsv->bhncv', jnp.einsum('bhncd,bhnsd->bhncs', qc, kc) * dmask[None, :, None], vc)
    xi = jnp.exp((idx + 1.0) * log_gamma[:, None])
    zeta = jnp.exp((CHUNK - 1.0 - idx) * log_gamma[:, None])
    d_state = jnp.einsum('bhnsd,bhnsv->bhndv', kc * zeta[None, :, None, :, None], vc)
    chunk_decay = jnp.broadcast_to(jnp.exp(CHUNK * log_gamma)[None, :, None, None, None], (B, RET_HEADS, nc, 1, 1))
    s_prev = scan_chunk_states(d_state, chunk_decay)
    cross = jnp.einsum('bhncd,bhndv->bhncv', qc, s_prev) * xi[None, :, None, :, None]
    o = from_chunks(inner + cross)
    o = o - jnp.mean(o, axis=-1, keepdims=True)
    o = o * lax.rsqrt(jnp.mean(o * o, axis=-1, keepdims=True) + EPS)
    o = o.reshape(B, L, RET_HEADS * RET_DV) * norm_g.astype(F32)
    return o * jax.nn.silu(g.astype(F32))


def gdn_mixer(q, k, v, beta_raw, a_raw, gate, conv_w, a_log, dt_bias, norm_g):
    B, L = q.shape[:2]
    qkv = jnp.concatenate([q, k, v], axis=-1).astype(F32)
    qkv = jax.nn.silu(causal_depthwise_conv(qkv, conv_w.astype(F32)))
    nq = GDN_HEADS * GDN_DK
    q, k, v = qkv[..., :nq], qkv[..., nq:2 * nq], qkv[..., 2 * nq:]
    q = l2_norm(q.reshape(B, L, GDN_HEADS, GDN_DK)) * GDN_DK ** -0.5
    k = l2_norm(k.reshape(B, L, GDN_HEADS, GDN_DK))
    v = v.reshape(B, L, GDN_HEADS, GDN_DV)
    beta = jax.nn.sigmoid(beta_raw.astype(F32))
    g = -jnp.exp(a_log.astype(F32)) * jax.nn.softplus(a_raw.astype(F32) + dt_bias.astype(F32))
    qc, kc, vc = to_chunks(q), to_chunks(k), to_chunks(v)
    bc = to_chunks(beta[..., None])[..., 0]
    gcum = jnp.cumsum(to_chunks(g[..., None])[..., 0], axis=-1)
    pos = jnp.arange(CHUNK)
    incl = pos[:, None] >= pos[None, :]
    strict = pos[:, None] > pos[None, :]
    diff = gcum[..., :, None] - gcum[..., None, :]
    decay_incl = jnp.exp(jnp.where(incl, diff, -jnp.inf))
    kk = jnp.einsum('bhncd,bhnsd->bhncs', kc, kc)
    a_mat = jnp.where(strict, kk * decay_incl, 0.0) * bc[..., :, None]
    rhs = jnp.concatenate([vc * bc[..., None], kc * (bc * jnp.exp(gcum))[..., None]], axis=-1)
    sol = lax.linalg.triangular_solve(a_mat, rhs, left_side=True, lower=True, unit_diagonal=True)
    u_c, w_c = sol[..., :GDN_DV], sol[..., GDN_DV:]
    aqk = jnp.einsum('bhncd,bhnsd->bhncs', qc, kc) * decay_incl
    q_dec = qc * jnp.exp(gcum)[..., None]
    k_end = kc * jnp.exp(gcum[..., -1:] - gcum)[..., None]
    dec_last = jnp.exp(gcum[..., -1])
    xs = tuple(jnp.moveaxis(t, 2, 0) for t in (u_c, w_c, q_dec, aqk, k_end, dec_last))

    def step(S, inp):
        u_i, w_i, qd_i, a_i, ke_i, dl_i = inp
        v_new = u_i - jnp.einsum('bhck,bhkv->bhcv', w_i, S)
        o_i = jnp.einsum('bhck,bhkv->bhcv', qd_i, S) + jnp.einsum('bhcs,bhsv->bhcv', a_i, v_new)
        S = S * dl_i[..., None, None] + jnp.einsum('bhck,bhcv->bhkv', ke_i, v_new)
        return S, o_i

    s0 = jnp.zeros((B, GDN_HEADS, GDN_DK, GDN_DV), F32)
    _, o = lax.scan(step, s0, xs)
    o = head_rms_norm(from_chunks(jnp.moveaxis(o, 0, 2)), norm_g)
    return o.reshape(B, L, GDN_HEADS * GDN_DV) * jax.nn.silu(gate.astype(F32))


def swiglu(x, w_up, w_down):
    gu = x @ w_up
    g, u = jnp.split(gu, 2, axis=-1)
    return (jax.nn.silu(g) * u) @ w_down


def setup_inputs(seed: int = 0) -> dict:
    key = jax.random.key(seed)
    ks = jax.random.split(key, 32)
    nrm = jax.random.normal

    def gain(k, shape):
        return 1.0 + 0.02 * nrm(k, shape, F32)

    x = nrm(ks[0], (BATCH, SEQ, D_MODEL), F32)
    p = nrm(ks[1], (DEPTH, BATCH, SEQ, PLE_DIM), F32)
    positions = jnp.arange(SEQ, dtype=jnp.int32)[None, :] + jax.random.randint(ks[2], (BATCH, 1), 0, SEQ, dtype=jnp.int32)
    norm_mix = gain(ks[3], (DEPTH, D_MODEL))
    w_in = nrm(ks[4], (DEPTH, D_MODEL, IN_WIDTH), F32) * D_MODEL ** -0.5
    w_out = nrm(ks[5], (DEPTH, MIX_WIDTH, D_MODEL), F32) * MIX_WIDTH ** -0.5
    gla_w_a2 = nrm(ks[6], (DEPTH, GLA_RANK, GLA_HEADS * GLA_DK), F32) * GLA_RANK ** -0.5
    gla_b_a = 0.1 * nrm(ks[7], (DEPTH, GLA_HEADS * GLA_DK), F32)
    gla_norm = gain(ks[8], (DEPTH, GLA_DV))
    s5_lam_re = -0.5 + 0.01 * nrm(ks[9], (DEPTH, S5_GROUPS, S5_STATE), F32)
    s5_lam_im = math.pi * jnp.arange(S5_STATE, dtype=F32) + 0.01 * nrm(ks[10], (DEPTH, S5_GROUPS, S5_STATE), F32)
    s5_log_dt = jax.random.uniform(ks[11], (DEPTH, S5_GROUPS), F32, math.log(S5_DT_MIN), math.log(S5_DT_MAX))
    s5_b_re = nrm(ks[12], (DEPTH, S5_GROUPS, S5_STATE, S5_CH), F32) * (2 * S5_CH) ** -0.5
    s5_b_im = nrm(ks[13], (DEPTH, S5_GROUPS, S5_STATE, S5_CH), F32) * (2 * S5_CH) ** -0.5
    s5_c_re = nrm(ks[14], (DEPTH, S5_GROUPS, S5_CH, S5_STATE), F32) * S5_STATE ** -0.5
    s5_c_im = nrm(ks[15], (DEPTH, S5_GROUPS, S5_CH, S5_STATE), F32) * S5_STATE ** -0.5
    s5_d = nrm(ks[16], (DEPTH, GROUP_WIDTH), F32)
    s5_w_glu = nrm(ks[17], (DEPTH, GROUP_WIDTH, GROUP_WIDTH), F32) * GROUP_WIDTH ** -0.5
    s5_b_glu = 0.01 * nrm(ks[18], (DEPTH, GROUP_WIDTH), F32)
    ret_norm = gain(ks[19], (DEPTH, GROUP_WIDTH))
    gdn_conv = nrm(ks[20], (DEPTH, GDN_CONV, 3 * GDN_HEADS * GDN_DK), F32) * GDN_CONV ** -0.5
    gdn_a_log = jnp.log(jax.random.uniform(ks[21], (DEPTH, GDN_HEADS), F32, 1.0, 16.0))
    dt = jnp.exp(jax.random.uniform(ks[22], (DEPTH, GDN_HEADS), F32, math.log(GDN_DT_MIN), math.log(GDN_DT_MAX)))
    gdn_dt_bias = dt + jnp.log(-jnp.expm1(-dt))
    gdn_norm = gain(ks[23], (DEPTH, GDN_DV))
    norm_ffn = gain(ks[24], (DEPTH, D_MODEL))
    w_ffn_up = nrm(ks[25], (DEPTH, D_MODEL, 2 * FFN_HIDDEN), F32) * D_MODEL ** -0.5
    w_ffn_down = nrm(ks[26], (DEPTH, FFN_HIDDEN, D_MODEL), F32) * FFN_HIDDEN ** -0.5
    norm_ple = gain(ks[27], (DEPTH, D_MODEL))
    w_ple_gate = nrm(ks[28], (DEPTH, D_MODEL, D_MODEL), F32) * D_MODEL ** -0.5
    w_ple_proj = nrm(ks[29], (DEPTH, PLE_DIM, D_MODEL), F32) * PLE_DIM ** -0.5
    norm_final = gain(ks[30], (D_MODEL,))
    return {"x": x, "p": p, "positions": positions, "norm_mix": norm_mix, "w_in": w_in, "w_out": w_out,
            "gla_w_a2": gla_w_a2, "gla_b_a": gla_b_a, "gla_norm": gla_norm,
            "s5_lam_re": s5_lam_re, "s5_lam_im": s5_lam_im, "s5_log_dt": s5_log_dt,
            "s5_b_re": s5_b_re, "s5_b_im": s5_b_im, "s5_c_re": s5_c_re, "s5_c_im": s5_c_im,
            "s5_d": s5_d, "s5_w_glu": s5_w_glu, "s5_b_glu": s5_b_glu, "ret_norm": ret_norm,
            "gdn_conv": gdn_conv, "gdn_a_log": gdn_a_log, "gdn_dt_bias": gdn_dt_bias, "gdn_norm": gdn_norm,
            "norm_ffn": norm_ffn, "w_ffn_up": w_ffn_up, "w_ffn_down": w_ffn_down,
            "norm_ple": norm_ple, "w_ple_gate": w_ple_gate, "w_ple_proj": w_ple_proj, "norm_final": norm_final}


def reference(x, p, positions, norm_mix, w_in, w_out, gla_w_a2, gla_b_a, gla_norm,
              s5_lam_re, s5_lam_im, s5_log_dt, s5_b_re, s5_b_im, s5_c_re, s5_c_im,
              s5_d, s5_w_glu, s5_b_glu, ret_norm, gdn_conv, gdn_a_log, gdn_dt_bias, gdn_norm,
              norm_ffn, w_ffn_up, w_ffn_down, norm_ple, w_ple_gate, w_ple_proj, norm_final):
    cos, sin = rope_tables(positions)
    h = x
    for i in range(DEPTH):
        hn = rms_norm(h, norm_mix[i])
        z = hn @ w_in[i]
        (aq, ak, av, a_low, ar, su, rq, rk, rv, rg, dq, dk, dv, db, da, dg) = split_columns(z)
        o_a = gla_mixer(aq, ak, av, a_low, ar, gla_w_a2[i], gla_b_a[i], gla_norm[i])
        o_b = s5_mixer(su, s5_lam_re[i], s5_lam_im[i], s5_log_dt[i], s5_b_re[i], s5_b_im[i],
                       s5_c_re[i], s5_c_im[i], s5_d[i], s5_w_glu[i], s5_b_glu[i])
        o_c = retention_mixer(rq, rk, rv, rg, cos, sin, ret_norm[i])
        o_d = gdn_mixer(dq, dk, dv, db, da, dg, gdn_conv[i], gdn_a_log[i], gdn_dt_bias[i], gdn_norm[i])
        mix = jnp.concatenate([o_a, o_b, o_c, o_d], axis=-1).astype(h.dtype)
        h = h + mix @ w_out[i]
        h = h + swiglu(rms_norm(h, norm_ffn[i]), w_ffn_up[i], w_ffn_down[i])
        ple_gate = jax.nn.sigmoid(rms_norm(h, norm_ple[i]) @ w_ple_gate[i])
        h = h + (p[i] @ w_ple_proj[i]) * ple_gate
    return rms_norm(h, norm_final)
```

```python
import numpy as np
import concourse.bass as bass
import concourse.mybir as mybir
from contextlib import ExitStack

F32 = mybir.dt.float32
BF16 = mybir.dt.bfloat16
I32 = mybir.dt.int32
ALU = mybir.AluOpType
AF = mybir.ActivationFunctionType
AX = mybir.AxisListType

SEM_CH = 20000
N_DMA_SEMS = 24


_ESZ = {}


def _esize(dt):
    v = _ESZ.get(dt)
    if v is None:
        n = str(dt)
        v = 2 if ("bfloat16" in n or "float16" in n or "int16" in n) else (1 if "8" in n else 4)
        _ESZ[dt] = v
    return v


def _region(a):
    dims = a.ap
    es = _esize(a.dtype)
    ps, pc = dims[0]
    off = a.offset
    if ps > 0:
        p0 = off // ps
        fo = off % ps
        p1 = p0 + pc
    else:
        p0, p1, fo = 0, 1 << 30, off
    lo = hi = fo
    for st, c in dims[1:]:
        ext = st * (c - 1)
        if ext >= 0:
            hi += ext
        else:
            lo += ext
    if type(a.tensor).__name__ == "PSumTensorHandle":
        return (a.tensor.name, p0, p1, 0, 1 << 30)
    return (a.tensor.name, p0, p1, lo * es, (hi + 1) * es)


class _Rec:
    def __init__(self, eng):
        self._eng = eng
        self.reads = []
        self.writes = []

    def __getattr__(self, name):
        f = getattr(self._eng, name)

        def call(*a, **kw):
            for i, v in enumerate(a):
                if type(v).__name__ == "AP":
                    (self.writes if i == 0 else self.reads).append(v)
            for kn, v in kw.items():
                if type(v).__name__ == "AP":
                    (self.writes if kn in ("out", "accum_out", "ap", "out_ap") else self.reads).append(v)
            return f(*a, **kw)
        return call


class _Stub:
    def __init__(self):
        self.reads = []
        self.writes = []

    def __getattr__(self, name):
        def call(*a, **kw):
            for i, v in enumerate(a):
                if type(v).__name__ == "AP":
                    (self.writes if i == 0 else self.reads).append(v)
            for kn, v in kw.items():
                if type(v).__name__ == "AP":
                    (self.writes if kn in ("out", "accum_out", "ap", "out_ap") else self.reads).append(v)
            return None
        return call


class KB:
    def __init__(self, nc, est=None):
        self.nc = nc
        self.st = ExitStack()
        self.eng = {"pe": nc.tensor, "act": nc.scalar, "dve": nc.vector, "pool": nc.gpsimd, "sp": nc.sync}
        self.cnt = {e: 0 for e in self.eng}
        self.sems = {e: [] for e in self.eng}
        self.clock = {e: {} for e in self.eng}
        self.hist = {}
        self.last_w = {}
        self.readers = {}
        self.events = {}
        self.dsem = [self.st.enter_context(nc.semaphore(f"dma{i}")) for i in range(N_DMA_SEMS)]
        self.dcnt = [0] * N_DMA_SEMS
        self.dnext = 0
        self.dnext_pool = 0
        self.nwaits = 0
        self._uid = 0
        self.tag = 'start'
        self.names = {}

    def sb(self, name, shape, dt=F32):
        return self.st.enter_context(self.nc.sbuf_tensor(name, list(shape), dt))

    def ps(self, name, shape, dt=F32):
        return self.st.enter_context(self.nc.psum_tensor(name, list(shape), dt))

    def _sem_for(self, e, n):
        i = (n - 1) // SEM_CH
        while len(self.sems[e]) <= i:
            self.sems[e].append(self.st.enter_context(self.nc.semaphore(f"t_{e}_{len(self.sems[e])}")))
        return self.sems[e][i], ((n - 1) % SEM_CH) + 1

    def _wait(self, e, src, n):
        if self.clock[e].get(src, 0) >= n:
            return
        if isinstance(src, int):
            self.eng[e].wait_ge(self.dsem[src], 16 * n)
        else:
            sem, val = self._sem_for(src, n)
            self.eng[e].wait_ge(sem, val)
        self.nwaits += 1
        ck = self.clock[e]
        for s2, n2 in self.hist.get((src, n), {}).items():
            if ck.get(s2, 0) < n2:
                ck[s2] = n2
        if ck.get(src, 0) < n:
            ck[src] = n

    def _deps(self, e, r, w):
        need = {}

        def add(sn):
            if sn is None:
                return
            s, n = sn
            if s == "pe" and e == "pe":
                return
            if need.get(s, 0) < n:
                need[s] = n
        for key in r:
            add(self.last_w.get(key))
        for key in w:
            add(self.last_w.get(key))
            for s, n in self.readers.get(key, {}).items():
                add((s, n))
        for s, n in need.items():
            self._wait(e, s, n)

    def _rdeps(self, e, reads, writes):
        need = {}
        for kind, regs in (("r", reads), ("w", writes)):
            for (nm, p0, p1, lo, hi) in regs:
                for ev in self.events.get(nm, ()):
                    if kind == "r" and ev[4] != "w":
                        continue
                    if ev[0] < p1 and p0 < ev[1] and ev[2] < hi and lo < ev[3]:
                        s_ = ev[5]
                        if s_ == "pe" and e == "pe":
                            continue
                        if need.get(s_, 0) < ev[6]:
                            need[s_] = ev[6]
        for s_, n in need.items():
            self._wait(e, s_, n)

    def _rupdate(self, src, n, reads, writes):
        for (nm, p0, p1, lo, hi) in reads:
            lst = self.events.setdefault(nm, [])
            for ev in lst:
                if ev[4] == "r" and ev[5] == src and ev[0] == p0 and ev[1] == p1 and ev[2] == lo and ev[3] == hi:
                    ev[6] = n
                    break
            else:
                lst.append([p0, p1, lo, hi, "r", src, n])
        for (nm, p0, p1, lo, hi) in writes:
            lst = self.events.setdefault(nm, [])
            lst[:] = [ev for ev in lst if not (p0 <= ev[0] and ev[1] <= p1 and lo <= ev[2] and ev[3] <= hi)]
            lst.append([p0, p1, lo, hi, "w", src, n])

    def op(self, e, fn, r=(), w=()):
        rec = _Rec(self.eng[e])
        stub = _Stub()
        fn(stub)
        reads = [_region(a) for a in stub.reads]
        writes = [_region(a) for a in stub.writes]
        self._rdeps(e, reads, writes)
        inst = fn(self.eng[e])
        n = self.cnt[e] = self.cnt[e] + 1
        sem, _ = self._sem_for(e, n)
        inst.then_inc(sem, 1)
        try:
            self.names[inst.ins.name] = self.tag
        except Exception:
            pass
        h = dict(self.clock[e])
        h[e] = n
        self.hist[(e, n)] = h
        self._rupdate(e, n, reads, writes)
        return inst

    def dma(self, q, out, in_, r=(), w=(), **kw):
        reads = [_region(in_)]
        writes = [_region(out)]
        self._rdeps(q, reads, writes)
        half = N_DMA_SEMS // 2
        if q == "pool":
            slot = half + self.dnext_pool
            self.dnext_pool = (self.dnext_pool + 1) % half
        else:
            slot = self.dnext
            self.dnext = (self.dnext + 1) % half
        if self.dcnt[slot] > 0:
            self._wait(q, slot, self.dcnt[slot])
        inst = self.eng[q].dma_start(out=out, in_=in_, **kw)
        inst.then_inc(self.dsem[slot], 16)
        n = self.dcnt[slot] = self.dcnt[slot] + 1
        self.hist[(slot, n)] = dict(self.clock[q])
        self._rupdate(slot, n, reads, writes)
        return inst

    def finish(self, e="sp"):
        for s in list(self.eng):
            if s != e and self.cnt[s] > 0:
                self._wait(e, s, self.cnt[s])
        for slot in range(N_DMA_SEMS):
            if self.dcnt[slot] > 0:
                self._wait(e, slot, self.dcnt[slot])

    def mm(self, out, lhsT, rhs, start=True, stop=True, r=(), w=()):
        return self.op("pe", lambda e: e.matmul(out, lhsT=lhsT, rhs=rhs, start=start, stop=stop), r=r, w=w)

    def tr(self, out, in_, ident, r=(), w=()):
        return self.op("pe", lambda e: e.transpose(out, in_, ident), r=r, w=w)

    def act(self, out, in_, func, r=(), w=(), **kw):
        return self.op("act", lambda e: e.activation(out=out, in_=in_, func=func, **kw), r=r, w=w)

    def tt(self, out, in0, in1, op, r=(), w=(), e="dve"):
        return self.op(e, lambda g: g.tensor_tensor(out=out, in0=in0, in1=in1, op=op), r=r, w=w)

    def ts(self, out, in0, s1, s2=None, op0=ALU.mult, op1=None, r=(), w=(), e="dve", **kw):
        if op1 is None:
            return self.op(e, lambda g: g.tensor_scalar(out=out, in0=in0, scalar1=s1, scalar2=None, op0=op0, **kw), r=r, w=w)
        return self.op(e, lambda g: g.tensor_scalar(out=out, in0=in0, scalar1=s1, scalar2=s2, op0=op0, op1=op1, **kw), r=r, w=w)

    def stt(self, out, in0, scalar, in1, op0, op1, r=(), w=()):
        return self.op("dve", lambda g: g.scalar_tensor_tensor(out=out, in0=in0, scalar=scalar, in1=in1, op0=op0, op1=op1), r=r, w=w)

    def cp(self, out, in_, r=(), w=(), e="dve"):
        if e == "act":
            return self.op("act", lambda g: g.copy(out=out, in_=in_), r=r, w=w)
        return self.op(e, lambda g: g.tensor_copy(out=out, in_=in_), r=r, w=w)

    def barrier(self):
        snap = dict(self.cnt)
        dsn = list(self.dcnt)
        for e in self.eng:
            for s, n in snap.items():
                if n > 0:
                    self._wait(e, s, n)
            for slot, n in enumerate(dsn):
                if n > 0:
                    self._wait(e, slot, n)


import math
from concourse.bass_utils import run_bass_kernel_spmd

DM = 1024
INW = 3096
FFH = 2816
PI = math.pi
LN_G = [math.log1p(-(2.0 ** (-5.0 - h))) for h in range(4)]
EPS = 1e-6
PSH = [("norm_mix", [1024]), ("w_in", [1024, 3096]), ("w_out", [1024, 1024]), ("gla_w_a2", [16, 128]),
       ("gla_b_a", [128]), ("gla_norm", [64]), ("s5_lam_re", [16, 64]), ("s5_lam_im", [16, 64]),
       ("s5_log_dt", [16]), ("s5_b_re", [16, 64, 16]), ("s5_b_im", [16, 64, 16]), ("s5_c_re", [16, 16, 64]),
       ("s5_c_im", [16, 16, 64]), ("s5_d", [256]), ("s5_w_glu", [256, 256]), ("s5_b_glu", [256]),
       ("ret_norm", [256]), ("gdn_conv", [4, 768]), ("gdn_a_log", [4]), ("gdn_dt_bias", [4]),
       ("gdn_norm", [64]), ("norm_ffn", [1024]), ("w_ffn_up", [1024, 5632]), ("w_ffn_down", [2816, 1024]),
       ("norm_ple", [1024]), ("w_ple_gate", [1024, 1024]), ("w_ple_proj", [256, 1024])]
FPIECES = [(0, 6), (6, 6), (12, 5), (17, 5)]


_K = [None]


class _Stop(Exception):
    pass


def build(L=2048, depth=2, dbg=False, stop=None):
    try:
        return _build(L, depth, dbg, stop)
    except _Stop as e:
        return e.args[0]


def _build(L, depth, dbg, stop):
    NT = L // 128
    nc = bass.Bass("TRN2", target_bir_lowering=False)
    k = KB(nc)
    D = {}

    def ck(name):
        k.tag = name
        if stop == name:
            k.finish("sp")
            print("STOP at", name, k.cnt, flush=True)
            raise _Stop(nc)

    def din(name, shape, dt=F32):
        D[name] = nc.dram_tensor(name, list(shape), dt, kind="ExternalInput").ap()
    din("x", [L, DM]); din("p", [depth, L, 256]); din("positions", [1, L], I32)
    for nm, sh in PSH:
        din(nm, [depth] + sh)
    din("norm_final", [1, 1024])
    out = nc.dram_tensor("out", [L, DM], F32, kind="ExternalOutput").ap()
    dbgo = nc.dram_tensor("dbg", [128, 8, L], F32, kind="ExternalOutput").ap() if dbg else None

    h = k.sb("h", [128, NT, DM])
    W = k.sb("W", [128, 8 * 3096 + 8 * 1024], BF16)
    win = W[:, 0:8 * 3096].rearrange("p (k n) -> p k n", k=8)
    wout = W[:, 8 * 3096:8 * 3096 + 8192].rearrange("p (k n) -> p k n", k=8)
    wup = W[:, 0:8 * 1536].rearrange("p (k n) -> p k n", k=8)
    wdn = W[:, 12288:12288 + 6 * 1024].rearrange("p (f n) -> p f n", f=6)
    wpg = W[:, 18432:18432 + 8192].rearrange("p (k n) -> p k n", k=8)
    wpp = W[:, 26624:26624 + 2048].rearrange("p (k n) -> p k n", k=2)
    actT = W[:, 28672:28672 + 3072].rearrange("p (f n) -> p f n", f=6)
    NF, NB = 8, 12
    A2 = k.sb("A2", [128, 8704])
    F = [A2[:, i * 512:(i + 1) * 512] for i in range(NF)]
    s5cos = A2[:, 4096:5120]; s5sin = A2[:, 5120:6144]; rtab = A2[:, 6144:7168]
    cdiag = A2[:, 7168:8704].bitcast(BF16)
    hn2T = A2[:, 0:8192].bitcast(BF16).rearrange("p (k n) -> p k n", k=8)
    Bt = [k.sb(f"B{i}", [128, 512], BF16) for i in range(NB)]
    FI = F[7].bitcast(I32)
    Rm = k.sb("Rm", [128, 128], BF16)
    ropeCt = k.sb("ropeCt", [128, 128], BF16); ropeSt = k.sb("ropeSt", [128, 128], BF16)
    ident_f = k.sb("ident_f", [128, 128]); ident_b = k.sb("ident_b", [128, 128], BF16)
    ones_f = k.sb("ones_f", [128, 128]); triL = k.sb("triL", [128, 128]); Ust = k.sb("Ust", [128, 128])
    negU = k.sb("negU", [128, 128]); blk1 = k.sb("blk1", [128, 128])
    ones_lo = k.sb("ones_lo", [128, 128]); ones_hi = k.sb("ones_hi", [128, 128])
    iota1 = k.sb("iota1", [128, 128])
    cms = k.sb("cms", [128, 128])
    dmask = k.sb("dmask", [128, 512]); xi = k.sb("xi", [128, 4]); zeta = k.sb("zeta", [128, 4])
    g128 = k.sb("g128", [128, 2]); mlohi = k.sb("mlohi", [128, 2]); EO = k.sb("EO", [128, 2])
    pcol = k.sb("pcol", [128, 1]); hm = k.sb("hm", [128, 4])
    Brem = k.sb("Brem", [128, 1024], BF16); Bimm = k.sb("Bimm", [128, 1024], BF16)
    Cpr = k.sb("Cpr", [128, 1024], BF16); Cpi = k.sb("Cpi", [128, 1024], BF16); Dfull = k.sb("Dfull", [128, 256], BF16)
    ropeC = nc.dram_tensor("ropeC_d", [128, L], BF16).ap(); ropeS = nc.dram_tensor("ropeS_d", [128, L], BF16).ap()
    stage = k.sb("stage", [8, 128])
    gcol = k.sb("gcol", [128, 8])
    small = k.sb("small", [128, 64])
    wa2 = k.sb("wa2", [16, 128]); nba = k.sb("nba", [128, 1]); gn_gla = k.sb("gn_gla", [128, 64])
    gn_gdn = k.sb("gn_gdn", [128, 64]); gn_ret = k.sb("gn_ret", [128, 256])
    alog_b = k.sb("alog_b", [128, 4]); dtb_b = k.sb("dtb_b", [128, 4]); nexpA = k.sb("nexpA", [128, 4])
    cw = k.sb("cw", [128, 24])
    wglu = k.sb("wglu", [128, 2, 256], BF16); bglu = k.sb("bglu", [128, 2]); dcol = k.sb("dcol", [128, 2])
    s5c = k.sb("s5c", [128, 96])
    gmask = k.sb("gmask", [128, 9 * 128], BF16)
    hn = k.sb("hn", [128, DM], BF16); hnT = k.sb("hnT", [128, 8, 128], BF16); mixT = k.sb("mixT", [128, 8, 128], BF16)
    alow = k.sb("alow", [16, 128])
    Sgla = k.sb("Sgla", [128, 64]); Sgla_b = k.sb("Sgla_b", [128, 64], BF16)
    Sret = k.sb("Sret", [128, 128]); Sret_b = k.sb("Sret_b", [128, 128], BF16)
    Sgdn = k.sb("Sgdn", [128, 128]); Sgdn_b = k.sb("Sgdn_b", [128, 128], BF16)
    xst = k.sb("xst", [128, 16])
    xc = k.sb("xc", [128, 6, 132], BF16)
    PS = [k.ps(f"PS{i}", [128, 512]) for i in range(6)]
    PBF = [k.ps(f"PB{i}", [128, 1024], BF16) for i in range(2)]
    cnt = {"ps": 0, "pb": 0}

    pinned = set()

    def bank(pin=False):
        while True:
            i = cnt["ps"] % 6; cnt["ps"] += 1
            if i not in pinned:
                break
        if pin:
            pinned.add(i)
        return PS[i], f"PS{i}"

    def unpin(key):
        pinned.discard(int(key[2:]))

    def bbank():
        i = cnt["pb"] % 2; cnt["pb"] += 1
        return PBF[i], f"PB{i}"

    def v3(ap, a):
        return ap.rearrange("p (a b) -> p a b", a=a)

    def bc_mid(ap2, a):
        return ap2.unsqueeze(1).broadcast_to([ap2.shape[0], a, ap2.shape[1]])

    def bc_last(ap2, n):
        return ap2.unsqueeze(2).broadcast_to([ap2.shape[0], ap2.shape[1], n])

    def colload(dst, dkey, src2d, n):
        k.dma("sp", stage[0:n, :], src2d, w=["stage"])
        b, bk = bank()
        k.tr(b[:, 0:n], stage[0:n, :], ident_f[0:n, 0:n], r=["stage", "ident_f"], w=[bk])
        k.cp(dst, b[:, 0:n], r=[bk], w=[dkey])

    def range_reduce(t, tkey, tf, fkey, ti, ikey):
        k.ts(tf, t, 1.0 / (2 * PI), r=[tkey], w=[fkey])
        k.cp(ti, tf, r=[fkey], w=[ikey])
        k.cp(tf, ti, r=[ikey], w=[fkey])
        k.stt(t, tf, -2.0 * PI, t, ALU.mult, ALU.add, r=[fkey, tkey], w=[tkey])
        k.ts(tf, t, PI, -2.0 * PI, op0=ALU.is_gt, op1=ALU.mult, r=[tkey], w=[fkey])
        k.tt(t, t, tf, ALU.add, r=[tkey, fkey], w=[tkey])
        k.ts(tf, t, -PI, 2.0 * PI, op0=ALU.is_lt, op1=ALU.mult, r=[tkey], w=[fkey])
        k.tt(t, t, tf, ALU.add, r=[tkey, fkey], w=[tkey])

    def rstd_from(dst, dkey, src, skey, scale, r_extra=()):
        k.act(dst, src, AF.Ln, scale=scale, bias=EPS, r=[skey] + list(r_extra), w=[dkey])
        k.act(dst, dst, AF.Exp, scale=-0.5, r=[dkey], w=[dkey])

    def norm_T(src_h, gkey_loaded, dstT, dkey, ntok_off=0):
        ss = small[:, 0:1]
        k.act(hn[:], src_h, AF.Square, accum_out=ss, r=["h"], w=["hn", "sm_ss"])
        rstd_from(small[:, 1:2], "sm_rs", ss, "sm_ss", 1.0 / DM)
        k.ts(hn[:], src_h, small[:, 1:2], r=["h", "sm_rs"], w=["hn"])
        pb, pk = bbank()
        for kc in range(8):
            k.tr(pb[:, kc * 128:(kc + 1) * 128], hn[:, kc * 128:(kc + 1) * 128], ident_b[:], r=["hn", "ident_b"], w=[pk])
        k.tt(dstT[:, :, ntok_off:ntok_off + 128], v3(pb[:, 0:1024], 8), bc_last(gcol[:, 0:8], 128), ALU.mult,
             r=[pk, gkey_loaded], w=[dkey])

    def head_norm_gate(o3, okeys, gn_ap, gnkey, sgate, sgkey, dst_bf, dkey, ft, ftkey, center=False):
        t0 = ft[:, 0:256]; t1 = ft[:, 256:512]
        if center:
            k.op("dve", lambda g: g.tensor_reduce(out=small[:, 8:12], in_=o3, axis=AX.X, op=ALU.add), r=okeys, w=["sm_m"])
            k.ts(small[:, 8:12], small[:, 8:12], 1.0 / 64, r=["sm_m"], w=["sm_m"])
            k.tt(v3(t0, 4), o3, bc_last(small[:, 8:12], 64), ALU.subtract, r=okeys + ["sm_m"], w=[ftkey])
            src3 = v3(t0, 4); skeys = [ftkey]
        else:
            src3 = o3; skeys = okeys
        k.act(v3(t1, 4), src3, AF.Square, r=skeys, w=[ftkey])
        k.op("dve", lambda g: g.tensor_reduce(out=small[:, 12:16], in_=v3(t1, 4), axis=AX.X, op=ALU.add), r=[ftkey], w=["sm_v"])
        rstd_from(small[:, 16:20], "sm_r4", small[:, 12:16], "sm_v", 1.0 / 64)
        k.tt(v3(t1, 4), src3, bc_last(small[:, 16:20], 64), ALU.mult, r=skeys + ["sm_r4"], w=[ftkey])
        k.tt(t1, t1, gn_ap, ALU.mult, r=[ftkey, gnkey], w=[ftkey])
        k.tt(dst_bf, t1, sgate, ALU.mult, r=[ftkey, sgkey], w=[dkey])

    def to_mixT(src_bf, skey, c0):
        pb, pk = bbank()
        for i in range(2):
            k.tr(pb[:, i * 128:(i + 1) * 128], src_bf[:, i * 128:(i + 1) * 128], ident_b[:], r=[skey, "ident_b"], w=[pk])
        k.cp(mixT[:, c0:c0 + 2, :], v3(pb[:, 0:256], 2), r=[pk], w=["mixT"], e="act")

    def proj_fm(b, bk, slot, c0, M=128):
        for kc in range(8):
            k.mm(b[0:M, slot * 128:(slot + 1) * 128], win[:, kc, c0:c0 + M], hnT[:, kc, :], start=(kc == 0), stop=(kc == 7),
                 r=["win", "hnT"], w=[bk])

    def proj_tm(b, bk, o0, c0, n):
        for kc in range(8):
            k.mm(b[:, o0:o0 + n], hnT[:, kc, :], win[:, kc, c0:c0 + n], start=(kc == 0), stop=(kc == 7),
                 r=["win", "hnT"], w=[bk])

    P = "pool"
    k.op(P, lambda e: e.memset(ones_f[:], 1.0), w=["ones_f"])
    k.op(P, lambda e: e.affine_select(out=ident_f[:], in_=ones_f[:], pattern=[[-1, 128]], compare_op=ALU.is_equal, fill=0.0, base=0, channel_multiplier=1), r=["ones_f"], w=["ident_f"])
    k.cp(ident_b[:], ident_f[:], r=["ident_f"], w=["ident_b"], e=P)
    k.op(P, lambda e: e.affine_select(out=triL[:], in_=ones_f[:], pattern=[[1, 128]], compare_op=ALU.is_ge, fill=0.0, base=0, channel_multiplier=-1), r=["ones_f"], w=["triL"])
    k.op(P, lambda e: e.affine_select(out=Ust[:], in_=ones_f[:], pattern=[[-1, 128]], compare_op=ALU.is_gt, fill=0.0, base=0, channel_multiplier=1), r=["ones_f"], w=["Ust"])
    k.op(P, lambda e: e.affine_select(out=negU[:], in_=ones_f[:], pattern=[[1, 128]], compare_op=ALU.is_gt, fill=0.0, base=0, channel_multiplier=-1), r=["ones_f"], w=["negU"])
    k.ts(negU[:], negU[:], -1.0, r=["negU"], w=["negU"], e=P)
    k.op(P, lambda e: e.memset(blk1[:], 0.0), w=["blk1"])
    k.op(P, lambda e: e.memset(blk1[0:64, 0:64], 1.0), w=["blk1"])
    k.op(P, lambda e: e.memset(blk1[64:128, 64:128], 1.0), w=["blk1"])
    k.op(P, lambda e: e.memset(ones_lo[:], 0.0), w=["ones_lo"])
    k.op(P, lambda e: e.memset(ones_lo[:, 0:64], 1.0), w=["ones_lo"])
    k.op(P, lambda e: e.memset(ones_hi[:], 0.0), w=["ones_hi"])
    k.op(P, lambda e: e.memset(ones_hi[:, 64:128], 1.0), w=["ones_hi"])
    k.op(P, lambda e: e.memset(mlohi[:], 0.0), w=["mlohi"])
    k.op(P, lambda e: e.memset(mlohi[0:64, 0:1], 1.0), w=["mlohi"])
    k.op(P, lambda e: e.memset(mlohi[64:128, 1:2], 1.0), w=["mlohi"])
    for i in range(2):
        k.op(P, lambda e: e.memset(g128[0:64, i:i + 1], math.exp(128 * LN_G[2 * i])), w=["g128"])
        k.op(P, lambda e: e.memset(g128[64:128, i:i + 1], math.exp(128 * LN_G[2 * i + 1])), w=["g128"])
    k.op(P, lambda e: e.iota(iota1[:], pattern=[[1, 128]], base=1, channel_multiplier=0, allow_small_or_imprecise_dtypes=True), w=["iota1"])
    k.op(P, lambda e: e.iota(cms[:], pattern=[[1, 128]], base=0, channel_multiplier=-1, allow_small_or_imprecise_dtypes=True), w=["cms"])
    k.op(P, lambda e: e.iota(pcol[:], pattern=[[0, 1]], base=0, channel_multiplier=1, allow_small_or_imprecise_dtypes=True), w=["pcol"])
    k.op("dve", lambda g: g.tensor_reduce(out=F[0][:, 0:8], in_=ident_f[:].rearrange("p (a b c) -> p a b c", a=4, b=2), axis=AX.X, op=ALU.add), r=["ident_f"], w=["F0"])
    k.op("dve", lambda g: g.tensor_reduce(out=EO[:], in_=F[0][:, 0:8].rearrange("p (a b) -> p b a", a=4), axis=AX.X, op=ALU.add), r=["F0"], w=["EO"])
    k.op("dve", lambda g: g.tensor_reduce(out=hm[:], in_=v3(ident_f[:], 4), axis=AX.X, op=ALU.add), r=["ident_f"], w=["hm"])
    rv4 = lambda ap: ap.rearrange("p (b h d) -> p b h d", b=2, h=2)
    k.op(P, lambda e: e.affine_select(out=F[0][:, 0:128], in_=ones_f[:], pattern=[[-1, 128]], compare_op=ALU.is_equal, fill=0.0, base=-32, channel_multiplier=1), r=["ones_f"], w=["F0"])
    k.op(P, lambda e: e.memset(rv4(F[0][:, 0:128])[:, :, 1, :], 0.0), w=["F0"])
    k.op(P, lambda e: e.affine_select(out=F[1][:, 0:128], in_=ones_f[:], pattern=[[-1, 128]], compare_op=ALU.is_equal, fill=0.0, base=32, channel_multiplier=1), r=["ones_f"], w=["F1"])
    k.op(P, lambda e: e.memset(rv4(F[1][:, 0:128])[:, :, 0, :], 0.0), w=["F1"])
    k.tt(Rm[:], F[1][:, 0:128], F[0][:, 0:128], ALU.subtract, r=["F0", "F1"], w=["Rm"], e=P)
    def bdmask(dst, m):
        nb_ = 128 // m
        k.op(P, lambda e: e.affine_select(out=dst, in_=ones_f[:], pattern=[[-m, nb_], [0, m]], compare_op=ALU.is_ge, fill=0.0, base=0, channel_multiplier=1), r=["ones_f"], w=["F2"])
        k.op(P, lambda e: e.affine_select(out=dst, in_=dst, pattern=[[m, nb_], [0, m]], compare_op=ALU.is_ge, fill=0.0, base=m - 1, channel_multiplier=-1), r=["F2"], w=["F2"])
    bds = {8: F[2][:, 0:128], 16: F[2][:, 128:256], 32: F[2][:, 256:384], 64: F[2][:, 384:512], 128: ones_f[:]}
    for m_ in (8, 16, 32, 64):
        bdmask(bds[m_], m_)
    k.cp(gmask[:, 0:128], bds[8], r=["F2"], w=["gmask"])
    for li_, m_ in enumerate((8, 16, 32, 64)):
        k.tt(F[3][:, 0:128], bds[2 * m_], bds[m_], ALU.subtract, r=["F2", "ones_f"], w=["F3"])
        k.tt(gmask[:, (1 + li_) * 128:(2 + li_) * 128], F[3][:, 0:128], Ust[:], ALU.mult, r=["F3", "Ust"], w=["gmask"])
        k.stt(gmask[:, (5 + li_) * 128:(6 + li_) * 128], F[3][:, 0:128], -1.0, negU[:], ALU.mult, ALU.mult, r=["F3", "negU"], w=["gmask"])
    for hh in range(4):
        k.act(dmask[:, hh * 128:(hh + 1) * 128], cms[:], AF.Exp, scale=LN_G[hh], r=["cms"], w=["dmask"])
        k.act(xi[:, hh:hh + 1], pcol[:], AF.Exp, scale=LN_G[hh], bias=LN_G[hh], r=["pcol"], w=["xi"])
        k.act(zeta[:, hh:hh + 1], pcol[:], AF.Exp, scale=-LN_G[hh], bias=127.0 * LN_G[hh], r=["pcol"], w=["zeta"])
    k.tt(v3(dmask[:], 4), v3(dmask[:], 4), bc_mid(triL[:], 4), ALU.mult, r=["dmask", "triL"], w=["dmask"])
    fidx = small[:, 32:33]; invf = small[:, 33:34]
    for b4 in range(4):
        k.op(P, lambda e: e.iota(fidx[32 * b4:32 * b4 + 32, :], pattern=[[0, 1]], base=0, channel_multiplier=1, allow_small_or_imprecise_dtypes=True), w=["fidx"])
    k.act(invf, fidx, AF.Exp, scale=-math.log(10000.0) / 31.0, r=["fidx"], w=["invf"])
    for c in range(0, L, 512):
        n = min(512, L - c)
        k.dma("sp", FI[:, 0:n], D["positions"][0:1, c:c + n].partition_broadcast(128), w=["F7"])
        k.cp(F[1][:, 0:n], FI[:, 0:n], r=["F7"], w=["F1"])
        k.ts(F[0][:, 0:n], F[1][:, 0:n], invf, r=["F1", "invf"], w=["F0"])
        k.ts(F[2][:, 0:n], F[0][:, 0:n], PI / 2, op0=ALU.add, r=["F0"], w=["F2"])
        range_reduce(F[0][:, 0:n], "F0", F[3][:, 0:n], "F3", FI[:, 0:n], "F7")
        k.act(Bt[0][:, 0:n], F[0][:, 0:n], AF.Sin, r=["F0"], w=["B0"])
        k.dma("sp", ropeS[:, c:c + n], Bt[0][:, 0:n], r=["B0"], w=["ropeS"])
        range_reduce(F[2][:, 0:n], "F2", F[3][:, 0:n], "F3", FI[:, 0:n], "F7")
        k.act(Bt[1][:, 0:n], F[2][:, 0:n], AF.Sin, r=["F2"], w=["B1"])
        k.dma("sp", ropeC[:, c:c + n], Bt[1][:, 0:n], r=["B1"], w=["ropeC"])
    for t in range(NT):
        k.dma("sp", h[:, t, :], D["x"][t * 128:(t + 1) * 128, :], w=["h"])

    ck("const")
    for l in range(depth):
        k.barrier()
        for kc in range(8):
            k.dma("pool", win[:, kc, 0:INW], D["w_in"][l, kc * 128:(kc + 1) * 128, :], w=["win"])
        for kc in range(8):
            k.dma("pool", wout[:, kc, :], D["w_out"][l, kc * 128:(kc + 1) * 128, :], w=["wout"])
        k.dma("pool", wglu[:], D["s5_w_glu"][l].rearrange("(kc q) n -> q kc n", q=128), w=["wglu"])
        ck("wload")
        colload(gcol[:, 0:8], "gcol", D["norm_mix"][l].rearrange("(a b) -> a b", b=128), 8)
        k.dma("sp", wa2[:], D["gla_w_a2"][l], w=["wa2"])
        colload(nba[:, 0:1], "nba", D["gla_b_a"][l].rearrange("(a b) -> a b", b=128), 1)
        k.ts(nba[:], nba[:], -1.0, r=["nba"], w=["nba"])
        k.dma("sp", gn_gla[:], D["gla_norm"][l:l + 1, :].partition_broadcast(128), w=["gn_gla"])
        k.dma("sp", gn_gdn[:], D["gdn_norm"][l:l + 1, :].partition_broadcast(128), w=["gn_gdn"])
        k.dma("sp", gn_ret[:], D["ret_norm"][l:l + 1, :].partition_broadcast(128), w=["gn_ret"])
        k.dma("sp", alog_b[:], D["gdn_a_log"][l:l + 1, :].partition_broadcast(128), w=["alog_b"])
        k.dma("sp", dtb_b[:], D["gdn_dt_bias"][l:l + 1, :].partition_broadcast(128), w=["dtb_b"])
        k.act(nexpA[:], alog_b[:], AF.Exp, r=["alog_b"], w=["nexpA"])
        k.ts(nexpA[:], nexpA[:], -1.0, r=["nexpA"], w=["nexpA"])
        for i in range(6):
            colload(cw[:, i * 4:(i + 1) * 4], "cw", D["gdn_conv"][l, :, i * 128:(i + 1) * 128], 4)
        for i in range(6):
            for j in range(4):
                k.ts(cdiag[:, (i * 4 + j) * 128:(i * 4 + j + 1) * 128], ident_f[:], cw[:, i * 4 + j:i * 4 + j + 1], r=["ident_f", "cw"], w=["cdiag"])
        colload(bglu[:, 0:2], "bglu", D["s5_b_glu"][l].rearrange("(a b) -> a b", b=128), 2)
        colload(dcol[:, 0:2], "dcol", D["s5_d"][l].rearrange("(a b) -> a b", b=128), 2)
        ck("params")
        s8 = F[4]
        k.dma("sp", s8[0:8, 0:128], D["s5_lam_re"][l].rearrange("(j g) p -> j (g p)", g=2), w=["F4"])
        k.dma("sp", s8[0:8, 128:256], D["s5_lam_im"][l].rearrange("(j g) p -> j (g p)", g=2), w=["F4"])
        k.dma("sp", s8[0:8, 256:258], D["s5_log_dt"][l].rearrange("(j g) -> j g", g=2), w=["F4"])
        k.act(s8[0:8, 256:258], s8[0:8, 256:258], AF.Exp, r=["F4"], w=["F4"])
        k.ts(s8[0:8, 0:128], s8[0:8, 0:128], -1e-4, op0=ALU.min, r=["F4"], w=["F4"])
        dt3 = s8[0:8, 256:258].unsqueeze(2).broadcast_to([8, 2, 64])
        k.tt(v3(s8[0:8, 260:388], 2), v3(s8[0:8, 0:128], 2), dt3, ALU.mult, r=["F4"], w=["F4"])
        k.tt(v3(F[5][0:8, 0:128], 2), v3(s8[0:8, 128:256], 2), dt3, ALU.mult, r=["F4"], w=["F5"])
        b, bk = bank()
        k.tr(b[:, 0:8], s8[0:8, 0:128], ident_f[0:8, 0:8], r=["F4", "ident_f"], w=[bk])
        k.tr(b[:, 8:16], s8[0:8, 128:256], ident_f[0:8, 0:8], r=["F4", "ident_f"], w=[bk])
        k.tr(b[:, 16:24], s8[0:8, 260:388], ident_f[0:8, 0:8], r=["F4", "ident_f"], w=[bk])
        k.tr(b[:, 24:32], F[5][0:8, 0:128], ident_f[0:8, 0:8], r=["F5", "ident_f"], w=[bk])
        k.cp(s5c[:, 0:32], b[:, 0:32], r=[bk], w=["s5c"])
        lr = s5c[:, 0:8]; li = s5c[:, 8:16]; lrd = s5c[:, 16:24]; th = s5c[:, 24:32]
        rr_ = s5c[:, 32:40]
        k.act(rr_, lrd, AF.Exp, r=["s5c"], w=["s5c"])
        k.cp(v3(rtab[:], 8), bc_last(rr_, 128), r=["s5c"], w=["rtab"])
        k.op("dve", lambda g: g.memset(v3(rtab[:], 8)[:, :, 0], 0.0), w=["rtab"])
        for half in range(2):
            sl = slice(half * 512, (half + 1) * 512)
            k.tt(v3(F[0][:], 4), bc_mid(iota1[:], 4), bc_last(th[:, half * 4:half * 4 + 4], 128), ALU.mult, r=["iota1", "s5c"], w=["F0"])
            k.ts(F[2][:], F[0][:], PI / 2, op0=ALU.add, r=["F0"], w=["F2"])
            range_reduce(F[0][:], "F0", F[3][:], "F3", FI[:], "F7")
            k.act(s5sin[:, sl], F[0][:], AF.Sin, r=["F0"], w=["s5sin"])
            range_reduce(F[2][:], "F2", F[3][:], "F3", FI[:], "F7")
            k.act(s5cos[:, sl], F[2][:], AF.Sin, r=["F2"], w=["s5cos"])
        k.cp(F[0][:, 0:8], th, r=["s5c"], w=["F0"])
        k.ts(F[2][:, 0:8], th, PI / 2, op0=ALU.add, r=["s5c"], w=["F2"])
        range_reduce(F[0][:, 0:8], "F0", F[3][:, 0:8], "F3", FI[:, 0:8], "F7")
        range_reduce(F[2][:, 0:8], "F2", F[3][:, 0:8], "F3", FI[:, 0:8], "F7")
        sth = s5c[:, 40:48]; cth = s5c[:, 48:56]
        k.act(sth, F[0][:, 0:8], AF.Sin, r=["F0"], w=["s5c"])
        k.act(cth, F[2][:, 0:8], AF.Sin, r=["F2"], w=["s5c"])
        am1 = s5c[:, 56:64]; ai = s5c[:, 64:72]; c1r = s5c[:, 72:80]; c1i = s5c[:, 80:88]; den = s5c[:, 88:96]
        k.tt(am1, rr_, cth, ALU.mult, r=["s5c"], w=["s5c"])
        k.ts(am1, am1, -1.0, op0=ALU.add, r=["s5c"], w=["s5c"])
        k.tt(ai, rr_, sth, ALU.mult, r=["s5c"], w=["s5c"])
        t8a = F[0][:, 16:24]; t8b = F[0][:, 24:32]
        k.tt(den, lr, lr, ALU.mult, r=["s5c"], w=["s5c"])
        k.tt(t8a, li, li, ALU.mult, r=["s5c"], w=["F0"])
        k.tt(den, den, t8a, ALU.add, r=["s5c", "F0"], w=["s5c"])
        k.op("dve", lambda g: g.reciprocal(out=den, in_=den), r=["s5c"], w=["s5c"])
        k.tt(t8a, am1, lr, ALU.mult, r=["s5c"], w=["F0"])
        k.tt(t8b, ai, li, ALU.mult, r=["s5c"], w=["F0"])
        k.tt(t8a, t8a, t8b, ALU.add, r=["F0"], w=["F0"])
        k.tt(c1r, t8a, den, ALU.mult, r=["F0", "s5c"], w=["s5c"])
        k.tt(t8a, ai, lr, ALU.mult, r=["s5c"], w=["F0"])
        k.tt(t8b, am1, li, ALU.mult, r=["s5c"], w=["F0"])
        k.tt(t8a, t8a, t8b, ALU.subtract, r=["F0"], w=["F0"])
        k.tt(c1i, t8a, den, ALU.mult, r=["F0", "s5c"], w=["s5c"])
        braw = F[5];
        k.dma("sp", v3(braw[:, 0:128], 8), D["s5_b_re"][l].rearrange("(j g) p h -> (g p) j h", g=2), w=["F5"])
        k.dma("sp", v3(braw[:, 128:256], 8), D["s5_b_im"][l].rearrange("(j g) p h -> (g p) j h", g=2), w=["F5"])
        bre3 = v3(braw[:, 0:128], 8); bim3 = v3(braw[:, 128:256], 8)
        tA = v3(F[6][:, 0:128], 8); tB = v3(F[6][:, 128:256], 8); bbr = v3(F[6][:, 256:384], 8); bbi = v3(F[6][:, 384:512], 8)
        k.tt(tA, bre3, bc_last(c1r, 16), ALU.mult, r=["F5", "s5c"], w=["F6"])
        k.tt(tB, bim3, bc_last(c1i, 16), ALU.mult, r=["F5", "s5c"], w=["F6"])
        k.tt(bbr, tA, tB, ALU.subtract, r=["F6", "F6"], w=["F6"])
        k.tt(tA, bim3, bc_last(c1r, 16), ALU.mult, r=["F5", "s5c"], w=["F6"])
        k.tt(tB, bre3, bc_last(c1i, 16), ALU.mult, r=["F5", "s5c"], w=["F6"])
        k.tt(bbi, tA, tB, ALU.add, r=["F6", "F6"], w=["F6"])
        Bre = Bt[2][:, 0:256]; Bim = Bt[2][:, 256:512]; CreT = Bt[3][:, 0:256]; nCimT = Bt[3][:, 256:512]
        for (src3, skey, dstB, dkey) in ((bbr, "F6", Bre, "B2"), (bbi, "F6", Bim, "B2")):
            X = Bt[0][:, 0:256].rearrange("p (j c) -> p j c", j=8)
            k.ts(X[:, :, 0:16], src3, mlohi[:, 0:1], r=[skey, "mlohi"], w=["B0"])
            k.ts(X[:, :, 16:32], src3, mlohi[:, 1:2], r=[skey, "mlohi"], w=["B0"])
            pb, pk = bbank()
            for T in range(2):
                k.tr(pb[:, T * 128:(T + 1) * 128], Bt[0][:, T * 128:(T + 1) * 128], ident_b[:], r=["B0", "ident_b"], w=[pk])
            k.cp(dstB[:], pb[:, 0:256], r=[pk], w=[dkey])
            dstM, dmk = (Brem, "Brem") if dstB is Bre else (Bimm, "Bimm")
            for T in range(2):
                k.tt(v3(dstM[:, T * 512:(T + 1) * 512], 4), bc_mid(dstB[:, T * 128:(T + 1) * 128], 4), bc_last(hm[:], 128), ALU.mult, r=[dkey, "hm"], w=[dmk])
        craw = F[5]
        k.dma("sp", v3(craw[:, 0:128], 2), D["s5_c_re"][l].rearrange("(t g) h p -> (g h) t p", t=2), w=["F5"])
        k.dma("sp", v3(craw[:, 128:256], 2), D["s5_c_im"][l].rearrange("(t g) h p -> (g h) t p", t=2), w=["F5"])
        for (c0, sgn, dstC, dkey) in ((0, 1.0, CreT, "B3"), (128, -1.0, nCimT, "B3")):
            Y = Bt[0][:, 0:256].rearrange("p (t c) -> p t c", t=2)
            cs3 = v3(craw[:, c0:c0 + 128], 2)
            k.ts(Y[:, :, 0:64], cs3, EO[:, 0:1], sgn, op0=ALU.mult, op1=ALU.mult, r=["F5", "EO"], w=["B0"])
            k.ts(Y[:, :, 64:128], cs3, EO[:, 1:2], sgn, op0=ALU.mult, op1=ALU.mult, r=["F5", "EO"], w=["B0"])
            pb, pk = bbank()
            for T in range(2):
                k.tr(pb[:, T * 128:(T + 1) * 128], Bt[0][:, T * 128:(T + 1) * 128], ident_b[:], r=["B0", "ident_b"], w=[pk])
            k.cp(dstC[:], pb[:, 0:256], r=[pk], w=[dkey])
            dstP, dpk = (Cpr, "Cpr") if dstC is CreT else (Cpi, "Cpi")
            k.op("dve", lambda g: g.memset(dstP[:], 0.0), w=[dpk])
            for T in range(2):
                for jj in range(4):
                    k.cp(dstP[:, T * 512 + jj * 128 + 32 * jj:T * 512 + jj * 128 + 32 * jj + 32], dstC[:, T * 128 + 32 * jj:T * 128 + 32 * jj + 32], r=[dkey], w=[dpk])
        for T in range(2):
            k.ts(Dfull[:, T * 128:(T + 1) * 128], ident_f[:], dcol[:, T:T + 1], r=["ident_f", "dcol"], w=["Dfull"])
        ck("s5setup")
        for (tn, key) in ((Sgla, "Sgla"), (Sret, "Sret"), (Sgdn, "Sgdn"), (xst, "xst")):
            k.op("dve", lambda g: g.memset(tn[:], 0.0), w=[key])
        for (tn, key) in ((Sgla_b, "Sgla_b"), (Sret_b, "Sret_b"), (Sgdn_b, "Sgdn_b"), (xc, "xc")):
            k.op("dve", lambda g: g.memset(tn[:], 0.0), w=[key])

        def gla_gen(t):
            k.tag = 'gla0'
            b, bk = bank(pin=True)
            proj_fm(b, bk, 0, 0); proj_fm(b, bk, 1, 128); proj_fm(b, bk, 2, 512, M=16)
            b2, b2k = bank(pin=True)
            proj_tm(b2, b2k, 0, 256, 256); proj_tm(b2, b2k, 256, 528, 256)
            yield
            k.tag = 'gla0'
            k.cp(alow[:], b[0:16, 256:384], r=[bk], w=["alow"])
            k.mm(b[:, 384:512], wa2[:], alow[:], r=["wa2", "alow"], w=[bk])
            e1 = F[0][:, 0:128]; sp_ = F[0][:, 128:256]; cs = F[0][:, 256:384]
            k.act(e1, b[:, 384:512], AF.Exp, scale=-1.0, bias=nba[:, 0:1], r=[bk, "nba"], w=["F0"])
            k.act(sp_, e1, AF.Ln, bias=1.0, r=["F0"], w=["F0"])
            k.op("dve", lambda g: g.tensor_tensor_scan(out=cs, data0=ones_f[:], data1=sp_, initial=0.0, op0=ALU.mult, op1=ALU.add), r=["ones_f", "F0"], w=["F0"])
            eb = F[1][:, 0:128]; enb = F[1][:, 128:256]; eke = F[1][:, 256:384]
            nbl = small[:, 2:3]; ebl = small[:, 3:4]
            k.ts(nbl, cs[:, 127:128], -1.0 / 16, r=["F0"], w=["sm_nbl"])
            k.act(eb, cs, AF.Exp, scale=-1.0 / 16, r=["F0"], w=["F1"])
            k.act(enb, cs, AF.Exp, scale=1.0 / 16, r=["F0"], w=["F1"])
            k.act(eke, cs, AF.Exp, scale=1.0 / 16, bias=nbl, r=["F0", "sm_nbl"], w=["F1"])
            k.act(ebl, nbl, AF.Exp, r=["sm_nbl"], w=["sm_ebl"])
            qtT = Bt[0][:, 0:128]; ktT = Bt[0][:, 128:256]; keT = Bt[0][:, 256:384]
            k.stt(qtT, b[:, 0:128], 32.0 ** -0.5, eb, ALU.mult, ALU.mult, r=[bk, "F1"], w=["B0"])
            k.tt(ktT, b[:, 128:256], enb, ALU.mult, r=[bk, "F1"], w=["B0"])
            k.tt(keT, b[:, 128:256], eke, ALU.mult, r=[bk, "F1"], w=["B0"])
            pb, pk = bbank()
            k.tr(pb[:, 0:128], keT, ident_b[:], r=["B0", "ident_b"], w=[pk])
            kend = Bt[0][:, 384:512]
            k.cp(kend, pb[:, 0:128], r=[pk], w=["B0"], e="act")
            v_bf = Bt[1][:, 0:256]; sr = F[2][:, 0:256]
            k.cp(v_bf, b2[:, 0:256], r=[b2k], w=["B1"], e="act")
            k.act(sr, b2[:, 256:512], AF.Silu, r=[b2k], w=["F2"])
            qtTm = Bt[6]; ktTm = Bt[7]
            k.tt(v3(qtTm[:], 4), bc_mid(qtT, 4), bc_last(hm[:], 128), ALU.mult, r=["B0", "hm"], w=["B6"])
            k.tt(v3(ktTm[:], 4), bc_mid(ktT, 4), bc_last(hm[:], 128), ALU.mult, r=["B0", "hm"], w=["B7"])
            b3, b3k = bank()
            for hh in range(4):
                k.mm(b3[:, hh * 128:(hh + 1) * 128], ktTm[:, hh * 128:(hh + 1) * 128], qtT, r=["B0", "B7"], w=[b3k])
            attn = Bt[2]
            k.tt(v3(attn[:], 4), v3(b3[:], 4), bc_mid(triL[:], 4), ALU.mult, r=[b3k, "triL"], w=["B2"])
            b4, b4k = bank(pin=True)
            for hh in range(4):
                ps_ = slice(32 * hh, 32 * hh + 32); vs = slice(64 * hh, 64 * hh + 64)
                k.mm(b4[:, vs], attn[:, hh * 128:(hh + 1) * 128], v_bf[:, vs], start=True, stop=False, r=["B2", "B1"], w=[b4k])
                k.mm(b4[:, vs], qtTm[:, hh * 128:(hh + 1) * 128], Sgla_b[:, :], start=False, stop=True, r=["B6", "Sgla_b"], w=[b4k])
            unpin(bk); unpin(b2k)
            yield
            k.tag = 'gla0'
            mixa = Bt[1][:, 256:512]
            head_norm_gate(v3(b4[:, 0:256], 4), [b4k], bc_mid(gn_gla[:], 4).rearrange("p a b -> p (a b)") if False else gn_gla[:].unsqueeze(1).broadcast_to([128, 4, 64]),
                           "gn_gla", sr, "F2", mixa, "B1", F[3], "F3", center=False) if False else None
            t1 = F[3][:, 256:512]
            k.act(v3(t1, 4), v3(b4[:, 0:256], 4), AF.Square, r=[b4k], w=["F3"])
            k.op("dve", lambda g: g.tensor_reduce(out=small[:, 12:16], in_=v3(t1, 4), axis=AX.X, op=ALU.add), r=["F3"], w=["sm_v"])
            rstd_from(small[:, 16:20], "sm_r4", small[:, 12:16], "sm_v", 1.0 / 64)
            k.tt(v3(t1, 4), v3(b4[:, 0:256], 4), bc_last(small[:, 16:20], 64), ALU.mult, r=[b4k, "sm_r4"], w=["F3"])
            k.tt(v3(t1, 4), v3(t1, 4), bc_mid(gn_gla[:], 4), ALU.mult, r=["F3", "gn_gla"], w=["F3"])
            k.tt(mixa, t1, sr, ALU.mult, r=["F3", "F2"], w=["B1"])
            to_mixT(mixa, "B1", 0)
            b5, b5k = bank()
            k.mm(b5[:, 0:256], kend, v_bf, r=["B0", "B1"], w=[b5k])
            k.tt(v3(F[1][:, 0:256], 4), v3(b5[:, 0:256], 4), bc_last(hm[:], 64), ALU.mult, r=[b5k, "hm"], w=["F1"])
            k.op("dve", lambda g: g.tensor_reduce(out=F[1][:, 256:320], in_=F[1][:, 0:256].rearrange("p (h v) -> p v h", h=4), axis=AX.X, op=ALU.add), r=["F1"], w=["F1"])
            k.stt(Sgla[:], Sgla[:], ebl, F[1][:, 256:320], ALU.mult, ALU.add, r=["Sgla", "sm_ebl", "F1"], w=["Sgla"])
            k.cp(Sgla_b[:], Sgla[:], r=["Sgla"], w=["Sgla_b"], e="act")

            unpin(b4k)

        def s5_gen(t):
            ck("gla")
            bu, buk = bank()
            proj_fm(bu, buk, 0, 784); proj_fm(bu, buk, 1, 912)
            uT = Bt[3][:, 0:256]
            k.cp(uT, bu[:, 0:256], r=[buk], w=["B3"], e="act")
            yield
            k.tag = 'gla'
            bY, bYk = bank(pin=True)
            for half in range(2):
                bR, bRk = bank(); bI, bIk = bank()
                T = half
                for jj in range(4):
                    k.mm(bR[:, jj * 128:(jj + 1) * 128], Brem[:, T * 512 + jj * 128:T * 512 + (jj + 1) * 128], uT[:, T * 128:(T + 1) * 128], r=["Brem", "B3"], w=[bRk])
                    k.mm(bI[:, jj * 128:(jj + 1) * 128], Bimm[:, T * 512 + jj * 128:T * 512 + (jj + 1) * 128], uT[:, T * 128:(T + 1) * 128], r=["Bimm", "B3"], w=[bIk])
                csl = s5cos[:, half * 512:(half + 1) * 512]; ssl = s5sin[:, half * 512:(half + 1) * 512]
                Vre = F[4]; Vim = F[5]
                (t1_, k1), (t2_, k2) = (F[6], "F6"), (F[7], "F7")
                k.tt(t1_[:], bR[:], csl, ALU.mult, r=[bRk, "s5cos"], w=[k1])
                k.tt(t2_[:], bI[:], ssl, ALU.mult, r=[bIk, "s5sin"], w=[k2])
                k.tt(Vre[:], t1_[:], t2_[:], ALU.add, r=[k1, k2], w=["F4"])
                k.tt(t1_[:], bI[:], csl, ALU.mult, r=[bIk, "s5cos"], w=[k1])
                k.tt(t2_[:], bR[:], ssl, ALU.mult, r=[bRk, "s5sin"], w=[k2])
                k.tt(Vim[:], t1_[:], t2_[:], ALU.subtract, r=[k1, k2], w=["F5"])
                hs = slice(half * 4, half * 4 + 4)
                k.tt(small[:, 56:60], xst[:, hs], s5c[:, 32 + half * 4:36 + half * 4], ALU.mult, r=["xst", "s5c"], w=["sm_xi"])
                k.tt(small[:, 60:64], xst[:, 8 + half * 4:12 + half * 4], s5c[:, 32 + half * 4:36 + half * 4], ALU.mult, r=["xst", "s5c"], w=["sm_xi"])
                k.tt(v3(Vre[:], 4)[:, :, 0], v3(Vre[:], 4)[:, :, 0], small[:, 56:60], ALU.add, r=["F4", "sm_xi"], w=["F4"])
                k.tt(v3(Vim[:], 4)[:, :, 0], v3(Vim[:], 4)[:, :, 0], small[:, 60:64], ALU.add, r=["F5", "sm_xi"], w=["F5"])
                yield
                k.tag = 'gla'
                Wre, Wim = t1_, t2_
                rt = rtab[:, half * 512:(half + 1) * 512]
                k.op("dve", lambda g: g.tensor_tensor_scan(out=Wre[:], data0=rt, data1=Vre[:], initial=0.0, op0=ALU.mult, op1=ALU.add), r=["rtab", "F4"], w=[k1])
                k.op("dve", lambda g: g.tensor_tensor_scan(out=Wim[:], data0=rt, data1=Vim[:], initial=0.0, op0=ALU.mult, op1=ALU.add), r=["rtab", "F5"], w=[k2])
                t3_ = F[4]; t4_ = F[5]; Xrb = Bt[4]; Xib = Bt[5]
                PE_ = "dve"
                k.tt(t3_[:], Wre[:], csl, ALU.mult, r=[k1, "s5cos"], w=["F4"], e=PE_)
                k.tt(t4_[:], Wim[:], ssl, ALU.mult, r=[k2, "s5sin"], w=["F5"], e=PE_)
                k.tt(Xrb[:], t3_[:], t4_[:], ALU.subtract, r=["F4", "F5"], w=["B4"], e=PE_)
                k.cp(xst[:, hs], v3(t3_[:], 4)[:, :, 127], r=["F4"], w=["xst"], e=PE_)
                k.tt(xst[:, hs], xst[:, hs], v3(t4_[:], 4)[:, :, 127], ALU.subtract, r=["xst", "F5"], w=["xst"], e=PE_)
                k.tt(t3_[:], Wre[:], ssl, ALU.mult, r=[k1, "s5sin"], w=["F4"], e=PE_)
                k.tt(t4_[:], Wim[:], csl, ALU.mult, r=[k2, "s5cos"], w=["F5"], e=PE_)
                k.tt(Xib[:], t3_[:], t4_[:], ALU.add, r=["F4", "F5"], w=["B5"], e=PE_)
                k.cp(xst[:, 8 + half * 4:12 + half * 4], v3(t3_[:], 4)[:, :, 127], r=["F4"], w=["xst"], e=PE_)
                k.tt(xst[:, 8 + half * 4:12 + half * 4], xst[:, 8 + half * 4:12 + half * 4], v3(t4_[:], 4)[:, :, 127], ALU.add, r=["xst", "F5"], w=["xst"], e=PE_)
                yield
                k.tag = 'gla'
                o_ = bY[:, T * 128:(T + 1) * 128]
                for jj in range(4):
                    c = slice(jj * 128, (jj + 1) * 128); pc = slice(T * 512 + jj * 128, T * 512 + (jj + 1) * 128)
                    k.mm(o_, Cpr[:, pc], Xrb[:, c], start=(jj == 0), stop=False, r=["Cpr", "B4"], w=[bYk])
                    k.mm(o_, Cpi[:, pc], Xib[:, c], start=False, stop=False, r=["Cpi", "B5"], w=[bYk])
                k.mm(o_, Dfull[:, T * 128:(T + 1) * 128], uT[:, T * 128:(T + 1) * 128], start=False, stop=True, r=["Dfull", "B3"], w=[bYk])
            yield
            k.tag = 'gla'
            y = bY[:, 0:256]; y2 = F[4][:, 0:256]; ge = F[4][:, 256:512]; geb = Bt[3][:, 256:512]
            k.act(y2, y, AF.Square, r=[bYk], w=["F4"])
            k.ts(y2, y2, 0.044715, 1.0, op0=ALU.mult, op1=ALU.add, r=["F4"], w=["F4"])
            k.tt(y2, y2, y, ALU.mult, r=["F4", bYk], w=["F4"])
            k.act(y2, y2, AF.Sigmoid, scale=1.5957691216057308, r=["F4"], w=["F4"])
            k.tt(ge, y2, y, ALU.mult, r=["F4", bYk], w=["F4"])
            k.cp(geb, ge, r=["F4"], w=["B3"], e="act")
            bz, bzk = bank()
            for oi in range(2):
                for kc in range(2):
                    k.mm(bz[:, oi * 128:(oi + 1) * 128], wglu[:, kc, oi * 128:(oi + 1) * 128], geb[:, kc * 128:(kc + 1) * 128],
                         start=(kc == 0), stop=(kc == 1), r=["wglu", "B3"], w=[bzk])
            sgl = F[5][:, 0:256]
            for oi in range(2):
                k.act(sgl[:, oi * 128:(oi + 1) * 128], bz[:, oi * 128:(oi + 1) * 128], AF.Sigmoid, bias=bglu[:, oi:oi + 1], r=[bzk, "bglu"], w=["F5"])
            k.tt(mixT[:, 2:4, :], v3(ge, 2), v3(sgl, 2), ALU.mult, r=["F4", "F5"], w=["mixT"])

            unpin(bYk)

        def ret_gen(t):
            tsl = slice(t * 128, (t + 1) * 128)
            ck("s5")
            bq, bqk = bank(pin=True); bkk_, bkkk = bank(pin=True)
            for i in range(2):
                proj_fm(bq, bqk, i, 1040 + 128 * i)
                proj_fm(bkk_, bkkk, i, 1296 + 128 * i)
            xq = Bt[8][:, 0:256]; xk = Bt[8][:, 256:512]
            k.cp(xq, bq[:, 0:256], r=[bqk], w=["B8"], e="act")
            k.cp(xk, bkk_[:, 0:256], r=[bkkk], w=["B8"], e="act")
            for i in range(2):
                k.mm(bq[:, 256 + i * 128:256 + (i + 1) * 128], Rm[:], xq[:, i * 128:(i + 1) * 128], r=["Rm", "B8"], w=[bqk])
                k.mm(bkk_[:, 256 + i * 128:256 + (i + 1) * 128], Rm[:], xk[:, i * 128:(i + 1) * 128], r=["Rm", "B8"], w=[bkkk])
            k.dma("sp", ropeCt[:], ropeC[:, tsl], r=["ropeC"], w=["ropeCt"])
            k.dma("sp", ropeSt[:], ropeS[:, tsl], r=["ropeS"], w=["ropeSt"])
            yield
            ck("r0")
            cosb = bc_mid(ropeCt[:], 2); sinb = bc_mid(ropeSt[:], 2)
            qT = Bt[0][:, 0:256]; kT = Bt[0][:, 256:512]
            ta = F[0][:, 0:256]; tb = F[0][:, 256:512]
            k.tt(v3(ta, 2), v3(bq[:, 0:256], 2), cosb, ALU.mult, r=[bqk, "ropeCt"], w=["F0"])
            k.tt(v3(tb, 2), v3(bq[:, 256:512], 2), sinb, ALU.mult, r=[bqk, "ropeSt"], w=["F0"])
            k.tt(qT, ta, tb, ALU.add, r=["F0", "F0"], w=["B0"])
            k.stt(v3(ta, 2), v3(bkk_[:, 0:256], 2), 0.125, cosb, ALU.mult, ALU.mult, r=[bkkk, "ropeCt"], w=["F0"])
            k.stt(v3(tb, 2), v3(bkk_[:, 256:512], 2), 0.125, sinb, ALU.mult, ALU.mult, r=[bkkk, "ropeSt"], w=["F0"])
            k.tt(kT, ta, tb, ALU.add, r=["F0", "F0"], w=["B0"])
            unpin(bqk); unpin(bkkk)
            pb, pk = bbank()
            for i in range(2):
                k.tr(pb[:, i * 128:(i + 1) * 128], kT[:, i * 128:(i + 1) * 128], ident_b[:], r=["B0", "ident_b"], w=[pk])
            ck("r1")
            Kz = Bt[1][:, 0:256]
            k.tt(v3(Kz, 4), v3(pb[:, 0:256], 4), bc_last(zeta[:], 64), ALU.mult, r=[pk, "zeta"], w=["B1"])
            b2, b2k = bank()
            proj_tm(b2, b2k, 0, 1552, 512)
            v_bf = Bt[1][:, 256:512]; sg = F[1][:, 0:256]
            k.cp(v_bf, b2[:, 0:256], r=[b2k], w=["B1"], e="act")
            k.act(sg, b2[:, 256:512], AF.Silu, r=[b2k], w=["F1"])
            qTm = Bt[6]; kTm = Bt[7]
            for hh in range(4):
                i = hh // 2
                k.ts(qTm[:, hh * 128:(hh + 1) * 128], qT[:, i * 128:(i + 1) * 128], mlohi[:, hh % 2:hh % 2 + 1], r=["B0", "mlohi"], w=["B6"])
                k.ts(kTm[:, hh * 128:(hh + 1) * 128], kT[:, i * 128:(i + 1) * 128], mlohi[:, hh % 2:hh % 2 + 1], r=["B0", "mlohi"], w=["B7"])
            b3, b3k = bank()
            for hh in range(4):
                i = hh // 2
                k.mm(b3[:, hh * 128:(hh + 1) * 128], kTm[:, hh * 128:(hh + 1) * 128], qT[:, i * 128:(i + 1) * 128], r=["B0", "B7"], w=[b3k])
            ck("r2")
            attn = Bt[2]
            k.tt(attn[:], b3[:], dmask[:], ALU.mult, r=[b3k, "dmask"], w=["B2"])
            b4, b4k = bank(pin=True)
            for hh in range(4):
                i = hh // 2; ps_ = slice(64 * (hh % 2), 64 * (hh % 2) + 64); vs = slice(64 * hh, 64 * hh + 64)
                k.mm(b4[:, vs], attn[:, hh * 128:(hh + 1) * 128], v_bf[:, vs], r=["B2", "B1"], w=[b4k])
                k.mm(b4[:, 256 + 64 * hh:256 + 64 * hh + 64], qTm[:, hh * 128:(hh + 1) * 128], Sret_b[:, i * 64:(i + 1) * 64], r=["B6", "Sret_b"], w=[b4k])
            yield
            k.tag = 'r3'
            o_ = F[3][:, 0:256]; t1 = F[3][:, 256:512]
            k.tt(v3(o_, 4), v3(b4[:, 256:512], 4), bc_last(xi[:], 64), ALU.mult, r=[b4k, "xi"], w=["F3"])
            k.tt(o_, o_, b4[:, 0:256], ALU.add, r=["F3", b4k], w=["F3"])
            k.op("dve", lambda g: g.tensor_reduce(out=small[:, 8:12], in_=v3(o_, 4), axis=AX.X, op=ALU.add), r=["F3"], w=["sm_m"])
            k.ts(small[:, 8:12], small[:, 8:12], 1.0 / 64, r=["sm_m"], w=["sm_m"])
            k.tt(v3(o_, 4), v3(o_, 4), bc_last(small[:, 8:12], 64), ALU.subtract, r=["F3", "sm_m"], w=["F3"])
            k.act(t1, o_, AF.Square, r=["F3"], w=["F3"])
            k.op("dve", lambda g: g.tensor_reduce(out=small[:, 12:16], in_=v3(t1, 4), axis=AX.X, op=ALU.add), r=["F3"], w=["sm_v"])
            rstd_from(small[:, 16:20], "sm_r4", small[:, 12:16], "sm_v", 1.0 / 64)
            k.tt(v3(t1, 4), v3(o_, 4), bc_last(small[:, 16:20], 64), ALU.mult, r=["F3", "sm_r4"], w=["F3"])
            k.tt(t1, t1, gn_ret[:], ALU.mult, r=["F3", "gn_ret"], w=["F3"])
            ck("r3")
            mixc = Bt[8][:, 0:256]
            k.tt(mixc, t1, sg, ALU.mult, r=["F3", "F1"], w=["B8"])
            to_mixT(mixc, "B8", 4)
            b5, b5k = bank()
            for i in range(2):
                k.mm(b5[:, i * 128:(i + 1) * 128], Kz[:, i * 128:(i + 1) * 128], v_bf[:, i * 128:(i + 1) * 128], r=["B1", "B1"], w=[b5k])
            for i in range(2):
                dsl = F[0][:, i * 64:(i + 1) * 64]
                k.ts(dsl, b5[:, i * 128:i * 128 + 64], mlohi[:, 0:1], r=[b5k, "mlohi"], w=["F0"])
                k.stt(dsl, b5[:, i * 128 + 64:(i + 1) * 128], mlohi[:, 1:2], dsl, ALU.mult, ALU.add, r=[b5k, "mlohi", "F0"], w=["F0"])
                k.stt(Sret[:, i * 64:(i + 1) * 64], Sret[:, i * 64:(i + 1) * 64], g128[:, i:i + 1], dsl, ALU.mult, ALU.add,
                      r=["Sret", "g128", "F0"], w=["Sret"])
            k.cp(Sret_b[:], Sret[:], r=["Sret"], w=["Sret_b"], e="act")

            unpin(b4k)

        def gdn_gen(t):
            ck("ret")
            bA, bAk = bank(); bB, bBk = bank()
            for i in range(4):
                proj_fm(bA, bAk, i, 2064 + 128 * i)
            for i in range(2):
                proj_fm(bB, bBk, i, 2064 + 128 * (4 + i))
            k.cp(xc[:, 0:4, 3:131], v3(bA[:], 4), r=[bAk], w=["xc"], e="act")
            k.cp(xc[:, 4:6, 3:131], v3(bB[:, 0:256], 2), r=[bBk], w=["xc"], e="act")
            k.tag = 'gdn.gates'
            b2, b2k = bank()
            proj_tm(b2, b2k, 0, 2832, 264)
            beta = small[:, 20:24]; gg = small[:, 24:28]; lnb = small[:, 28:32]
            sg = F[2][:, 256:512]
            k.act(beta, b2[:, 0:4], AF.Sigmoid, r=[b2k], w=["sm_beta"])
            k.act(sg, b2[:, 8:264], AF.Silu, r=[b2k], w=["F2"])
            k.tt(gg, b2[:, 4:8], dtb_b[:], ALU.add, r=[b2k, "dtb_b"], w=["sm_g"])
            k.act(gg, gg, AF.Exp, r=["sm_g"], w=["sm_g"])
            k.act(gg, gg, AF.Ln, bias=1.0, r=["sm_g"], w=["sm_g"])
            k.tt(gg, gg, nexpA[:], ALU.mult, r=["sm_g", "nexpA"], w=["sm_g"])
            k.act(lnb, beta, AF.Ln, r=["sm_beta"], w=["sm_lnb"])
            bg, bgk = bank()
            k.mm(bg[:, 0:4], triL[:], gg, r=["triL", "sm_g"], w=[bgk])
            k.mm(bg[:, 4:8], ones_f[:], gg, r=["ones_f", "sm_g"], w=[bgk])
            gg2 = gg.rearrange("p (i two) -> p two i", two=2)
            k.mm(bg[:, 8:10], ones_lo[:], gg2[:, 0, :], start=True, stop=False, r=["ones_lo", "sm_g"], w=[bgk])
            k.mm(bg[:, 8:10], ones_hi[:], gg2[:, 1, :], start=False, stop=True, r=["ones_hi", "sm_g"], w=[bgk])
            gcum = small[:, 36:40]; eg = small[:, 40:44]; eke4 = small[:, 44:48]; dlS = small[:, 48:50]; bexp = small[:, 52:56]
            k.cp(gcum, bg[:, 0:4], r=[bgk], w=["sm_gc"])
            k.act(eg, gcum, AF.Exp, r=["sm_gc"], w=["sm_eg"])
            k.tt(eke4, bg[:, 4:8], gcum, ALU.subtract, r=[bgk, "sm_gc"], w=["sm_eke"])
            k.act(eke4, eke4, AF.Exp, r=["sm_eke"], w=["sm_eke"])
            k.act(dlS, bg[:, 8:10], AF.Exp, r=[bgk], w=["sm_dls"])
            k.tt(bexp, beta, eg, ALU.mult, r=["sm_beta", "sm_eg"], w=["sm_bexp"])
            yield
            k.tag = 'gdn.conv'
            bC, bCk = bank(); bD, bDk = bank()
            for i in range(6):
                ob = bC[:, i * 128:(i + 1) * 128] if i < 4 else bD[:, (i - 4) * 128:(i - 3) * 128]
                obk = bCk if i < 4 else bDk
                for j in range(4):
                    k.mm(ob, cdiag[:, (i * 4 + j) * 128:(i * 4 + j + 1) * 128], xc[:, i, j:j + 128], start=(j == 0), stop=(j == 3), r=["cdiag", "xc"], w=[obk])
            k.cp(xc[:, :, 0:3], xc[:, :, 128:131], r=["xc"], w=["xc"])
            qk = F[0]
            vTb = Bt[0][:, 0:256]
            k.act(qk[:], bC[:], AF.Silu, r=[bCk], w=["F0"])
            k.act(vTb, bD[:, 0:256], AF.Silu, r=[bDk], w=["B0"])
            sq = F[1]
            k.act(sq[:], qk[:], AF.Square, r=["F0"], w=["F1"])
            bn, bnk = bank()
            for i in range(4):
                k.mm(bn[:, i * 128:(i + 1) * 128], blk1[:], sq[:, i * 128:(i + 1) * 128], r=["blk1", "F1"], w=[bnk])
            rinv = F[1]
            k.act(rinv[:], bn[:], AF.Ln, bias=EPS, r=[bnk], w=["F1"])
            k.act(rinv[:], rinv[:], AF.Exp, scale=-0.5, r=["F1"], w=["F1"])
            qkT = Bt[1]
            k.stt(qkT[:, 0:256], qk[:, 0:256], 0.125, rinv[:, 0:256], ALU.mult, ALU.mult, r=["F0", "F1"], w=["B1"])
            k.tt(qkT[:, 256:512], qk[:, 256:512], rinv[:, 256:512], ALU.mult, r=["F0", "F1"], w=["B1"])
            pb, pk = bbank()
            for i in range(2):
                k.tr(pb[:, i * 128:(i + 1) * 128], qkT[:, 256 + i * 128:256 + (i + 1) * 128], ident_b[:], r=["B1", "ident_b"], w=[pk])
                k.tr(pb[:, 256 + i * 128:256 + (i + 1) * 128], vTb[:, i * 128:(i + 1) * 128], ident_b[:], r=["B0", "ident_b"], w=[pk])
            Ru = Bt[3][:, 0:256]; Rw = Bt[3][:, 256:512]; Kend = Bt[4][:, 0:256]
            k.tt(v3(Ru, 4), v3(pb[:, 256:512], 4), bc_last(beta, 64), ALU.mult, r=[pk, "sm_beta"], w=["B3"])
            k.tt(v3(Rw, 4), v3(pb[:, 0:256], 4), bc_last(bexp, 64), ALU.mult, r=[pk, "sm_bexp"], w=["B3"])
            k.tt(v3(Kend, 4), v3(pb[:, 0:256], 4), bc_last(eke4, 64), ALU.mult, r=[pk, "sm_eke"], w=["B4"])
            k.tag = 'gdn.decay'
            rhsA = F[3]; rhsB = F[4]
            k.tt(v3(rhsA[:], 4), bc_mid(triL[:], 4), bc_last(gg, 128), ALU.mult, r=["triL", "sm_g"], w=["F3"])
            k.tt(v3(rhsB[:], 4), bc_mid(ident_f[:], 4), bc_last(lnb, 128), ALU.mult, r=["ident_f", "sm_lnb"], w=["F4"])
            k.tt(rhsB[:], rhsB[:], rhsA[:], ALU.add, r=["F3", "F4"], w=["F4"])
            bD1, bD1k = bank(); bD2, bD2k = bank()
            for hh in range(4):
                c = slice(hh * 128, (hh + 1) * 128)
                k.mm(bD1[:, c], Ust[:], rhsA[:, c], r=["Ust", "F3"], w=[bD1k])
                k.mm(bD2[:, c], Ust[:], rhsB[:, c], r=["Ust", "F4"], w=[bD2k])
            E1 = F[5]; E2 = F[6]
            k.act(E1[:], bD1[:], AF.Exp, r=[bD1k], w=["F5"])
            k.act(E2[:], bD2[:], AF.Exp, r=[bD2k], w=["F6"])
            bK, bKk = bank(); bQ, bQk = bank()
            kTm = Bt[7]
            for hh in range(4):
                i = hh // 2
                k.ts(kTm[:, hh * 128:(hh + 1) * 128], qkT[:, 256 + i * 128:256 + (i + 1) * 128], mlohi[:, hh % 2:hh % 2 + 1], r=["B1", "mlohi"], w=["B7"])
            for hh in range(4):
                i = hh // 2; c = slice(hh * 128, (hh + 1) * 128)
                k.mm(bK[:, c], kTm[:, c], qkT[:, 256 + i * 128:256 + (i + 1) * 128], r=["B1", "B7"], w=[bKk])
                k.mm(bQ[:, c], kTm[:, c], qkT[:, i * 128:(i + 1) * 128], r=["B1", "B7"], w=[bQk])
            aqk = Bt[2]; Mb = Bt[5]; Nb = Bt[10]
            k.tt(E1[:], E1[:], bQ[:], ALU.mult, r=["F5", bQk], w=["F5"])
            k.tt(v3(aqk[:], 4), v3(E1[:], 4), bc_mid(triL[:], 4), ALU.mult, r=["F5", "triL"], w=["B2"])
            k.tt(E2[:], E2[:], bK[:], ALU.mult, r=["F6", bKk], w=["F6"])
            k.tt(v3(Mb[:], 4), v3(E2[:], 4), bc_mid(negU[:], 4), ALU.mult, r=["F6", "negU"], w=["B5"])
            pb2, pk2 = bbank()
            for hh in range(4):
                c = slice(hh * 128, (hh + 1) * 128)
                k.tr(pb2[:, c], Mb[:, c], ident_b[:], r=["B5", "ident_b"], w=[pk2])
            k.cp(Nb[:], pb2[:, 0:512], r=[pk2], w=["B10"], e="act")
            k.tag = 'gdn.solve'
            Ttb_t = Bt[6]; Qb = Bt[0]; Db = Bt[9]; Pb2 = Bt[11]
            M0 = Bt[7]; N0 = Bt[8]
            k.tt(v3(M0[:], 4), v3(Mb[:], 4), bc_mid(gmask[:, 0:128], 4), ALU.mult, r=["B5", "gmask"], w=["B7"])
            k.tt(v3(N0[:], 4), v3(Nb[:], 4), bc_mid(gmask[:, 0:128], 4), ALU.mult, r=["B10", "gmask"], w=["B8"])
            k.tt(v3(Ttb_t[:], 4), v3(M0[:], 4), bc_mid(ident_b[:], 4), ALU.add, r=["B7", "ident_b"], w=["B6"])
            bM, bMk = bank(); bN, bNk = bank()
            for hh in range(4):
                c = slice(hh * 128, (hh + 1) * 128)
                k.mm(bN[:, c], M0[:, c], N0[:, c], r=["B7", "B8"], w=[bNk])
                k.mm(bM[:, c], N0[:, c], M0[:, c], r=["B7", "B8"], w=[bMk])
            k.cp(Qb[:], bN[:], r=[bNk], w=["B0"], e="act")
            k.cp(Db[:], bM[:], r=[bMk], w=["B9"])
            bT, bTk = bank()
            for hh in range(4):
                c = slice(hh * 128, (hh + 1) * 128)
                k.mm(bT[:, c], ident_b[:], Ttb_t[:, c], start=True, stop=False, r=["ident_b", "B6"], w=[bTk])
                k.mm(bT[:, c], Qb[:, c], Ttb_t[:, c], start=False, stop=True, r=["B0", "B6"], w=[bTk])
            k.cp(Ttb_t[:], bT[:], r=[bTk], w=["B6"], e="act")
            bN, bNk = bank()
            for hh in range(4):
                c = slice(hh * 128, (hh + 1) * 128)
                k.mm(bN[:, c], Db[:, c], Qb[:, c], r=["B9", "B0"], w=[bNk])
            k.cp(N0[:], bN[:], r=[bNk], w=["B8"])
            bT, bTk = bank()
            for hh in range(4):
                c = slice(hh * 128, (hh + 1) * 128)
                k.mm(bT[:, c], ident_b[:], Ttb_t[:, c], start=True, stop=False, r=["ident_b", "B6"], w=[bTk])
                k.mm(bT[:, c], N0[:, c], Ttb_t[:, c], start=False, stop=True, r=["B8", "B6"], w=[bTk])
            k.cp(Ttb_t[:], bT[:], r=[bTk], w=["B6"], e="act")
            pb3, pk3 = bbank()
            for hh in range(4):
                c = slice(hh * 128, (hh + 1) * 128)
                k.tr(pb3[:, c], Ttb_t[:, c], ident_b[:], r=["B6", "ident_b"], w=[pk3])
            k.cp(Db[:], pb3[:, 0:512], r=[pk3], w=["B9"])
            Nm = Bt[8]; Mm = Bt[7]
            for li_, m_ in enumerate((8, 16, 32, 64)):
                last = (m_ == 64)
                k.tt(v3(Nm[:], 4), v3(Nb[:], 4), bc_mid(gmask[:, (1 + li_) * 128:(2 + li_) * 128], 4), ALU.mult, r=["B10", "gmask"], w=["B8"])
                if not last:
                    k.tt(v3(Mm[:], 4), v3(Mb[:], 4), bc_mid(gmask[:, (5 + li_) * 128:(6 + li_) * 128], 4), ALU.mult, r=["B5", "gmask"], w=["B7"])
                bP, bPk = bank()
                for hh in range(4):
                    c = slice(hh * 128, (hh + 1) * 128)
                    k.mm(bP[:, c], Nm[:, c], Ttb_t[:, c], r=["B8", "B6"], w=[bPk])
                if not last:
                    bQ2, bQ2k = bank()
                    for hh in range(4):
                        c = slice(hh * 128, (hh + 1) * 128)
                        k.mm(bQ2[:, c], Mm[:, c], Db[:, c], r=["B7", "B9"], w=[bQ2k])
                k.cp(Qb[:], bP[:], r=[bPk], w=["B0"], e="act")
                if not last:
                    k.cp(Pb2[:], bQ2[:], r=[bQ2k], w=["B11"])
                bT, bTk = bank()
                for hh in range(4):
                    c = slice(hh * 128, (hh + 1) * 128)
                    k.mm(bT[:, c], ident_b[:], Ttb_t[:, c], start=True, stop=False, r=["ident_b", "B6"], w=[bTk])
                    k.mm(bT[:, c], Db[:, c], Qb[:, c], start=False, stop=True, r=["B9", "B0"], w=[bTk])
                if not last:
                    bD_, bD_k = bank()
                    for hh in range(4):
                        c = slice(hh * 128, (hh + 1) * 128)
                        k.mm(bD_[:, c], ident_b[:], Db[:, c], start=True, stop=False, r=["ident_b", "B9"], w=[bD_k])
                        k.mm(bD_[:, c], Ttb_t[:, c], Pb2[:, c], start=False, stop=True, r=["B6", "B11"], w=[bD_k])
                k.cp(Ttb_t[:], bT[:], r=[bTk], w=["B6"], e="act")
                if not last:
                    k.cp(Db[:], bD_[:], r=[bD_k], w=["B9"])
            k.tag = 'gdn.state'
            bW, bWk = bank()
            nwT = Bt[9]
            for hh in range(4):
                i = hh // 2; c = slice(hh * 128, (hh + 1) * 128)
                k.mm(bW[:, c], Rw[:, i * 128:(i + 1) * 128], Ttb_t[:, c], r=["B3", "B6"], w=[bWk])
            for hh in range(4):
                c = slice(hh * 128, (hh + 1) * 128)
                k.ts(nwT[:, c], bW[:, c], mlohi[:, hh % 2:hh % 2 + 1], -1.0, op0=ALU.mult, op1=ALU.mult, r=[bWk, "mlohi"], w=["B9"])
            bV, bVk = bank()
            for hh in range(4):
                i = hh // 2; ps_ = slice(64 * (hh % 2), 64 * (hh % 2) + 64); c = slice(hh * 128, (hh + 1) * 128); vs = slice(64 * hh, 64 * hh + 64)
                k.mm(bV[:, vs], Ttb_t[:, c], Ru[:, vs], start=True, stop=False, r=["B6", "B3"], w=[bVk])
                k.mm(bV[:, vs], nwT[:, c], Sgdn_b[:, i * 64:(i + 1) * 64], start=False, stop=True, r=["B9", "Sgdn_b"], w=[bVk])
            vnew = Bt[11][:, 0:256]
            k.cp(vnew, bV[:, 0:256], r=[bVk], w=["B11"], e="act")
            qTm = Bt[8]
            for hh in range(4):
                i = hh // 2
                k.ts(qTm[:, hh * 128:(hh + 1) * 128], qkT[:, i * 128:(i + 1) * 128], mlohi[:, hh % 2:hh % 2 + 1], r=["B1", "mlohi"], w=["B8"])
            bO, bOk = bank(pin=True)
            for hh in range(4):
                i = hh // 2; ps_ = slice(64 * (hh % 2), 64 * (hh % 2) + 64); c = slice(hh * 128, (hh + 1) * 128); vs = slice(64 * hh, 64 * hh + 64)
                k.mm(bO[:, vs], aqk[:, c], vnew[:, vs], r=["B2", "B11"], w=[bOk])
                k.mm(bO[:, 256 + 64 * hh:256 + 64 * hh + 64], qTm[:, c], Sgdn_b[:, i * 64:(i + 1) * 64], r=["B8", "Sgdn_b"], w=[bOk])
            yield
            k.tag = 'gdn.state'
            o_ = F[3][:, 0:256]; t1 = F[3][:, 256:512]
            k.tt(v3(o_, 4), v3(bO[:, 256:512], 4), bc_last(eg, 64), ALU.mult, r=[bOk, "sm_eg"], w=["F3"])
            k.tt(o_, o_, bO[:, 0:256], ALU.add, r=["F3", bOk], w=["F3"])
            k.act(t1, o_, AF.Square, r=["F3"], w=["F3"])
            k.op("dve", lambda g: g.tensor_reduce(out=small[:, 12:16], in_=v3(t1, 4), axis=AX.X, op=ALU.add), r=["F3"], w=["sm_v"])
            rstd_from(small[:, 16:20], "sm_r4", small[:, 12:16], "sm_v", 1.0 / 64)
            k.tt(v3(t1, 4), v3(o_, 4), bc_last(small[:, 16:20], 64), ALU.mult, r=["F3", "sm_r4"], w=["F3"])
            k.tt(v3(t1, 4), v3(t1, 4), bc_mid(gn_gdn[:], 4), ALU.mult, r=["F3", "gn_gdn"], w=["F3"])
            mixd = Bt[5][:, 0:256]
            k.tt(mixd, t1, sg, ALU.mult, r=["F3", "F2"], w=["B5"])
            to_mixT(mixd, "B5", 6)
            bS, bSk = bank()
            for i in range(2):
                k.mm(bS[:, i * 128:(i + 1) * 128], Kend[:, i * 128:(i + 1) * 128], vnew[:, i * 128:(i + 1) * 128], r=["B4", "B11"], w=[bSk])
            for i in range(2):
                dsl = F[3][:, i * 64:(i + 1) * 64]
                k.ts(dsl, bS[:, i * 128:i * 128 + 64], mlohi[:, 0:1], r=[bSk, "mlohi"], w=["F3"])
                k.stt(dsl, bS[:, i * 128 + 64:(i + 1) * 128], mlohi[:, 1:2], dsl, ALU.mult, ALU.add, r=[bSk, "mlohi", "F3"], w=["F3"])
                k.stt(Sgdn[:, i * 64:(i + 1) * 64], Sgdn[:, i * 64:(i + 1) * 64], dlS[:, i:i + 1], dsl, ALU.mult, ALU.add,
                      r=["Sgdn", "sm_dls", "F3"], w=["Sgdn"])
            k.cp(Sgdn_b[:], Sgdn[:], r=["Sgdn"], w=["Sgdn_b"], e="act")

            unpin(bOk)

        def outproj(t):
            tsl = slice(t * 128, (t + 1) * 128)
            ck("gdn")
            if dbg and l == 0:
                k.cp(F[0][:, 0:512], mixT[:, 0:4, :].rearrange("p a b -> p (a b)"), r=["mixT"], w=["F0"])
                k.cp(F[1][:, 0:512], mixT[:, 4:8, :].rearrange("p a b -> p (a b)"), r=["mixT"], w=["F1"])
                k.dma("sp", dbgo[:, 0:4, tsl], v3(F[0][:], 4), r=["F0"], w=["dbgo"])
                k.dma("sp", dbgo[:, 4:8, tsl], v3(F[1][:], 4), r=["F1"], w=["dbgo"])
            k.tag = 'outproj'
            for nh in range(2):
                bo, bok = bank()
                for kc in range(8):
                    k.mm(bo[:], mixT[:, kc, :], wout[:, kc, nh * 512:(nh + 1) * 512], start=(kc == 0), stop=(kc == 7), r=["mixT", "wout"], w=[bok])
                k.tt(h[:, t, nh * 512:(nh + 1) * 512], h[:, t, nh * 512:(nh + 1) * 512], bo[:], ALU.add, r=["h", bok], w=["h"])

        def step(g):
            try:
                next(g)
            except StopIteration:
                pass

        k.tag = 'normT'
        norm_T(h[:, 0, :], "gcol", hnT, "hnT")
        gcur = gla_gen(0); step(gcur); step(gcur)
        for t in range(NT):
            s5g = s5_gen(t); step(s5g)
            step(s5g)
            step(gcur)
            step(s5g)
            rtg = ret_gen(t); step(rtg)
            step(s5g)
            step(rtg)
            step(s5g)
            step(s5g)
            gdg = gdn_gen(t); step(gdg)
            step(s5g)
            step(rtg)
            step(gdg)
            if t + 1 < NT:
                k.tag = 'normT'
                norm_T(h[:, t + 1, :], "gcol", hnT, "hnT")
                gcur = gla_gen(t + 1); step(gcur); step(gcur)
            step(gdg)
            outproj(t)

        ck("phaseA")
        k.barrier()
        colload(gcol[:, 0:8], "gcol", D["norm_ffn"][l].rearrange("(a b) -> a b", b=128), 8)
        for t in range(NT):
            norm_T(h[:, t, :], "gcol", hn2T, "hn2T", ntok_off=t * 128)
        GT = min(4, NT)
        NG = NT // GT
        NTOK = GT * 128
        pieces = []
        f0 = 0
        while f0 < 22:
            fn = min(3, 22 - f0)
            pieces.append((f0, fn)); f0 += fn
        for pi, (f0, fn) in enumerate(pieces):
            bsel = pi % 2
            wupb = W[:, bsel * 9216:bsel * 9216 + 6144].rearrange("p (k n) -> p k n", k=8)
            wdnb = W[:, bsel * 9216 + 6144:bsel * 9216 + 9216].rearrange("p (f n) -> p f n", f=3)
            actb = W[:, 28672 + bsel * 1536:28672 + (bsel + 1) * 1536].rearrange("p (f n) -> p f n", f=3)
            ku, kd, ka = f"wup{bsel}", f"wdn{bsel}", f"actT{bsel}"
            for kc in range(8):
                k.dma("pool", wupb[:, kc, 0:fn * 128], D["w_ffn_up"][l, kc * 128:(kc + 1) * 128, f0 * 128:(f0 + fn) * 128], w=[ku])
                k.dma("pool", wupb[:, kc, 384:384 + fn * 128], D["w_ffn_up"][l, kc * 128:(kc + 1) * 128, FFH + f0 * 128:FFH + (f0 + fn) * 128], w=[ku])
            for fi in range(fn):
                k.dma("pool", wdnb[:, fi, :], D["w_ffn_down"][l, (f0 + fi) * 128:(f0 + fi + 1) * 128, :], w=[kd])
            if pi == 0:
                for kc in range(8):
                    k.dma("pool", wpg[:, kc, :], D["w_ple_gate"][l, kc * 128:(kc + 1) * 128, :], w=["wpg"])
                k.dma("pool", wpp[:], D["w_ple_proj"][l].rearrange("(kc q) n -> q kc n", q=128), w=["wpp"])
            for gi in range(NG):
                g0 = gi * NTOK
                for fi in range(fn):
                    bg_, bgk_ = bank(); bu_, buk_ = bank()
                    for kc in range(8):
                        k.mm(bg_[:, 0:NTOK], wupb[:, kc, fi * 128:(fi + 1) * 128], hn2T[:, kc, g0:g0 + NTOK], start=(kc == 0), stop=(kc == 7), r=[ku, "hn2T"], w=[bgk_])
                    for kc in range(8):
                        k.mm(bu_[:, 0:NTOK], wupb[:, kc, 384 + fi * 128:384 + (fi + 1) * 128], hn2T[:, kc, g0:g0 + NTOK], start=(kc == 0), stop=(kc == 7), r=[ku, "hn2T"], w=[buk_])
                    sgt = Bt[fi % 2]
                    k.act(sgt[:, 0:NTOK], bg_[:, 0:NTOK], AF.Silu, r=[bgk_], w=[f"B{fi % 2}"])
                    k.tt(actb[:, fi, 0:NTOK], sgt[:, 0:NTOK], bu_[:, 0:NTOK], ALU.mult, r=[f"B{fi % 2}", buk_], w=[ka])
                for tt_ in range(GT):
                    t = gi * GT + tt_
                    for nh in range(2):
                        bo, bok = bank()
                        for fi in range(fn):
                            k.mm(bo[:], actb[:, fi, tt_ * 128:(tt_ + 1) * 128], wdnb[:, fi, nh * 512:(nh + 1) * 512], start=(fi == 0), stop=(fi == fn - 1), r=[ka, kd], w=[bok])
                        k.tt(h[:, t, nh * 512:(nh + 1) * 512], h[:, t, nh * 512:(nh + 1) * 512], bo[:], ALU.add, r=["h", bok], w=["h"])
        ck("ffn")
        k.barrier()
        colload(gcol[:, 0:8], "gcol", D["norm_ple"][l].rearrange("(a b) -> a b", b=128), 8)
        for t in range(NT):
            norm_T(h[:, t, :], "gcol", hn2T, "hn2T", ntok_off=t * 128)
        ptmp = A2[:, 8192:8704]
        for t in range(NT):
            pin = Bt[t % 2]; pT = Bt[2 + t % 2]
            k.dma("pool", pin[:, 0:256], D["p"][l, t * 128:(t + 1) * 128, :], w=[f"B{t % 2}"])
            pb, pk = bbank()
            for i in range(2):
                k.tr(pb[:, i * 128:(i + 1) * 128], pin[:, i * 128:(i + 1) * 128], ident_b[:], r=[f"B{t % 2}", "ident_b"], w=[pk])
            k.cp(pT[:, 0:256], pb[:, 0:256], r=[pk], w=[f"B{2 + t % 2}"], e="act")
            for nh in range(2):
                bg_, bgk_ = bank(); bp_, bpk_ = bank()
                for kc in range(8):
                    k.mm(bg_[:], hn2T[:, kc, t * 128:(t + 1) * 128], wpg[:, kc, nh * 512:(nh + 1) * 512], start=(kc == 0), stop=(kc == 7), r=["hn2T", "wpg"], w=[bgk_])
                for kc in range(2):
                    k.mm(bp_[:], pT[:, kc * 128:(kc + 1) * 128], wpp[:, kc, nh * 512:(nh + 1) * 512], start=(kc == 0), stop=(kc == 1), r=[f"B{2 + t % 2}", "wpp"], w=[bpk_])
                k.act(ptmp, bg_[:], AF.Sigmoid, r=[bgk_], w=["ptmp"])
                k.tt(ptmp, ptmp, bp_[:], ALU.mult, r=["ptmp", bpk_], w=["ptmp"])
                k.tt(h[:, t, nh * 512:(nh + 1) * 512], h[:, t, nh * 512:(nh + 1) * 512], ptmp, ALU.add, r=["h", "ptmp"], w=["h"])

    k.barrier()
    ck("ple")
    k.dma("sp", F[2][:], D["norm_final"][0:1, 0:512].partition_broadcast(128), w=["F2"])
    k.dma("sp", F[3][:], D["norm_final"][0:1, 512:1024].partition_broadcast(128), w=["F3"])
    for t in range(NT):
        ss = small[:, 0:1]
        k.act(hn[:], h[:, t, :], AF.Square, accum_out=ss, r=["h"], w=["hn", "sm_ss"])
        rstd_from(small[:, 1:2], "sm_rs", ss, "sm_ss", 1.0 / DM)
        for nh in range(2):
            ob = F[4 + nh]
            k.stt(ob[:], h[:, t, nh * 512:(nh + 1) * 512], small[:, 1:2], F[2 + nh][:], ALU.mult, ALU.mult, r=["h", "sm_rs", f"F{2 + nh}"], w=[f"F{4 + nh}"])
            k.dma("sp", out[t * 128:(t + 1) * 128, nh * 512:(nh + 1) * 512], ob[:], r=[f"F{4 + nh}"], w=["out"])
    k.finish("sp")
    _K[0] = k
    print("built: insts", k.cnt, "waits", k.nwaits, "sbuf_left", nc.sbuf_bytes_remaining, flush=True)
    return nc


_CACHE = {}


def kernel(**inputs):
    L, depth = 2048, 2
    if "nc" not in _CACHE:
        _CACHE["nc"] = build(L, depth)
    nc = _CACHE["nc"]
    shared = {}
    for nm, _ in PSH:
        shared[nm] = np.ascontiguousarray(np.asarray(inputs[nm], dtype=np.float32))
    shared["norm_final"] = np.ascontiguousarray(np.asarray(inputs["norm_final"], dtype=np.float32).reshape(1, 1024))
    x = np.asarray(inputs["x"], dtype=np.float32)
    p = np.asarray(inputs["p"], dtype=np.float32)
    pos = np.asarray(inputs["positions"]).astype(np.int32)
    in_maps = []
    for b in range(8):
        m = dict(shared)
        m["x"] = np.ascontiguousarray(x[b])
        m["p"] = np.ascontiguousarray(p[:, b])
        m["positions"] = np.ascontiguousarray(pos[b:b + 1])
        in_maps.append(m)
    res = run_bass_kernel_spmd(nc, in_maps, core_ids=list(range(8)))
    return np.stack([np.asarray(r["out"], dtype=np.float32) for r in res.results], axis=0)
```

```python
import numpy as np
import concourse.bass as bass
import concourse.mybir as mybir
from contextlib import ExitStack

F32 = mybir.dt.float32
BF16 = mybir.dt.bfloat16
I32 = mybir.dt.int32
ALU = mybir.AluOpType
AF = mybir.ActivationFunctionType
AX = mybir.AxisListType

SEM_CH = 20000
N_DMA_SEMS = 24


_ESZ = {}


def _esize(dt):
    v = _ESZ.get(dt)
    if v is None:
        n = str(dt)
        v = 2 if ("bfloat16" in n or "float16" in n or "int16" in n) else (1 if "8" in n else 4)
        _ESZ[dt] = v
    return v


def _region(a):
    dims = a.ap
    es = _esize(a.dtype)
    ps, pc = dims[0]
    off = a.offset
    if ps > 0:
        p0 = off // ps
        fo = off % ps
        p1 = p0 + pc
    else:
        p0, p1, fo = 0, 1 << 30, off
    lo = hi = fo
    for st, c in dims[1:]:
        ext = st * (c - 1)
        if ext >= 0:
            hi += ext
        else:
            lo += ext
    if type(a.tensor).__name__ == "PSumTensorHandle":
        return (a.tensor.name, p0, p1, 0, 1 << 30)
    return (a.tensor.name, p0, p1, lo * es, (hi + 1) * es)


class _Rec:
    def __init__(self, eng):
        self._eng = eng
        self.reads = []
        self.writes = []

    def __getattr__(self, name):
        f = getattr(self._eng, name)

        def call(*a, **kw):
            for i, v in enumerate(a):
                if type(v).__name__ == "AP":
                    (self.writes if i == 0 else self.reads).append(v)
            for kn, v in kw.items():
                if type(v).__name__ == "AP":
                    (self.writes if kn in ("out", "accum_out", "ap", "out_ap") else self.reads).append(v)
            return f(*a, **kw)
        return call


class _Stub:
    def __init__(self):
        self.reads = []
        self.writes = []

    def __getattr__(self, name):
        def call(*a, **kw):
            for i, v in enumerate(a):
                if type(v).__name__ == "AP":
                    (self.writes if i == 0 else self.reads).append(v)
            for kn, v in kw.items():
                if type(v).__name__ == "AP":
                    (self.writes if kn in ("out", "accum_out", "ap", "out_ap") else self.reads).append(v)
            return None
        return call


class KB:
    def __init__(self, nc, est=None):
        self.nc = nc
        self.st = ExitStack()
        self.eng = {"pe": nc.tensor, "act": nc.scalar, "dve": nc.vector, "pool": nc.gpsimd, "sp": nc.sync}
        self.cnt = {e: 0 for e in self.eng}
        self.sems = {e: [] for e in self.eng}
        self.clock = {e: {} for e in self.eng}
        self.hist = {}
        self.last_w = {}
        self.readers = {}
        self.events = {}
        self.dsem = [self.st.enter_context(nc.semaphore(f"dma{i}")) for i in range(N_DMA_SEMS)]
        self.dcnt = [0] * N_DMA_SEMS
        self.dnext = 0
        self.dnext_pool = 0
        self.nwaits = 0
        self._uid = 0
        self.tag = 'start'
        self.names = {}

    def sb(self, name, shape, dt=F32):
        return self.st.enter_context(self.nc.sbuf_tensor(name, list(shape), dt))

    def ps(self, name, shape, dt=F32):
        return self.st.enter_context(self.nc.psum_tensor(name, list(shape), dt))

    def _sem_for(self, e, n):
        i = (n - 1) // SEM_CH
        while len(self.sems[e]) <= i:
            self.sems[e].append(self.st.enter_context(self.nc.semaphore(f"t_{e}_{len(self.sems[e])}")))
        return self.sems[e][i], ((n - 1) % SEM_CH) + 1

    def _wait(self, e, src, n):
        if self.clock[e].get(src, 0) >= n:
            return
        if isinstance(src, int):
            self.eng[e].wait_ge(self.dsem[src], 16 * n)
        else:
            sem, val = self._sem_for(src, n)
            self.eng[e].wait_ge(sem, val)
        self.nwaits += 1
        ck = self.clock[e]
        for s2, n2 in self.hist.get((src, n), {}).items():
            if ck.get(s2, 0) < n2:
                ck[s2] = n2
        if ck.get(src, 0) < n:
            ck[src] = n

    def _deps(self, e, r, w):
        need = {}

        def add(sn):
            if sn is None:
                return
            s, n = sn
            if s == "pe" and e == "pe":
                return
            if need.get(s, 0) < n:
                need[s] = n
        for key in r:
            add(self.last_w.get(key))
        for key in w:
            add(self.last_w.get(key))
            for s, n in self.readers.get(key, {}).items():
                add((s, n))
        for s, n in need.items():
            self._wait(e, s, n)

    def _rdeps(self, e, reads, writes):
        need = {}
        for kind, regs in (("r", reads), ("w", writes)):
            for (nm, p0, p1, lo, hi) in regs:
                for ev in self.events.get(nm, ()):
                    if kind == "r" and ev[4] != "w":
                        continue
                    if ev[0] < p1 and p0 < ev[1] and ev[2] < hi and lo < ev[3]:
                        s_ = ev[5]
                        if s_ == "pe" and e == "pe":
                            continue
                        if need.get(s_, 0) < ev[6]:
                            need[s_] = ev[6]
        for s_, n in need.items():
            self._wait(e, s_, n)

    def _rupdate(self, src, n, reads, writes):
        for (nm, p0, p1, lo, hi) in reads:
            lst = self.events.setdefault(nm, [])
            for ev in lst:
                if ev[4] == "r" and ev[5] == src and ev[0] == p0 and ev[1] == p1 and ev[2] == lo and ev[3] == hi:
                    ev[6] = n
                    break
            else:
                lst.append([p0, p1, lo, hi, "r", src, n])
        for (nm, p0, p1, lo, hi) in writes:
            lst = self.events.setdefault(nm, [])
            lst[:] = [ev for ev in lst if not (p0 <= ev[0] and ev[1] <= p1 and lo <= ev[2] and ev[3] <= hi)]
            lst.append([p0, p1, lo, hi, "w", src, n])

    def op(self, e, fn, r=(), w=()):
        rec = _Rec(self.eng[e])
        stub = _Stub()
        fn(stub)
        reads = [_region(a) for a in stub.reads]
        writes = [_region(a) for a in stub.writes]
        self._rdeps(e, reads, writes)
        inst = fn(self.eng[e])
        n = self.cnt[e] = self.cnt[e] + 1
        sem, _ = self._sem_for(e, n)
        inst.then_inc(sem, 1)
        try:
            self.names[inst.ins.name] = self.tag
        except Exception:
            pass
        h = dict(self.clock[e])
        h[e] = n
        self.hist[(e, n)] = h
        self._rupdate(e, n, reads, writes)
        return inst

    def dma(self, q, out, in_, r=(), w=(), **kw):
        reads = [_region(in_)]
        writes = [_region(out)]
        self._rdeps(q, reads, writes)
        half = N_DMA_SEMS // 2
        if q == "pool":
            slot = half + self.dnext_pool
            self.dnext_pool = (self.dnext_pool + 1) % half
        else:
            slot = self.dnext
            self.dnext = (self.dnext + 1) % half
        if self.dcnt[slot] > 0:
            self._wait(q, slot, self.dcnt[slot])
        inst = self.eng[q].dma_start(out=out, in_=in_, **kw)
        inst.then_inc(self.dsem[slot], 16)
        n = self.dcnt[slot] = self.dcnt[slot] + 1
        self.hist[(slot, n)] = dict(self.clock[q])
        self._rupdate(slot, n, reads, writes)
        return inst

    def finish(self, e="sp"):
        for s in list(self.eng):
            if s != e and self.cnt[s] > 0:
                self._wait(e, s, self.cnt[s])
        for slot in range(N_DMA_SEMS):
            if self.dcnt[slot] > 0:
                self._wait(e, slot, self.dcnt[slot])

    def mm(self, out, lhsT, rhs, start=True, stop=True, r=(), w=()):
        return self.op("pe", lambda e: e.matmul(out, lhsT=lhsT, rhs=rhs, start=start, stop=stop), r=r, w=w)

    def tr(self, out, in_, ident, r=(), w=()):
        return self.op("pe", lambda e: e.transpose(out, in_, ident), r=r, w=w)

    def act(self, out, in_, func, r=(), w=(), **kw):
        return self.op("act", lambda e: e.activation(out=out, in_=in_, func=func, **kw), r=r, w=w)

    def tt(self, out, in0, in1, op, r=(), w=(), e="dve"):
        return self.op(e, lambda g: g.tensor_tensor(out=out, in0=in0, in1=in1, op=op), r=r, w=w)

    def ts(self, out, in0, s1, s2=None, op0=ALU.mult, op1=None, r=(), w=(), e="dve", **kw):
        if op1 is None:
            return self.op(e, lambda g: g.tensor_scalar(out=out, in0=in0, scalar1=s1, scalar2=None, op0=op0, **kw), r=r, w=w)
        return self.op(e, lambda g: g.tensor_scalar(out=out, in0=in0, scalar1=s1, scalar2=s2, op0=op0, op1=op1, **kw), r=r, w=w)

    def stt(self, out, in0, scalar, in1, op0, op1, r=(), w=()):
        return self.op("dve", lambda g: g.scalar_tensor_tensor(out=out, in0=in0, scalar=scalar, in1=in1, op0=op0, op1=op1), r=r, w=w)

    def cp(self, out, in_, r=(), w=(), e="dve"):
        if e == "act":
            return self.op("act", lambda g: g.copy(out=out, in_=in_), r=r, w=w)
        return self.op(e, lambda g: g.tensor_copy(out=out, in_=in_), r=r, w=w)

    def barrier(self):
        snap = dict(self.cnt)
        dsn = list(self.dcnt)
        for e in self.eng:
            for s, n in snap.items():
                if n > 0:
                    self._wait(e, s, n)
            for slot, n in enumerate(dsn):
                if n > 0:
                    self._wait(e, slot, n)


import math
from concourse.bass_utils import run_bass_kernel_spmd

DM = 1024
INW = 3096
FFH = 2816
PI = math.pi
LN_G = [math.log1p(-(2.0 ** (-5.0 - h))) for h in range(4)]
EPS = 1e-6
PSH = [("norm_mix", [1024]), ("w_in", [1024, 3096]), ("w_out", [1024, 1024]), ("gla_w_a2", [16, 128]),
       ("gla_b_a", [128]), ("gla_norm", [64]), ("s5_lam_re", [16, 64]), ("s5_lam_im", [16, 64]),
       ("s5_log_dt", [16]), ("s5_b_re", [16, 64, 16]), ("s5_b_im", [16, 64, 16]), ("s5_c_re", [16, 16, 64]),
       ("s5_c_im", [16, 16, 64]), ("s5_d", [256]), ("s5_w_glu", [256, 256]), ("s5_b_glu", [256]),
       ("ret_norm", [256]), ("gdn_conv", [4, 768]), ("gdn_a_log", [4]), ("gdn_dt_bias", [4]),
       ("gdn_norm", [64]), ("norm_ffn", [1024]), ("w_ffn_up", [1024, 5632]), ("w_ffn_down", [2816, 1024]),
       ("norm_ple", [1024]), ("w_ple_gate", [1024, 1024]), ("w_ple_proj", [256, 1024])]
FPIECES = [(0, 6), (6, 6), (12, 5), (17, 5)]


_K = [None]


class _Stop(Exception):
    pass


def build(L=2048, depth=2, dbg=False, stop=None):
    try:
        return _build(L, depth, dbg, stop)
    except _Stop as e:
        return e.args[0]


def _build(L, depth, dbg, stop):
    NT = L // 128
    nc = bass.Bass("TRN2", target_bir_lowering=False)
    k = KB(nc)
    D = {}

    def ck(name):
        k.tag = name
        if stop == name:
            k.finish("sp")
            print("STOP at", name, k.cnt, flush=True)
            raise _Stop(nc)

    def din(name, shape, dt=F32):
        D[name] = nc.dram_tensor(name, list(shape), dt, kind="ExternalInput").ap()
    din("x", [L, DM]); din("p", [depth, L, 256]); din("positions", [1, L], I32)
    for nm, sh in PSH:
        din(nm, [depth] + sh)
    din("norm_final", [1, 1024])
    out = nc.dram_tensor("out", [L, DM], F32, kind="ExternalOutput").ap()
    dbgo = nc.dram_tensor("dbg", [128, 8, L], F32, kind="ExternalOutput").ap() if dbg else None

    h = k.sb("h", [128, NT, DM])
    W = k.sb("W", [128, 8 * 3096 + 8 * 1024], BF16)
    win = W[:, 0:8 * 3096].rearrange("p (k n) -> p k n", k=8)
    wout = W[:, 8 * 3096:8 * 3096 + 8192].rearrange("p (k n) -> p k n", k=8)
    wup = W[:, 0:8 * 1536].rearrange("p (k n) -> p k n", k=8)
    wdn = W[:, 12288:12288 + 6 * 1024].rearrange("p (f n) -> p f n", f=6)
    wpg = W[:, 18432:18432 + 8192].rearrange("p (k n) -> p k n", k=8)
    wpp = W[:, 26624:26624 + 2048].rearrange("p (k n) -> p k n", k=2)
    actT = W[:, 28672:28672 + 3072].rearrange("p (f n) -> p f n", f=6)
    NF, NB = 8, 12
    A2 = k.sb("A2", [128, 8704])
    F = [A2[:, i * 512:(i + 1) * 512] for i in range(NF)]
    s5cos = A2[:, 4096:5120]; s5sin = A2[:, 5120:6144]; rtab = A2[:, 6144:7168]
    cdiag = A2[:, 7168:8704].bitcast(BF16)
    hn2T = A2[:, 0:8192].bitcast(BF16).rearrange("p (k n) -> p k n", k=8)
    Bt = [k.sb(f"B{i}", [128, 512], BF16) for i in range(NB)]
    FI = F[7].bitcast(I32)
    Rm = k.sb("Rm", [128, 128], BF16)
    ropeCt = k.sb("ropeCt", [128, 128], BF16); ropeSt = k.sb("ropeSt", [128, 128], BF16)
    ident_f = k.sb("ident_f", [128, 128]); ident_b = k.sb("ident_b", [128, 128], BF16)
    ones_f = k.sb("ones_f", [128, 128]); triL = k.sb("triL", [128, 128]); Ust = k.sb("Ust", [128, 128])
    negU = k.sb("negU", [128, 128]); blk1 = k.sb("blk1", [128, 128])
    ones_lo = k.sb("ones_lo", [128, 128]); ones_hi = k.sb("ones_hi", [128, 128])
    iota1 = k.sb("iota1", [128, 128])
    cms = k.sb("cms", [128, 128])
    dmask = k.sb("dmask", [128, 512]); xi = k.sb("xi", [128, 4]); zeta = k.sb("zeta", [128, 4])
    g128 = k.sb("g128", [128, 2]); mlohi = k.sb("mlohi", [128, 2]); EO = k.sb("EO", [128, 2])
    pcol = k.sb("pcol", [128, 1]); hm = k.sb("hm", [128, 4])
    Brem = k.sb("Brem", [128, 1024], BF16); Bimm = k.sb("Bimm", [128, 1024], BF16)
    Cpr = k.sb("Cpr", [128, 1024], BF16); Cpi = k.sb("Cpi", [128, 1024], BF16); Dfull = k.sb("Dfull", [128, 256], BF16)
    ropeC = nc.dram_tensor("ropeC_d", [128, L], BF16).ap(); ropeS = nc.dram_tensor("ropeS_d", [128, L], BF16).ap()
    stage = k.sb("stage", [8, 128])
    gcol = k.sb("gcol", [128, 8])
    small = k.sb("small", [128, 64])
    wa2 = k.sb("wa2", [16, 128]); nba = k.sb("nba", [128, 1]); gn_gla = k.sb("gn_gla", [128, 64])
    gn_gdn = k.sb("gn_gdn", [128, 64]); gn_ret = k.sb("gn_ret", [128, 256])
    alog_b = k.sb("alog_b", [128, 4]); dtb_b = k.sb("dtb_b", [128, 4]); nexpA = k.sb("nexpA", [128, 4])
    cw = k.sb("cw", [128, 24])
    wglu = k.sb("wglu", [128, 2, 256], BF16); bglu = k.sb("bglu", [128, 2]); dcol = k.sb("dcol", [128, 2])
    s5c = k.sb("s5c", [128, 96])
    gmask = k.sb("gmask", [128, 9 * 128], BF16)
    hn = k.sb("hn", [128, DM], BF16); hnT = k.sb("hnT", [128, 8, 128], BF16); mixT = k.sb("mixT", [128, 8, 128], BF16)
    alow = k.sb("alow", [16, 128])
    Sgla = k.sb("Sgla", [128, 64]); Sgla_b = k.sb("Sgla_b", [128, 64], BF16)
    Sret = k.sb("Sret", [128, 128]); Sret_b = k.sb("Sret_b", [128, 128], BF16)
    Sgdn = k.sb("Sgdn", [128, 128]); Sgdn_b = k.sb("Sgdn_b", [128, 128], BF16)
    xst = k.sb("xst", [128, 16])
    xc = k.sb("xc", [128, 6, 132], BF16)
    PS = [k.ps(f"PS{i}", [128, 512]) for i in range(6)]
    PBF = [k.ps(f"PB{i}", [128, 1024], BF16) for i in range(2)]
    cnt = {"ps": 0, "pb": 0}

    pinned = set()

    def bank(pin=False):
        while True:
            i = cnt["ps"] % 6; cnt["ps"] += 1
            if i not in pinned:
                break
        if pin:
            pinned.add(i)
        return PS[i], f"PS{i}"

    def unpin(key):
        pinned.discard(int(key[2:]))

    def bbank():
        i = cnt["pb"] % 2; cnt["pb"] += 1
        return PBF[i], f"PB{i}"

    def v3(ap, a):
        return ap.rearrange("p (a b) -> p a b", a=a)

    def bc_mid(ap2, a):
        return ap2.unsqueeze(1).broadcast_to([ap2.shape[0], a, ap2.shape[1]])

    def bc_last(ap2, n):
        return ap2.unsqueeze(2).broadcast_to([ap2.shape[0], ap2.shape[1], n])

    def colload(dst, dkey, src2d, n):
        k.dma("sp", stage[0:n, :], src2d, w=["stage"])
        b, bk = bank()
        k.tr(b[:, 0:n], stage[0:n, :], ident_f[0:n, 0:n], r=["stage", "ident_f"], w=[bk])
        k.cp(dst, b[:, 0:n], r=[bk], w=[dkey])

    def range_reduce(t, tkey, tf, fkey, ti, ikey):
        k.ts(tf, t, 1.0 / (2 * PI), r=[tkey], w=[fkey])
        k.cp(ti, tf, r=[fkey], w=[ikey])
        k.cp(tf, ti, r=[ikey], w=[fkey])
        k.stt(t, tf, -2.0 * PI, t, ALU.mult, ALU.add, r=[fkey, tkey], w=[tkey])
        k.ts(tf, t, PI, -2.0 * PI, op0=ALU.is_gt, op1=ALU.mult, r=[tkey], w=[fkey])
        k.tt(t, t, tf, ALU.add, r=[tkey, fkey], w=[tkey])
        k.ts(tf, t, -PI, 2.0 * PI, op0=ALU.is_lt, op1=ALU.mult, r=[tkey], w=[fkey])
        k.tt(t, t, tf, ALU.add, r=[tkey, fkey], w=[tkey])

    def rstd_from(dst, dkey, src, skey, scale, r_extra=()):
        k.act(dst, src, AF.Ln, scale=scale, bias=EPS, r=[skey] + list(r_extra), w=[dkey])
        k.act(dst, dst, AF.Exp, scale=-0.5, r=[dkey], w=[dkey])

    def norm_T(src_h, gkey_loaded, dstT, dkey, ntok_off=0):
        ss = small[:, 0:1]
        k.act(hn[:], src_h, AF.Square, accum_out=ss, r=["h"], w=["hn", "sm_ss"])
        rstd_from(small[:, 1:2], "sm_rs", ss, "sm_ss", 1.0 / DM)
        k.ts(hn[:], src_h, small[:, 1:2], r=["h", "sm_rs"], w=["hn"])
        pb, pk = bbank()
        for kc in range(8):
            k.tr(pb[:, kc * 128:(kc + 1) * 128], hn[:, kc * 128:(kc + 1) * 128], ident_b[:], r=["hn", "ident_b"], w=[pk])
        k.tt(dstT[:, :, ntok_off:ntok_off + 128], v3(pb[:, 0:1024], 8), bc_last(gcol[:, 0:8], 128), ALU.mult,
             r=[pk, gkey_loaded], w=[dkey])

    def head_norm_gate(o3, okeys, gn_ap, gnkey, sgate, sgkey, dst_bf, dkey, ft, ftkey, center=False):
        t0 = ft[:, 0:256]; t1 = ft[:, 256:512]
        if center:
            k.op("dve", lambda g: g.tensor_reduce(out=small[:, 8:12], in_=o3, axis=AX.X, op=ALU.add), r=okeys, w=["sm_m"])
            k.ts(small[:, 8:12], small[:, 8:12], 1.0 / 64, r=["sm_m"], w=["sm_m"])
            k.tt(v3(t0, 4), o3, bc_last(small[:, 8:12], 64), ALU.subtract, r=okeys + ["sm_m"], w=[ftkey])
            src3 = v3(t0, 4); skeys = [ftkey]
        else:
            src3 = o3; skeys = okeys
        k.act(v3(t1, 4), src3, AF.Square, r=skeys, w=[ftkey])
        k.op("dve", lambda g: g.tensor_reduce(out=small[:, 12:16], in_=v3(t1, 4), axis=AX.X, op=ALU.add), r=[ftkey], w=["sm_v"])
        rstd_from(small[:, 16:20], "sm_r4", small[:, 12:16], "sm_v", 1.0 / 64)
        k.tt(v3(t1, 4), src3, bc_last(small[:, 16:20], 64), ALU.mult, r=skeys + ["sm_r4"], w=[ftkey])
        k.tt(t1, t1, gn_ap, ALU.mult, r=[ftkey, gnkey], w=[ftkey])
        k.tt(dst_bf, t1, sgate, ALU.mult, r=[ftkey, sgkey], w=[dkey])

    def to_mixT(src_bf, skey, c0):
        pb, pk = bbank()
        for i in range(2):
            k.tr(pb[:, i * 128:(i + 1) * 128], src_bf[:, i * 128:(i + 1) * 128], ident_b[:], r=[skey, "ident_b"], w=[pk])
        k.cp(mixT[:, c0:c0 + 2, :], v3(pb[:, 0:256], 2), r=[pk], w=["mixT"], e="act")

    def proj_fm(b, bk, slot, c0, M=128):
        for kc in range(8):
            k.mm(b[0:M, slot * 128:(slot + 1) * 128], win[:, kc, c0:c0 + M], hnT[:, kc, :], start=(kc == 0), stop=(kc == 7),
                 r=["win", "hnT"], w=[bk])

    def proj_tm(b, bk, o0, c0, n):
        for kc in range(8):
            k.mm(b[:, o0:o0 + n], hnT[:, kc, :], win[:, kc, c0:c0 + n], start=(kc == 0), stop=(kc == 7),
                 r=["win", "hnT"], w=[bk])

    P = "pool"
    k.op(P, lambda e: e.memset(ones_f[:], 1.0), w=["ones_f"])
    k.op(P, lambda e: e.affine_select(out=ident_f[:], in_=ones_f[:], pattern=[[-1, 128]], compare_op=ALU.is_equal, fill=0.0, base=0, channel_multiplier=1), r=["ones_f"], w=["ident_f"])
    k.cp(ident_b[:], ident_f[:], r=["ident_f"], w=["ident_b"], e=P)
    k.op(P, lambda e: e.affine_select(out=triL[:], in_=ones_f[:], pattern=[[1, 128]], compare_op=ALU.is_ge, fill=0.0, base=0, channel_multiplier=-1), r=["ones_f"], w=["triL"])
    k.op(P, lambda e: e.affine_select(out=Ust[:], in_=ones_f[:], pattern=[[-1, 128]], compare_op=ALU.is_gt, fill=0.0, base=0, channel_multiplier=1), r=["ones_f"], w=["Ust"])
    k.op(P, lambda e: e.affine_select(out=negU[:], in_=ones_f[:], pattern=[[1, 128]], compare_op=ALU.is_gt, fill=0.0, base=0, channel_multiplier=-1), r=["ones_f"], w=["negU"])
    k.ts(negU[:], negU[:], -1.0, r=["negU"], w=["negU"], e=P)
    k.op(P, lambda e: e.memset(blk1[:], 0.0), w=["blk1"])
    k.op(P, lambda e: e.memset(blk1[0:64, 0:64], 1.0), w=["blk1"])
    k.op(P, lambda e: e.memset(blk1[64:128, 64:128], 1.0), w=["blk1"])
    k.op(P, lambda e: e.memset(ones_lo[:], 0.0), w=["ones_lo"])
    k.op(P, lambda e: e.memset(ones_lo[:, 0:64], 1.0), w=["ones_lo"])
    k.op(P, lambda e: e.memset(ones_hi[:], 0.0), w=["ones_hi"])
    k.op(P, lambda e: e.memset(ones_hi[:, 64:128], 1.0), w=["ones_hi"])
    k.op(P, lambda e: e.memset(mlohi[:], 0.0), w=["mlohi"])
    k.op(P, lambda e: e.memset(mlohi[0:64, 0:1], 1.0), w=["mlohi"])
    k.op(P, lambda e: e.memset(mlohi[64:128, 1:2], 1.0), w=["mlohi"])
    for i in range(2):
        k.op(P, lambda e: e.memset(g128[0:64, i:i + 1], math.exp(128 * LN_G[2 * i])), w=["g128"])
        k.op(P, lambda e: e.memset(g128[64:128, i:i + 1], math.exp(128 * LN_G[2 * i + 1])), w=["g128"])
    k.op(P, lambda e: e.iota(iota1[:], pattern=[[1, 128]], base=1, channel_multiplier=0, allow_small_or_imprecise_dtypes=True), w=["iota1"])
    k.op(P, lambda e: e.iota(cms[:], pattern=[[1, 128]], base=0, channel_multiplier=-1, allow_small_or_imprecise_dtypes=True), w=["cms"])
    k.op(P, lambda e: e.iota(pcol[:], pattern=[[0, 1]], base=0, channel_multiplier=1, allow_small_or_imprecise_dtypes=True), w=["pcol"])
    k.op("dve", lambda g: g.tensor_reduce(out=F[0][:, 0:8], in_=ident_f[:].rearrange("p (a b c) -> p a b c", a=4, b=2), axis=AX.X, op=ALU.add), r=["ident_f"], w=["F0"])
    k.op("dve", lambda g: g.tensor_reduce(out=EO[:], in_=F[0][:, 0:8].rearrange("p (a b) -> p b a", a=4), axis=AX.X, op=ALU.add), r=["F0"], w=["EO"])
    k.op("dve", lambda g: g.tensor_reduce(out=hm[:], in_=v3(ident_f[:], 4), axis=AX.X, op=ALU.add), r=["ident_f"], w=["hm"])
    rv4 = lambda ap: ap.rearrange("p (b h d) -> p b h d", b=2, h=2)
    k.op(P, lambda e: e.affine_select(out=F[0][:, 0:128], in_=ones_f[:], pattern=[[-1, 128]], compare_op=ALU.is_equal, fill=0.0, base=-32, channel_multiplier=1), r=["ones_f"], w=["F0"])
    k.op(P, lambda e: e.memset(rv4(F[0][:, 0:128])[:, :, 1, :], 0.0), w=["F0"])
    k.op(P, lambda e: e.affine_select(out=F[1][:, 0:128], in_=ones_f[:], pattern=[[-1, 128]], compare_op=ALU.is_equal, fill=0.0, base=32, channel_multiplier=1), r=["ones_f"], w=["F1"])
    k.op(P, lambda e: e.memset(rv4(F[1][:, 0:128])[:, :, 0, :], 0.0), w=["F1"])
    k.tt(Rm[:], F[1][:, 0:128], F[0][:, 0:128], ALU.subtract, r=["F0", "F1"], w=["Rm"], e=P)
    def bdmask(dst, m):
        nb_ = 128 // m
        k.op(P, lambda e: e.affine_select(out=dst, in_=ones_f[:], pattern=[[-m, nb_], [0, m]], compare_op=ALU.is_ge, fill=0.0, base=0, channel_multiplier=1), r=["ones_f"], w=["F2"])
        k.op(P, lambda e: e.affine_select(out=dst, in_=dst, pattern=[[m, nb_], [0, m]], compare_op=ALU.is_ge, fill=0.0, base=m - 1, channel_multiplier=-1), r=["F2"], w=["F2"])
    bds = {8: F[2][:, 0:128], 16: F[2][:, 128:256], 32: F[2][:, 256:384], 64: F[2][:, 384:512], 128: ones_f[:]}
    for m_ in (8, 16, 32, 64):
        bdmask(bds[m_], m_)
    k.cp(gmask[:, 0:128], bds[8], r=["F2"], w=["gmask"])
    for li_, m_ in enumerate((8, 16, 32, 64)):
        k.tt(F[3][:, 0:128], bds[2 * m_], bds[m_], ALU.subtract, r=["F2", "ones_f"], w=["F3"])
        k.tt(gmask[:, (1 + li_) * 128:(2 + li_) * 128], F[3][:, 0:128], Ust[:], ALU.mult, r=["F3", "Ust"], w=["gmask"])
        k.stt(gmask[:, (5 + li_) * 128:(6 + li_) * 128], F[3][:, 0:128], -1.0, negU[:], ALU.mult, ALU.mult, r=["F3", "negU"], w=["gmask"])
    for hh in range(4):
        k.act(dmask[:, hh * 128:(hh + 1) * 128], cms[:], AF.Exp, scale=LN_G[hh], r=["cms"], w=["dmask"])
        k.act(xi[:, hh:hh + 1], pcol[:], AF.Exp, scale=LN_G[hh], bias=LN_G[hh], r=["pcol"], w=["xi"])
        k.act(zeta[:, hh:hh + 1], pcol[:], AF.Exp, scale=-LN_G[hh], bias=127.0 * LN_G[hh], r=["pcol"], w=["zeta"])
    k.tt(v3(dmask[:], 4), v3(dmask[:], 4), bc_mid(triL[:], 4), ALU.mult, r=["dmask", "triL"], w=["dmask"])
    fidx = small[:, 32:33]; invf = small[:, 33:34]
    for b4 in range(4):
        k.op(P, lambda e: e.iota(fidx[32 * b4:32 * b4 + 32, :], pattern=[[0, 1]], base=0, channel_multiplier=1, allow_small_or_imprecise_dtypes=True), w=["fidx"])
    k.act(invf, fidx, AF.Exp, scale=-math.log(10000.0) / 31.0, r=["fidx"], w=["invf"])
    for c in range(0, L, 512):
        n = min(512, L - c)
        k.dma("sp", FI[:, 0:n], D["positions"][0:1, c:c + n].partition_broadcast(128), w=["F7"])
        k.cp(F[1][:, 0:n], FI[:, 0:n], r=["F7"], w=["F1"])
        k.ts(F[0][:, 0:n], F[1][:, 0:n], invf, r=["F1", "invf"], w=["F0"])
        k.ts(F[2][:, 0:n], F[0][:, 0:n], PI / 2, op0=ALU.add, r=["F0"], w=["F2"])
        range_reduce(F[0][:, 0:n], "F0", F[3][:, 0:n], "F3", FI[:, 0:n], "F7")
        k.act(Bt[0][:, 0:n], F[0][:, 0:n], AF.Sin, r=["F0"], w=["B0"])
        k.dma("sp", ropeS[:, c:c + n], Bt[0][:, 0:n], r=["B0"], w=["ropeS"])
        range_reduce(F[2][:, 0:n], "F2", F[3][:, 0:n], "F3", FI[:, 0:n], "F7")
        k.act(Bt[1][:, 0:n], F[2][:, 0:n], AF.Sin, r=["F2"], w=["B1"])
        k.dma("sp", ropeC[:, c:c + n], Bt[1][:, 0:n], r=["B1"], w=["ropeC"])
    for t in range(NT):
        k.dma("sp", h[:, t, :], D["x"][t * 128:(t + 1) * 128, :], w=["h"])

    ck("const")
    for l in range(depth):
        k.barrier()
        for kc in range(8):
            k.dma("pool", win[:, kc, 0:INW], D["w_in"][l, kc * 128:(kc + 1) * 128, :], w=["win"])
        for kc in range(8):
            k.dma("pool", wout[:, kc, :], D["w_out"][l, kc * 128:(kc + 1) * 128, :], w=["wout"])
        k.dma("pool", wglu[:], D["s5_w_glu"][l].rearrange("(kc q) n -> q kc n", q=128), w=["wglu"])
        ck("wload")
        colload(gcol[:, 0:8], "gcol", D["norm_mix"][l].rearrange("(a b) -> a b", b=128), 8)
        k.dma("sp", wa2[:], D["gla_w_a2"][l], w=["wa2"])
        colload(nba[:, 0:1], "nba", D["gla_b_a"][l].rearrange("(a b) -> a b", b=128), 1)
        k.ts(nba[:], nba[:], -1.0, r=["nba"], w=["nba"])
        k.dma("sp", gn_gla[:], D["gla_norm"][l:l + 1, :].partition_broadcast(128), w=["gn_gla"])
        k.dma("sp", gn_gdn[:], D["gdn_norm"][l:l + 1, :].partition_broadcast(128), w=["gn_gdn"])
        k.dma("sp", gn_ret[:], D["ret_norm"][l:l + 1, :].partition_broadcast(128), w=["gn_ret"])
        k.dma("sp", alog_b[:], D["gdn_a_log"][l:l + 1, :].partition_broadcast(128), w=["alog_b"])
        k.dma("sp", dtb_b[:], D["gdn_dt_bias"][l:l + 1, :].partition_broadcast(128), w=["dtb_b"])
        k.act(nexpA[:], alog_b[:], AF.Exp, r=["alog_b"], w=["nexpA"])
        k.ts(nexpA[:], nexpA[:], -1.0, r=["nexpA"], w=["nexpA"])
        for i in range(6):
            colload(cw[:, i * 4:(i + 1) * 4], "cw", D["gdn_conv"][l, :, i * 128:(i + 1) * 128], 4)
        for i in range(6):
            for j in range(4):
                k.ts(cdiag[:, (i * 4 + j) * 128:(i * 4 + j + 1) * 128], ident_f[:], cw[:, i * 4 + j:i * 4 + j + 1], r=["ident_f", "cw"], w=["cdiag"])
        colload(bglu[:, 0:2], "bglu", D["s5_b_glu"][l].rearrange("(a b) -> a b", b=128), 2)
        colload(dcol[:, 0:2], "dcol", D["s5_d"][l].rearrange("(a b) -> a b", b=128), 2)
        ck("params")
        s8 = F[4]
        k.dma("sp", s8[0:8, 0:128], D["s5_lam_re"][l].rearrange("(j g) p -> j (g p)", g=2), w=["F4"])
        k.dma("sp", s8[0:8, 128:256], D["s5_lam_im"][l].rearrange("(j g) p -> j (g p)", g=2), w=["F4"])
        k.dma("sp", s8[0:8, 256:258], D["s5_log_dt"][l].rearrange("(j g) -> j g", g=2), w=["F4"])
        k.act(s8[0:8, 256:258], s8[0:8, 256:258], AF.Exp, r=["F4"], w=["F4"])
        k.ts(s8[0:8, 0:128], s8[0:8, 0:128], -1e-4, op0=ALU.min, r=["F4"], w=["F4"])
        dt3 = s8[0:8, 256:258].unsqueeze(2).broadcast_to([8, 2, 64])
        k.tt(v3(s8[0:8, 260:388], 2), v3(s8[0:8, 0:128], 2), dt3, ALU.mult, r=["F4"], w=["F4"])
        k.tt(v3(F[5][0:8, 0:128], 2), v3(s8[0:8, 128:256], 2), dt3, ALU.mult, r=["F4"], w=["F5"])
        b, bk = bank()
        k.tr(b[:, 0:8], s8[0:8, 0:128], ident_f[0:8, 0:8], r=["F4", "ident_f"], w=[bk])
        k.tr(b[:, 8:16], s8[0:8, 128:256], ident_f[0:8, 0:8], r=["F4", "ident_f"], w=[bk])
        k.tr(b[:, 16:24], s8[0:8, 260:388], ident_f[0:8, 0:8], r=["F4", "ident_f"], w=[bk])
        k.tr(b[:, 24:32], F[5][0:8, 0:128], ident_f[0:8, 0:8], r=["F5", "ident_f"], w=[bk])
        k.cp(s5c[:, 0:32], b[:, 0:32], r=[bk], w=["s5c"])
        lr = s5c[:, 0:8]; li = s5c[:, 8:16]; lrd = s5c[:, 16:24]; th = s5c[:, 24:32]
        rr_ = s5c[:, 32:40]
        k.act(rr_, lrd, AF.Exp, r=["s5c"], w=["s5c"])
        k.cp(v3(rtab[:], 8), bc_last(rr_, 128), r=["s5c"], w=["rtab"])
        k.op("dve", lambda g: g.memset(v3(rtab[:], 8)[:, :, 0], 0.0), w=["rtab"])
        for half in range(2):
            sl = slice(half * 512, (half + 1) * 512)
            k.tt(v3(F[0][:], 4), bc_mid(iota1[:], 4), bc_last(th[:, half * 4:half * 4 + 4], 128), ALU.mult, r=["iota1", "s5c"], w=["F0"])
            k.ts(F[2][:], F[0][:], PI / 2, op0=ALU.add, r=["F0"], w=["F2"])
            range_reduce(F[0][:], "F0", F[3][:], "F3", FI[:], "F7")
            k.act(s5sin[:, sl], F[0][:], AF.Sin, r=["F0"], w=["s5sin"])
            range_reduce(F[2][:], "F2", F[3][:], "F3", FI[:], "F7")
            k.act(s5cos[:, sl], F[2][:], AF.Sin, r=["F2"], w=["s5cos"])
        k.cp(F[0][:, 0:8], th, r=["s5c"], w=["F0"])
        k.ts(F[2][:, 0:8], th, PI / 2, op0=ALU.add, r=["s5c"], w=["F2"])
        range_reduce(F[0][:, 0:8], "F0", F[3][:, 0:8], "F3", FI[:, 0:8], "F7")
        range_reduce(F[2][:, 0:8], "F2", F[3][:, 0:8], "F3", FI[:, 0:8], "F7")
        sth = s5c[:, 40:48]; cth = s5c[:, 48:56]
        k.act(sth, F[0][:, 0:8], AF.Sin, r=["F0"], w=["s5c"])
        k.act(cth, F[2][:, 0:8], AF.Sin, r=["F2"], w=["s5c"])
        am1 = s5c[:, 56:64]; ai = s5c[:, 64:72]; c1r = s5c[:, 72:80]; c1i = s5c[:, 80:88]; den = s5c[:, 88:96]
        k.tt(am1, rr_, cth, ALU.mult, r=["s5c"], w=["s5c"])
        k.ts(am1, am1, -1.0, op0=ALU.add, r=["s5c"], w=["s5c"])
        k.tt(ai, rr_, sth, ALU.mult, r=["s5c"], w=["s5c"])
        t8a = F[0][:, 16:24]; t8b = F[0][:, 24:32]
        k.tt(den, lr, lr, ALU.mult, r=["s5c"], w=["s5c"])
        k.tt(t8a, li, li, ALU.mult, r=["s5c"], w=["F0"])
        k.tt(den, den, t8a, ALU.add, r=["s5c", "F0"], w=["s5c"])
        k.op("dve", lambda g: g.reciprocal(out=den, in_=den), r=["s5c"], w=["s5c"])
        k.tt(t8a, am1, lr, ALU.mult, r=["s5c"], w=["F0"])
        k.tt(t8b, ai, li, ALU.mult, r=["s5c"], w=["F0"])
        k.tt(t8a, t8a, t8b, ALU.add, r=["F0"], w=["F0"])
        k.tt(c1r, t8a, den, ALU.mult, r=["F0", "s5c"], w=["s5c"])
        k.tt(t8a, ai, lr, ALU.mult, r=["s5c"], w=["F0"])
        k.tt(t8b, am1, li, ALU.mult, r=["s5c"], w=["F0"])
        k.tt(t8a, t8a, t8b, ALU.subtract, r=["F0"], w=["F0"])
        k.tt(c1i, t8a, den, ALU.mult, r=["F0", "s5c"], w=["s5c"])
        braw = F[5];
        k.dma("sp", v3(braw[:, 0:128], 8), D["s5_b_re"][l].rearrange("(j g) p h -> (g p) j h", g=2), w=["F5"])
        k.dma("sp", v3(braw[:, 128:256], 8), D["s5_b_im"][l].rearrange("(j g) p h -> (g p) j h", g=2), w=["F5"])
        bre3 = v3(braw[:, 0:128], 8); bim3 = v3(braw[:, 128:256], 8)
        tA = v3(F[6][:, 0:128], 8); tB = v3(F[6][:, 128:256], 8); bbr = v3(F[6][:, 256:384], 8); bbi = v3(F[6][:, 384:512], 8)
        k.tt(tA, bre3, bc_last(c1r, 16), ALU.mult, r=["F5", "s5c"], w=["F6"])
        k.tt(tB, bim3, bc_last(c1i, 16), ALU.mult, r=["F5", "s5c"], w=["F6"])
        k.tt(bbr, tA, tB, ALU.subtract, r=["F6", "F6"], w=["F6"])
        k.tt(tA, bim3, bc_last(c1r, 16), ALU.mult, r=["F5", "s5c"], w=["F6"])
        k.tt(tB, bre3, bc_last(c1i, 16), ALU.mult, r=["F5", "s5c"], w=["F6"])
        k.tt(bbi, tA, tB, ALU.add, r=["F6", "F6"], w=["F6"])
        Bre = Bt[2][:, 0:256]; Bim = Bt[2][:, 256:512]; CreT = Bt[3][:, 0:256]; nCimT = Bt[3][:, 256:512]
        for (src3, skey, dstB, dkey) in ((bbr, "F6", Bre, "B2"), (bbi, "F6", Bim, "B2")):
            X = Bt[0][:, 0:256].rearrange("p (j c) -> p j c", j=8)
            k.ts(X[:, :, 0:16], src3, mlohi[:, 0:1], r=[skey, "mlohi"], w=["B0"])
            k.ts(X[:, :, 16:32], src3, mlohi[:, 1:2], r=[skey, "mlohi"], w=["B0"])
            pb, pk = bbank()
            for T in range(2):
                k.tr(pb[:, T * 128:(T + 1) * 128], Bt[0][:, T * 128:(T + 1) * 128], ident_b[:], r=["B0", "ident_b"], w=[pk])
            k.cp(dstB[:], pb[:, 0:256], r=[pk], w=[dkey])
            dstM, dmk = (Brem, "Brem") if dstB is Bre else (Bimm, "Bimm")
            for T in range(2):
                k.tt(v3(dstM[:, T * 512:(T + 1) * 512], 4), bc_mid(dstB[:, T * 128:(T + 1) * 128], 4), bc_last(hm[:], 128), ALU.mult, r=[dkey, "hm"], w=[dmk])
        craw = F[5]
        k.dma("sp", v3(craw[:, 0:128], 2), D["s5_c_re"][l].rearrange("(t g) h p -> (g h) t p", t=2), w=["F5"])
        k.dma("sp", v3(craw[:, 128:256], 2), D["s5_c_im"][l].rearrange("(t g) h p -> (g h) t p", t=2), w=["F5"])
        for (c0, sgn, dstC, dkey) in ((0, 1.0, CreT, "B3"), (128, -1.0, nCimT, "B3")):
            Y = Bt[0][:, 0:256].rearrange("p (t c) -> p t c", t=2)
            cs3 = v3(craw[:, c0:c0 + 128], 2)
            k.ts(Y[:, :, 0:64], cs3, EO[:, 0:1], sgn, op0=ALU.mult, op1=ALU.mult, r=["F5", "EO"], w=["B0"])
            k.ts(Y[:, :, 64:128], cs3, EO[:, 1:2], sgn, op0=ALU.mult, op1=ALU.mult, r=["F5", "EO"], w=["B0"])
            pb, pk = bbank()
            for T in range(2):
                k.tr(pb[:, T * 128:(T + 1) * 128], Bt[0][:, T * 128:(T + 1) * 128], ident_b[:], r=["B0", "ident_b"], w=[pk])
            k.cp(dstC[:], pb[:, 0:256], r=[pk], w=[dkey])
            dstP, dpk = (Cpr, "Cpr") if dstC is CreT else (Cpi, "Cpi")
            k.op("dve", lambda g: g.memset(dstP[:], 0.0), w=[dpk])
            for T in range(2):
                for jj in range(4):
                    k.cp(dstP[:, T * 512 + jj * 128 + 32 * jj:T * 512 + jj * 128 + 32 * jj + 32], dstC[:, T * 128 + 32 * jj:T * 128 + 32 * jj + 32], r=[dkey], w=[dpk])
        for T in range(2):
            k.ts(Dfull[:, T * 128:(T + 1) * 128], ident_f[:], dcol[:, T:T + 1], r=["ident_f", "dcol"], w=["Dfull"])
        ck("s5setup")
        for (tn, key) in ((Sgla, "Sgla"), (Sret, "Sret"), (Sgdn, "Sgdn"), (xst, "xst")):
            k.op("dve", lambda g: g.memset(tn[:], 0.0), w=[key])
        for (tn, key) in ((Sgla_b, "Sgla_b"), (Sret_b, "Sret_b"), (Sgdn_b, "Sgdn_b"), (xc, "xc")):
            k.op("dve", lambda g: g.memset(tn[:], 0.0), w=[key])

        def gla_gen(t):
            k.tag = 'gla0'
            b, bk = bank(pin=True)
            proj_fm(b, bk, 0, 0); proj_fm(b, bk, 1, 128); proj_fm(b, bk, 2, 512, M=16)
            b2, b2k = bank(pin=True)
            proj_tm(b2, b2k, 0, 256, 256); proj_tm(b2, b2k, 256, 528, 256)
            yield
            k.tag = 'gla0'
            k.cp(alow[:], b[0:16, 256:384], r=[bk], w=["alow"])
            k.mm(b[:, 384:512], wa2[:], alow[:], r=["wa2", "alow"], w=[bk])
            e1 = F[0][:, 0:128]; sp_ = F[0][:, 128:256]; cs = F[0][:, 256:384]
            k.act(e1, b[:, 384:512], AF.Exp, scale=-1.0, bias=nba[:, 0:1], r=[bk, "nba"], w=["F0"])
            k.act(sp_, e1, AF.Ln, bias=1.0, r=["F0"], w=["F0"])
            k.op("dve", lambda g: g.tensor_tensor_scan(out=cs, data0=ones_f[:], data1=sp_, initial=0.0, op0=ALU.mult, op1=ALU.add), r=["ones_f", "F0"], w=["F0"])
            eb = F[1][:, 0:128]; enb = F[1][:, 128:256]; eke = F[1][:, 256:384]
            nbl = small[:, 2:3]; ebl = small[:, 3:4]
            k.ts(nbl, cs[:, 127:128], -1.0 / 16, r=["F0"], w=["sm_nbl"])
            k.act(eb, cs, AF.Exp, scale=-1.0 / 16, r=["F0"], w=["F1"])
            k.act(enb, cs, AF.Exp, scale=1.0 / 16, r=["F0"], w=["F1"])
            k.act(eke, cs, AF.Exp, scale=1.0 / 16, bias=nbl, r=["F0", "sm_nbl"], w=["F1"])
            k.act(ebl, nbl, AF.Exp, r=["sm_nbl"], w=["sm_ebl"])
            qtT = Bt[0][:, 0:128]; ktT = Bt[0][:, 128:256]; keT = Bt[0][:, 256:384]
            k.stt(qtT, b[:, 0:128], 32.0 ** -0.5, eb, ALU.mult, ALU.mult, r=[bk, "F1"], w=["B0"])
            k.tt(ktT, b[:, 128:256], enb, ALU.mult, r=[bk, "F1"], w=["B0"])
            k.tt(keT, b[:, 128:256], eke, ALU.mult, r=[bk, "F1"], w=["B0"])
            pb, pk = bbank()
            k.tr(pb[:, 0:128], keT, ident_b[:], r=["B0", "ident_b"], w=[pk])
            kend = Bt[0][:, 384:512]
            k.cp(kend, pb[:, 0:128], r=[pk], w=["B0"], e="act")
            v_bf = Bt[1][:, 0:256]; sr = F[2][:, 0:256]
            k.cp(v_bf, b2[:, 0:256], r=[b2k], w=["B1"], e="act")
            k.act(sr, b2[:, 256:512], AF.Silu, r=[b2k], w=["F2"])
            qtTm = Bt[3]; ktTm = Bt[4]
            k.tt(v3(qtTm[:], 4), bc_mid(qtT, 4), bc_last(hm[:], 128), ALU.mult, r=["B0", "hm"], w=["B3"])
            k.tt(v3(ktTm[:], 4), bc_mid(ktT, 4), bc_last(hm[:], 128), ALU.mult, r=["B0", "hm"], w=["B4"])
            b3, b3k = bank()
            for hh in range(4):
                k.mm(b3[:, hh * 128:(hh + 1) * 128], ktTm[:, hh * 128:(hh + 1) * 128], qtT, r=["B0", "B4"], w=[b3k])
            attn = Bt[2]
            k.tt(v3(attn[:], 4), v3(b3[:], 4), bc_mid(triL[:], 4), ALU.mult, r=[b3k, "triL"], w=["B2"])
            b4, b4k = bank(pin=True)
            for hh in range(4):
                ps_ = slice(32 * hh, 32 * hh + 32); vs = slice(64 * hh, 64 * hh + 64)
                k.mm(b4[:, vs], attn[:, hh * 128:(hh + 1) * 128], v_bf[:, vs], start=True, stop=False, r=["B2", "B1"], w=[b4k])
                k.mm(b4[:, vs], qtTm[:, hh * 128:(hh + 1) * 128], Sgla_b[:, :], start=False, stop=True, r=["B3", "Sgla_b"], w=[b4k])
            unpin(bk); unpin(b2k)
            yield
            k.tag = 'gla0'
            mixa = Bt[1][:, 256:512]
            head_norm_gate(v3(b4[:, 0:256], 4), [b4k], bc_mid(gn_gla[:], 4).rearrange("p a b -> p (a b)") if False else gn_gla[:].unsqueeze(1).broadcast_to([128, 4, 64]),
                           "gn_gla", sr, "F2", mixa, "B1", F[3], "F3", center=False) if False else None
            t1 = F[3][:, 256:512]
            k.act(v3(t1, 4), v3(b4[:, 0:256], 4), AF.Square, r=[b4k], w=["F3"])
            k.op("dve", lambda g: g.tensor_reduce(out=small[:, 12:16], in_=v3(t1, 4), axis=AX.X, op=ALU.add), r=["F3"], w=["sm_v"])
            rstd_from(small[:, 16:20], "sm_r4", small[:, 12:16], "sm_v", 1.0 / 64)
            k.tt(v3(t1, 4), v3(b4[:, 0:256], 4), bc_last(small[:, 16:20], 64), ALU.mult, r=[b4k, "sm_r4"], w=["F3"])
            k.tt(v3(t1, 4), v3(t1, 4), bc_mid(gn_gla[:], 4), ALU.mult, r=["F3", "gn_gla"], w=["F3"])
            k.tt(mixa, t1, sr, ALU.mult, r=["F3", "F2"], w=["B1"])
            to_mixT(mixa, "B1", 0)
            b5, b5k = bank()
            k.mm(b5[:, 0:256], kend, v_bf, r=["B0", "B1"], w=[b5k])
            k.tt(v3(F[1][:, 0:256], 4), v3(b5[:, 0:256], 4), bc_last(hm[:], 64), ALU.mult, r=[b5k, "hm"], w=["F1"])
            k.op("dve", lambda g: g.tensor_reduce(out=F[1][:, 256:320], in_=F[1][:, 0:256].rearrange("p (h v) -> p v h", h=4), axis=AX.X, op=ALU.add), r=["F1"], w=["F1"])
            k.stt(Sgla[:], Sgla[:], ebl, F[1][:, 256:320], ALU.mult, ALU.add, r=["Sgla", "sm_ebl", "F1"], w=["Sgla"])
            k.cp(Sgla_b[:], Sgla[:], r=["Sgla"], w=["Sgla_b"], e="act")

            unpin(b4k)

        def s5_gen(t):
            ck("gla")
            bu, buk = bank()
            proj_fm(bu, buk, 0, 784); proj_fm(bu, buk, 1, 912)
            uT = Bt[3][:, 0:256]
            k.cp(uT, bu[:, 0:256], r=[buk], w=["B3"], e="act")
            yield
            k.tag = 'gla'
            bY, bYk = bank(pin=True)
            for half in range(2):
                bR, bRk = bank(); bI, bIk = bank()
                T = half
                for jj in range(4):
                    k.mm(bR[:, jj * 128:(jj + 1) * 128], Brem[:, T * 512 + jj * 128:T * 512 + (jj + 1) * 128], uT[:, T * 128:(T + 1) * 128], r=["Brem", "B3"], w=[bRk])
                    k.mm(bI[:, jj * 128:(jj + 1) * 128], Bimm[:, T * 512 + jj * 128:T * 512 + (jj + 1) * 128], uT[:, T * 128:(T + 1) * 128], r=["Bimm", "B3"], w=[bIk])
                csl = s5cos[:, half * 512:(half + 1) * 512]; ssl = s5sin[:, half * 512:(half + 1) * 512]
                Vre = F[4]; Vim = F[5]
                (t1_, k1), (t2_, k2) = ((F[6], "F6"), (F[7], "F7")) if half == 0 else ((F[0], "F0"), (F[1], "F1"))
                k.tt(t1_[:], bR[:], csl, ALU.mult, r=[bRk, "s5cos"], w=[k1])
                k.tt(t2_[:], bI[:], ssl, ALU.mult, r=[bIk, "s5sin"], w=[k2])
                k.tt(Vre[:], t1_[:], t2_[:], ALU.add, r=[k1, k2], w=["F4"])
                k.tt(t1_[:], bI[:], csl, ALU.mult, r=[bIk, "s5cos"], w=[k1])
                k.tt(t2_[:], bR[:], ssl, ALU.mult, r=[bRk, "s5sin"], w=[k2])
                k.tt(Vim[:], t1_[:], t2_[:], ALU.subtract, r=[k1, k2], w=["F5"])
                hs = slice(half * 4, half * 4 + 4)
                k.tt(small[:, 56:60], xst[:, hs], s5c[:, 32 + half * 4:36 + half * 4], ALU.mult, r=["xst", "s5c"], w=["sm_xi"])
                k.tt(small[:, 60:64], xst[:, 8 + half * 4:12 + half * 4], s5c[:, 32 + half * 4:36 + half * 4], ALU.mult, r=["xst", "s5c"], w=["sm_xi"])
                k.tt(v3(Vre[:], 4)[:, :, 0], v3(Vre[:], 4)[:, :, 0], small[:, 56:60], ALU.add, r=["F4", "sm_xi"], w=["F4"])
                k.tt(v3(Vim[:], 4)[:, :, 0], v3(Vim[:], 4)[:, :, 0], small[:, 60:64], ALU.add, r=["F5", "sm_xi"], w=["F5"])
                Wre, Wim = t1_, t2_
                rt = rtab[:, half * 512:(half + 1) * 512]
                k.op("dve", lambda g: g.tensor_tensor_scan(out=Wre[:], data0=rt, data1=Vre[:], initial=0.0, op0=ALU.mult, op1=ALU.add), r=["rtab", "F4"], w=[k1])
                k.op("dve", lambda g: g.tensor_tensor_scan(out=Wim[:], data0=rt, data1=Vim[:], initial=0.0, op0=ALU.mult, op1=ALU.add), r=["rtab", "F5"], w=[k2])
                t3_ = F[2]; t4_ = F[3]; Xrb = Bt[4]; Xib = Bt[5]
                PE_ = "dve"
                k.tt(t3_[:], Wre[:], csl, ALU.mult, r=[k1, "s5cos"], w=["F2"], e=PE_)
                k.tt(t4_[:], Wim[:], ssl, ALU.mult, r=[k2, "s5sin"], w=["F3"], e=PE_)
                k.tt(Xrb[:], t3_[:], t4_[:], ALU.subtract, r=["F2", "F3"], w=["B4"], e=PE_)
                k.cp(xst[:, hs], v3(t3_[:], 4)[:, :, 127], r=["F2"], w=["xst"], e=PE_)
                k.tt(xst[:, hs], xst[:, hs], v3(t4_[:], 4)[:, :, 127], ALU.subtract, r=["xst", "F3"], w=["xst"], e=PE_)
                k.tt(t3_[:], Wre[:], ssl, ALU.mult, r=[k1, "s5sin"], w=["F2"], e=PE_)
                k.tt(t4_[:], Wim[:], csl, ALU.mult, r=[k2, "s5cos"], w=["F3"], e=PE_)
                k.tt(Xib[:], t3_[:], t4_[:], ALU.add, r=["F2", "F3"], w=["B5"], e=PE_)
                k.cp(xst[:, 8 + half * 4:12 + half * 4], v3(t3_[:], 4)[:, :, 127], r=["F2"], w=["xst"], e=PE_)
                k.tt(xst[:, 8 + half * 4:12 + half * 4], xst[:, 8 + half * 4:12 + half * 4], v3(t4_[:], 4)[:, :, 127], ALU.add, r=["xst", "F3"], w=["xst"], e=PE_)
                o_ = bY[:, T * 128:(T + 1) * 128]
                for jj in range(4):
                    c = slice(jj * 128, (jj + 1) * 128); pc = slice(T * 512 + jj * 128, T * 512 + (jj + 1) * 128)
                    k.mm(o_, Cpr[:, pc], Xrb[:, c], start=(jj == 0), stop=False, r=["Cpr", "B4"], w=[bYk])
                    k.mm(o_, Cpi[:, pc], Xib[:, c], start=False, stop=False, r=["Cpi", "B5"], w=[bYk])
                k.mm(o_, Dfull[:, T * 128:(T + 1) * 128], uT[:, T * 128:(T + 1) * 128], start=False, stop=True, r=["Dfull", "B3"], w=[bYk])
            yield
            k.tag = 'gla'
            y = bY[:, 0:256]; y2 = F[2][:, 0:256]; ge = F[2][:, 256:512]; geb = Bt[3][:, 256:512]
            k.act(y2, y, AF.Square, r=[bYk], w=["F2"])
            k.ts(y2, y2, 0.044715, 1.0, op0=ALU.mult, op1=ALU.add, r=["F2"], w=["F2"])
            k.tt(y2, y2, y, ALU.mult, r=["F2", bYk], w=["F2"])
            k.act(y2, y2, AF.Sigmoid, scale=1.5957691216057308, r=["F2"], w=["F2"])
            k.tt(ge, y2, y, ALU.mult, r=["F2", bYk], w=["F2"])
            k.cp(geb, ge, r=["F2"], w=["B3"], e="act")
            bz, bzk = bank()
            for oi in range(2):
                for kc in range(2):
                    k.mm(bz[:, oi * 128:(oi + 1) * 128], wglu[:, kc, oi * 128:(oi + 1) * 128], geb[:, kc * 128:(kc + 1) * 128],
                         start=(kc == 0), stop=(kc == 1), r=["wglu", "B3"], w=[bzk])
            sgl = F[3][:, 0:256]
            for oi in range(2):
                k.act(sgl[:, oi * 128:(oi + 1) * 128], bz[:, oi * 128:(oi + 1) * 128], AF.Sigmoid, bias=bglu[:, oi:oi + 1], r=[bzk, "bglu"], w=["F3"])
            k.tt(mixT[:, 2:4, :], v3(ge, 2), v3(sgl, 2), ALU.mult, r=["F2", "F3"], w=["mixT"])

            unpin(bYk)

        def ret_gen(t):
            tsl = slice(t * 128, (t + 1) * 128)
            ck("s5")
            bq, bqk = bank(pin=True); bkk_, bkkk = bank(pin=True)
            for i in range(2):
                proj_fm(bq, bqk, i, 1040 + 128 * i)
                proj_fm(bkk_, bkkk, i, 1296 + 128 * i)
            xq = Bt[5][:, 0:256]; xk = Bt[5][:, 256:512]
            k.cp(xq, bq[:, 0:256], r=[bqk], w=["B5"], e="act")
            k.cp(xk, bkk_[:, 0:256], r=[bkkk], w=["B5"], e="act")
            for i in range(2):
                k.mm(bq[:, 256 + i * 128:256 + (i + 1) * 128], Rm[:], xq[:, i * 128:(i + 1) * 128], r=["Rm", "B5"], w=[bqk])
                k.mm(bkk_[:, 256 + i * 128:256 + (i + 1) * 128], Rm[:], xk[:, i * 128:(i + 1) * 128], r=["Rm", "B5"], w=[bkkk])
            yield
            ck("r0")
            cosb = bc_mid(ropeCt[:], 2); sinb = bc_mid(ropeSt[:], 2)
            qT = Bt[0][:, 0:256]; kT = Bt[0][:, 256:512]
            ta = F[0][:, 0:256]; tb = F[0][:, 256:512]
            k.tt(v3(ta, 2), v3(bq[:, 0:256], 2), cosb, ALU.mult, r=[bqk, "ropeCt"], w=["F0"])
            k.tt(v3(tb, 2), v3(bq[:, 256:512], 2), sinb, ALU.mult, r=[bqk, "ropeSt"], w=["F0"])
            k.tt(qT, ta, tb, ALU.add, r=["F0", "F0"], w=["B0"])
            k.stt(v3(ta, 2), v3(bkk_[:, 0:256], 2), 0.125, cosb, ALU.mult, ALU.mult, r=[bkkk, "ropeCt"], w=["F0"])
            k.stt(v3(tb, 2), v3(bkk_[:, 256:512], 2), 0.125, sinb, ALU.mult, ALU.mult, r=[bkkk, "ropeSt"], w=["F0"])
            k.tt(kT, ta, tb, ALU.add, r=["F0", "F0"], w=["B0"])
            unpin(bqk); unpin(bkkk)
            pb, pk = bbank()
            for i in range(2):
                k.tr(pb[:, i * 128:(i + 1) * 128], kT[:, i * 128:(i + 1) * 128], ident_b[:], r=["B0", "ident_b"], w=[pk])
            ck("r1")
            Kz = Bt[1][:, 0:256]
            k.tt(v3(Kz, 4), v3(pb[:, 0:256], 4), bc_last(zeta[:], 64), ALU.mult, r=[pk, "zeta"], w=["B1"])
            b2, b2k = bank()
            proj_tm(b2, b2k, 0, 1552, 512)
            v_bf = Bt[1][:, 256:512]; sg = F[1][:, 0:256]
            k.cp(v_bf, b2[:, 0:256], r=[b2k], w=["B1"], e="act")
            k.act(sg, b2[:, 256:512], AF.Silu, r=[b2k], w=["F1"])
            qTm = Bt[6]; kTm = Bt[7]
            for hh in range(4):
                i = hh // 2
                k.ts(qTm[:, hh * 128:(hh + 1) * 128], qT[:, i * 128:(i + 1) * 128], mlohi[:, hh % 2:hh % 2 + 1], r=["B0", "mlohi"], w=["B6"])
                k.ts(kTm[:, hh * 128:(hh + 1) * 128], kT[:, i * 128:(i + 1) * 128], mlohi[:, hh % 2:hh % 2 + 1], r=["B0", "mlohi"], w=["B7"])
            b3, b3k = bank()
            for hh in range(4):
                i = hh // 2
                k.mm(b3[:, hh * 128:(hh + 1) * 128], kTm[:, hh * 128:(hh + 1) * 128], qT[:, i * 128:(i + 1) * 128], r=["B0", "B7"], w=[b3k])
            ck("r2")
            attn = Bt[2]
            k.tt(attn[:], b3[:], dmask[:], ALU.mult, r=[b3k, "dmask"], w=["B2"])
            b4, b4k = bank(pin=True)
            for hh in range(4):
                i = hh // 2; ps_ = slice(64 * (hh % 2), 64 * (hh % 2) + 64); vs = slice(64 * hh, 64 * hh + 64)
                k.mm(b4[:, vs], attn[:, hh * 128:(hh + 1) * 128], v_bf[:, vs], r=["B2", "B1"], w=[b4k])
                k.mm(b4[:, 256 + 64 * hh:256 + 64 * hh + 64], qTm[:, hh * 128:(hh + 1) * 128], Sret_b[:, i * 64:(i + 1) * 64], r=["B6", "Sret_b"], w=[b4k])
            yield
            k.tag = 'r3'
            o_ = F[3][:, 0:256]; t1 = F[3][:, 256:512]
            k.tt(v3(o_, 4), v3(b4[:, 256:512], 4), bc_last(xi[:], 64), ALU.mult, r=[b4k, "xi"], w=["F3"])
            k.tt(o_, o_, b4[:, 0:256], ALU.add, r=["F3", b4k], w=["F3"])
            k.op("dve", lambda g: g.tensor_reduce(out=small[:, 8:12], in_=v3(o_, 4), axis=AX.X, op=ALU.add), r=["F3"], w=["sm_m"])
            k.ts(small[:, 8:12], small[:, 8:12], 1.0 / 64, r=["sm_m"], w=["sm_m"])
            k.tt(v3(o_, 4), v3(o_, 4), bc_last(small[:, 8:12], 64), ALU.subtract, r=["F3", "sm_m"], w=["F3"])
            k.act(t1, o_, AF.Square, r=["F3"], w=["F3"])
            k.op("dve", lambda g: g.tensor_reduce(out=small[:, 12:16], in_=v3(t1, 4), axis=AX.X, op=ALU.add), r=["F3"], w=["sm_v"])
            rstd_from(small[:, 16:20], "sm_r4", small[:, 12:16], "sm_v", 1.0 / 64)
            k.tt(v3(t1, 4), v3(o_, 4), bc_last(small[:, 16:20], 64), ALU.mult, r=["F3", "sm_r4"], w=["F3"])
            k.tt(t1, t1, gn_ret[:], ALU.mult, r=["F3", "gn_ret"], w=["F3"])
            ck("r3")
            mixc = Bt[3][:, 0:256]
            k.tt(mixc, t1, sg, ALU.mult, r=["F3", "F1"], w=["B3"])
            to_mixT(mixc, "B3", 4)
            b5, b5k = bank()
            for i in range(2):
                k.mm(b5[:, i * 128:(i + 1) * 128], Kz[:, i * 128:(i + 1) * 128], v_bf[:, i * 128:(i + 1) * 128], r=["B1", "B1"], w=[b5k])
            for i in range(2):
                dsl = F[0][:, i * 64:(i + 1) * 64]
                k.ts(dsl, b5[:, i * 128:i * 128 + 64], mlohi[:, 0:1], r=[b5k, "mlohi"], w=["F0"])
                k.stt(dsl, b5[:, i * 128 + 64:(i + 1) * 128], mlohi[:, 1:2], dsl, ALU.mult, ALU.add, r=[b5k, "mlohi", "F0"], w=["F0"])
                k.stt(Sret[:, i * 64:(i + 1) * 64], Sret[:, i * 64:(i + 1) * 64], g128[:, i:i + 1], dsl, ALU.mult, ALU.add,
                      r=["Sret", "g128", "F0"], w=["Sret"])
            k.cp(Sret_b[:], Sret[:], r=["Sret"], w=["Sret_b"], e="act")

            unpin(b4k)

        def gdn_gen(t):
            ck("ret")
            bA, bAk = bank(); bB, bBk = bank()
            for i in range(4):
                proj_fm(bA, bAk, i, 2064 + 128 * i)
            for i in range(2):
                proj_fm(bB, bBk, i, 2064 + 128 * (4 + i))
            k.cp(xc[:, 0:4, 3:131], v3(bA[:], 4), r=[bAk], w=["xc"], e="act")
            k.cp(xc[:, 4:6, 3:131], v3(bB[:, 0:256], 2), r=[bBk], w=["xc"], e="act")
            yield
            k.tag = 'gdn.conv'
            bC, bCk = bank(); bD, bDk = bank()
            for i in range(6):
                ob = bC[:, i * 128:(i + 1) * 128] if i < 4 else bD[:, (i - 4) * 128:(i - 3) * 128]
                obk = bCk if i < 4 else bDk
                for j in range(4):
                    k.mm(ob, cdiag[:, (i * 4 + j) * 128:(i * 4 + j + 1) * 128], xc[:, i, j:j + 128], start=(j == 0), stop=(j == 3), r=["cdiag", "xc"], w=[obk])
            k.cp(xc[:, :, 0:3], xc[:, :, 128:131], r=["xc"], w=["xc"])
            qk = F[0]
            vTb = Bt[0][:, 0:256]
            k.act(qk[:], bC[:], AF.Silu, r=[bCk], w=["F0"])
            k.act(vTb, bD[:, 0:256], AF.Silu, r=[bDk], w=["B0"])
            sq = F[1]
            k.act(sq[:], qk[:], AF.Square, r=["F0"], w=["F1"])
            bn, bnk = bank()
            for i in range(4):
                k.mm(bn[:, i * 128:(i + 1) * 128], blk1[:], sq[:, i * 128:(i + 1) * 128], r=["blk1", "F1"], w=[bnk])
            rinv = F[1]
            k.act(rinv[:], bn[:], AF.Ln, bias=EPS, r=[bnk], w=["F1"])
            k.act(rinv[:], rinv[:], AF.Exp, scale=-0.5, r=["F1"], w=["F1"])
            qkT = Bt[1]
            k.stt(qkT[:, 0:256], qk[:, 0:256], 0.125, rinv[:, 0:256], ALU.mult, ALU.mult, r=["F0", "F1"], w=["B1"])
            k.tt(qkT[:, 256:512], qk[:, 256:512], rinv[:, 256:512], ALU.mult, r=["F0", "F1"], w=["B1"])
            pb, pk = bbank()
            for i in range(2):
                k.tr(pb[:, i * 128:(i + 1) * 128], qkT[:, 256 + i * 128:256 + (i + 1) * 128], ident_b[:], r=["B1", "ident_b"], w=[pk])
                k.tr(pb[:, 256 + i * 128:256 + (i + 1) * 128], vTb[:, i * 128:(i + 1) * 128], ident_b[:], r=["B0", "ident_b"], w=[pk])
            k.tag = 'gdn.gates'
            b2, b2k = bank()
            proj_tm(b2, b2k, 0, 2832, 264)
            beta = small[:, 20:24]; gg = small[:, 24:28]; lnb = small[:, 28:32]
            sg = F[2][:, 0:256]
            k.act(beta, b2[:, 0:4], AF.Sigmoid, r=[b2k], w=["sm_beta"])
            k.act(sg, b2[:, 8:264], AF.Silu, r=[b2k], w=["F2"])
            k.tt(gg, b2[:, 4:8], dtb_b[:], ALU.add, r=[b2k, "dtb_b"], w=["sm_g"])
            k.act(gg, gg, AF.Exp, r=["sm_g"], w=["sm_g"])
            k.act(gg, gg, AF.Ln, bias=1.0, r=["sm_g"], w=["sm_g"])
            k.tt(gg, gg, nexpA[:], ALU.mult, r=["sm_g", "nexpA"], w=["sm_g"])
            k.act(lnb, beta, AF.Ln, r=["sm_beta"], w=["sm_lnb"])
            bg, bgk = bank()
            k.mm(bg[:, 0:4], triL[:], gg, r=["triL", "sm_g"], w=[bgk])
            k.mm(bg[:, 4:8], ones_f[:], gg, r=["ones_f", "sm_g"], w=[bgk])
            gg2 = gg.rearrange("p (i two) -> p two i", two=2)
            k.mm(bg[:, 8:10], ones_lo[:], gg2[:, 0, :], start=True, stop=False, r=["ones_lo", "sm_g"], w=[bgk])
            k.mm(bg[:, 8:10], ones_hi[:], gg2[:, 1, :], start=False, stop=True, r=["ones_hi", "sm_g"], w=[bgk])
            gcum = small[:, 36:40]; eg = small[:, 40:44]; eke4 = small[:, 44:48]; dlS = small[:, 48:50]; bexp = small[:, 52:56]
            k.cp(gcum, bg[:, 0:4], r=[bgk], w=["sm_gc"])
            k.act(eg, gcum, AF.Exp, r=["sm_gc"], w=["sm_eg"])
            k.tt(eke4, bg[:, 4:8], gcum, ALU.subtract, r=[bgk, "sm_gc"], w=["sm_eke"])
            k.act(eke4, eke4, AF.Exp, r=["sm_eke"], w=["sm_eke"])
            k.act(dlS, bg[:, 8:10], AF.Exp, r=[bgk], w=["sm_dls"])
            k.tt(bexp, beta, eg, ALU.mult, r=["sm_beta", "sm_eg"], w=["sm_bexp"])
            Ru = Bt[3][:, 0:256]; Rw = Bt[3][:, 256:512]; Kend = Bt[4][:, 0:256]
            k.tt(v3(Ru, 4), v3(pb[:, 256:512], 4), bc_last(beta, 64), ALU.mult, r=[pk, "sm_beta"], w=["B3"])
            k.tt(v3(Rw, 4), v3(pb[:, 0:256], 4), bc_last(bexp, 64), ALU.mult, r=[pk, "sm_bexp"], w=["B3"])
            k.tt(v3(Kend, 4), v3(pb[:, 0:256], 4), bc_last(eke4, 64), ALU.mult, r=[pk, "sm_eke"], w=["B4"])
            k.tag = 'gdn.decay'
            rhsA = F[3]; rhsB = F[4]
            k.tt(v3(rhsA[:], 4), bc_mid(triL[:], 4), bc_last(gg, 128), ALU.mult, r=["triL", "sm_g"], w=["F3"])
            k.tt(v3(rhsB[:], 4), bc_mid(ident_f[:], 4), bc_last(lnb, 128), ALU.mult, r=["ident_f", "sm_lnb"], w=["F4"])
            k.tt(rhsB[:], rhsB[:], rhsA[:], ALU.add, r=["F3", "F4"], w=["F4"])
            bD1, bD1k = bank(); bD2, bD2k = bank()
            for hh in range(4):
                c = slice(hh * 128, (hh + 1) * 128)
                k.mm(bD1[:, c], Ust[:], rhsA[:, c], r=["Ust", "F3"], w=[bD1k])
                k.mm(bD2[:, c], Ust[:], rhsB[:, c], r=["Ust", "F4"], w=[bD2k])
            E1 = F[5]; E2 = F[6]
            k.act(E1[:], bD1[:], AF.Exp, r=[bD1k], w=["F5"])
            k.act(E2[:], bD2[:], AF.Exp, r=[bD2k], w=["F6"])
            bK, bKk = bank(); bQ, bQk = bank()
            kTm = Bt[7]
            for hh in range(4):
                i = hh // 2
                k.ts(kTm[:, hh * 128:(hh + 1) * 128], qkT[:, 256 + i * 128:256 + (i + 1) * 128], mlohi[:, hh % 2:hh % 2 + 1], r=["B1", "mlohi"], w=["B7"])
            for hh in range(4):
                i = hh // 2; c = slice(hh * 128, (hh + 1) * 128)
                k.mm(bK[:, c], kTm[:, c], qkT[:, 256 + i * 128:256 + (i + 1) * 128], r=["B1", "B7"], w=[bKk])
                k.mm(bQ[:, c], kTm[:, c], qkT[:, i * 128:(i + 1) * 128], r=["B1", "B7"], w=[bQk])
            aqk = Bt[2]; Mb = Bt[5]; Nb = Bt[10]
            k.tt(E1[:], E1[:], bQ[:], ALU.mult, r=["F5", bQk], w=["F5"])
            k.tt(v3(aqk[:], 4), v3(E1[:], 4), bc_mid(triL[:], 4), ALU.mult, r=["F5", "triL"], w=["B2"])
            k.tt(E2[:], E2[:], bK[:], ALU.mult, r=["F6", bKk], w=["F6"])
            k.tt(v3(Mb[:], 4), v3(E2[:], 4), bc_mid(negU[:], 4), ALU.mult, r=["F6", "negU"], w=["B5"])
            pb2, pk2 = bbank()
            for hh in range(4):
                c = slice(hh * 128, (hh + 1) * 128)
                k.tr(pb2[:, c], Mb[:, c], ident_b[:], r=["B5", "ident_b"], w=[pk2])
            k.cp(Nb[:], pb2[:, 0:512], r=[pk2], w=["B10"], e="act")
            k.tag = 'gdn.solve'
            Ttb_t = Bt[6]; Qb = Bt[0]; Db = Bt[9]; Pb2 = Bt[11]
            M0 = Bt[7]; N0 = Bt[8]
            k.tt(v3(M0[:], 4), v3(Mb[:], 4), bc_mid(gmask[:, 0:128], 4), ALU.mult, r=["B5", "gmask"], w=["B7"])
            k.tt(v3(N0[:], 4), v3(Nb[:], 4), bc_mid(gmask[:, 0:128], 4), ALU.mult, r=["B10", "gmask"], w=["B8"])
            k.tt(v3(Ttb_t[:], 4), v3(M0[:], 4), bc_mid(ident_b[:], 4), ALU.add, r=["B7", "ident_b"], w=["B6"])
            bM, bMk = bank(); bN, bNk = bank()
            for hh in range(4):
                c = slice(hh * 128, (hh + 1) * 128)
                k.mm(bN[:, c], M0[:, c], N0[:, c], r=["B7", "B8"], w=[bNk])
                k.mm(bM[:, c], N0[:, c], M0[:, c], r=["B7", "B8"], w=[bMk])
            k.cp(Qb[:], bN[:], r=[bNk], w=["B0"], e="act")
            k.cp(Db[:], bM[:], r=[bMk], w=["B9"])
            bT, bTk = bank()
            for hh in range(4):
                c = slice(hh * 128, (hh + 1) * 128)
                k.mm(bT[:, c], ident_b[:], Ttb_t[:, c], start=True, stop=False, r=["ident_b", "B6"], w=[bTk])
                k.mm(bT[:, c], Qb[:, c], Ttb_t[:, c], start=False, stop=True, r=["B0", "B6"], w=[bTk])
            k.cp(Ttb_t[:], bT[:], r=[bTk], w=["B6"], e="act")
            bN, bNk = bank()
            for hh in range(4):
                c = slice(hh * 128, (hh + 1) * 128)
                k.mm(bN[:, c], Db[:, c], Qb[:, c], r=["B9", "B0"], w=[bNk])
            k.cp(N0[:], bN[:], r=[bNk], w=["B8"])
            bT, bTk = bank()
            for hh in range(4):
                c = slice(hh * 128, (hh + 1) * 128)
                k.mm(bT[:, c], ident_b[:], Ttb_t[:, c], start=True, stop=False, r=["ident_b", "B6"], w=[bTk])
                k.mm(bT[:, c], N0[:, c], Ttb_t[:, c], start=False, stop=True, r=["B8", "B6"], w=[bTk])
            k.cp(Ttb_t[:], bT[:], r=[bTk], w=["B6"], e="act")
            pb3, pk3 = bbank()
            for hh in range(4):
                c = slice(hh * 128, (hh + 1) * 128)
                k.tr(pb3[:, c], Ttb_t[:, c], ident_b[:], r=["B6", "ident_b"], w=[pk3])
            k.cp(Db[:], pb3[:, 0:512], r=[pk3], w=["B9"])
            Nm = Bt[8]; Mm = Bt[7]
            for li_, m_ in enumerate((8, 16, 32, 64)):
                last = (m_ == 64)
                k.tt(v3(Nm[:], 4), v3(Nb[:], 4), bc_mid(gmask[:, (1 + li_) * 128:(2 + li_) * 128], 4), ALU.mult, r=["B10", "gmask"], w=["B8"])
                if not last:
                    k.tt(v3(Mm[:], 4), v3(Mb[:], 4), bc_mid(gmask[:, (5 + li_) * 128:(6 + li_) * 128], 4), ALU.mult, r=["B5", "gmask"], w=["B7"])
                bP, bPk = bank()
                for hh in range(4):
                    c = slice(hh * 128, (hh + 1) * 128)
                    k.mm(bP[:, c], Nm[:, c], Ttb_t[:, c], r=["B8", "B6"], w=[bPk])
                if not last:
                    bQ2, bQ2k = bank()
                    for hh in range(4):
                        c = slice(hh * 128, (hh + 1) * 128)
                        k.mm(bQ2[:, c], Mm[:, c], Db[:, c], r=["B7", "B9"], w=[bQ2k])
                k.cp(Qb[:], bP[:], r=[bPk], w=["B0"], e="act")
                if not last:
                    k.cp(Pb2[:], bQ2[:], r=[bQ2k], w=["B11"])
                bT, bTk = bank()
                for hh in range(4):
                    c = slice(hh * 128, (hh + 1) * 128)
                    k.mm(bT[:, c], ident_b[:], Ttb_t[:, c], start=True, stop=False, r=["ident_b", "B6"], w=[bTk])
                    k.mm(bT[:, c], Db[:, c], Qb[:, c], start=False, stop=True, r=["B9", "B0"], w=[bTk])
                if not last:
                    bD_, bD_k = bank()
                    for hh in range(4):
                        c = slice(hh * 128, (hh + 1) * 128)
                        k.mm(bD_[:, c], ident_b[:], Db[:, c], start=True, stop=False, r=["ident_b", "B9"], w=[bD_k])
                        k.mm(bD_[:, c], Ttb_t[:, c], Pb2[:, c], start=False, stop=True, r=["B6", "B11"], w=[bD_k])
                k.cp(Ttb_t[:], bT[:], r=[bTk], w=["B6"], e="act")
                if not last:
                    k.cp(Db[:], bD_[:], r=[bD_k], w=["B9"])
            k.tag = 'gdn.state'
            bW, bWk = bank()
            nwT = Bt[9]
            for hh in range(4):
                i = hh // 2; c = slice(hh * 128, (hh + 1) * 128)
                k.mm(bW[:, c], Rw[:, i * 128:(i + 1) * 128], Ttb_t[:, c], r=["B3", "B6"], w=[bWk])
            for hh in range(4):
                c = slice(hh * 128, (hh + 1) * 128)
                k.ts(nwT[:, c], bW[:, c], mlohi[:, hh % 2:hh % 2 + 1], -1.0, op0=ALU.mult, op1=ALU.mult, r=[bWk, "mlohi"], w=["B9"])
            bV, bVk = bank()
            for hh in range(4):
                i = hh // 2; ps_ = slice(64 * (hh % 2), 64 * (hh % 2) + 64); c = slice(hh * 128, (hh + 1) * 128); vs = slice(64 * hh, 64 * hh + 64)
                k.mm(bV[:, vs], Ttb_t[:, c], Ru[:, vs], start=True, stop=False, r=["B6", "B3"], w=[bVk])
                k.mm(bV[:, vs], nwT[:, c], Sgdn_b[:, i * 64:(i + 1) * 64], start=False, stop=True, r=["B9", "Sgdn_b"], w=[bVk])
            vnew = Bt[0][:, 256:512]
            k.cp(vnew, bV[:, 0:256], r=[bVk], w=["B0"], e="act")
            qTm = Bt[8]
            for hh in range(4):
                i = hh // 2
                k.ts(qTm[:, hh * 128:(hh + 1) * 128], qkT[:, i * 128:(i + 1) * 128], mlohi[:, hh % 2:hh % 2 + 1], r=["B1", "mlohi"], w=["B8"])
            bO, bOk = bank(pin=True)
            for hh in range(4):
                i = hh // 2; ps_ = slice(64 * (hh % 2), 64 * (hh % 2) + 64); c = slice(hh * 128, (hh + 1) * 128); vs = slice(64 * hh, 64 * hh + 64)
                k.mm(bO[:, vs], aqk[:, c], vnew[:, vs], r=["B2", "B0"], w=[bOk])
                k.mm(bO[:, 256 + 64 * hh:256 + 64 * hh + 64], qTm[:, c], Sgdn_b[:, i * 64:(i + 1) * 64], r=["B8", "Sgdn_b"], w=[bOk])
            yield
            k.tag = 'gdn.state'
            o_ = F[3][:, 0:256]; t1 = F[3][:, 256:512]
            k.tt(v3(o_, 4), v3(bO[:, 256:512], 4), bc_last(eg, 64), ALU.mult, r=[bOk, "sm_eg"], w=["F3"])
            k.tt(o_, o_, bO[:, 0:256], ALU.add, r=["F3", bOk], w=["F3"])
            k.act(t1, o_, AF.Square, r=["F3"], w=["F3"])
            k.op("dve", lambda g: g.tensor_reduce(out=small[:, 12:16], in_=v3(t1, 4), axis=AX.X, op=ALU.add), r=["F3"], w=["sm_v"])
            rstd_from(small[:, 16:20], "sm_r4", small[:, 12:16], "sm_v", 1.0 / 64)
            k.tt(v3(t1, 4), v3(o_, 4), bc_last(small[:, 16:20], 64), ALU.mult, r=["F3", "sm_r4"], w=["F3"])
            k.tt(v3(t1, 4), v3(t1, 4), bc_mid(gn_gdn[:], 4), ALU.mult, r=["F3", "gn_gdn"], w=["F3"])
            mixd = Bt[2][:, 0:256]
            k.tt(mixd, t1, sg, ALU.mult, r=["F3", "F2", "B2"], w=["B2"])
            to_mixT(mixd, "B2", 6)
            bS, bSk = bank()
            for i in range(2):
                k.mm(bS[:, i * 128:(i + 1) * 128], Kend[:, i * 128:(i + 1) * 128], vnew[:, i * 128:(i + 1) * 128], r=["B4", "B0"], w=[bSk])
            for i in range(2):
                dsl = F[0][:, i * 64:(i + 1) * 64]
                k.ts(dsl, bS[:, i * 128:i * 128 + 64], mlohi[:, 0:1], r=[bSk, "mlohi"], w=["F0"])
                k.stt(dsl, bS[:, i * 128 + 64:(i + 1) * 128], mlohi[:, 1:2], dsl, ALU.mult, ALU.add, r=[bSk, "mlohi", "F0"], w=["F0"])
                k.stt(Sgdn[:, i * 64:(i + 1) * 64], Sgdn[:, i * 64:(i + 1) * 64], dlS[:, i:i + 1], dsl, ALU.mult, ALU.add,
                      r=["Sgdn", "sm_dls", "F0"], w=["Sgdn"])
            k.cp(Sgdn_b[:], Sgdn[:], r=["Sgdn"], w=["Sgdn_b"], e="act")

            unpin(bOk)

        def outproj(t):
            tsl = slice(t * 128, (t + 1) * 128)
            ck("gdn")
            if dbg and l == 0:
                k.cp(F[0][:, 0:512], mixT[:, 0:4, :].rearrange("p a b -> p (a b)"), r=["mixT"], w=["F0"])
                k.cp(F[1][:, 0:512], mixT[:, 4:8, :].rearrange("p a b -> p (a b)"), r=["mixT"], w=["F1"])
                k.dma("sp", dbgo[:, 0:4, tsl], v3(F[0][:], 4), r=["F0"], w=["dbgo"])
                k.dma("sp", dbgo[:, 4:8, tsl], v3(F[1][:], 4), r=["F1"], w=["dbgo"])
            k.tag = 'outproj'
            for nh in range(2):
                bo, bok = bank()
                for kc in range(8):
                    k.mm(bo[:], mixT[:, kc, :], wout[:, kc, nh * 512:(nh + 1) * 512], start=(kc == 0), stop=(kc == 7), r=["mixT", "wout"], w=[bok])
                k.tt(h[:, t, nh * 512:(nh + 1) * 512], h[:, t, nh * 512:(nh + 1) * 512], bo[:], ALU.add, r=["h", bok], w=["h"])

        def step(g):
            try:
                next(g)
            except StopIteration:
                pass

        k.tag = 'normT'
        norm_T(h[:, 0, :], "gcol", hnT, "hnT")
        gcur = gla_gen(0); step(gcur)
        for t in range(NT):
            k.dma("sp", ropeCt[:], ropeC[:, t * 128:(t + 1) * 128])
            k.dma("sp", ropeSt[:], ropeS[:, t * 128:(t + 1) * 128])
            step(gcur)
            s5g = s5_gen(t); step(s5g)
            step(gcur)
            step(s5g)
            rtg = ret_gen(t); step(rtg)
            step(s5g)
            step(rtg)
            gdg = gdn_gen(t); step(gdg)
            step(rtg)
            step(gdg)
            if t + 1 < NT:
                k.tag = 'normT'
                norm_T(h[:, t + 1, :], "gcol", hnT, "hnT")
                gcur = gla_gen(t + 1); step(gcur)
            step(gdg)
            outproj(t)

        ck("phaseA")
        k.barrier()
        colload(gcol[:, 0:8], "gcol", D["norm_ffn"][l].rearrange("(a b) -> a b", b=128), 8)
        for t in range(NT):
            norm_T(h[:, t, :], "gcol", hn2T, "hn2T", ntok_off=t * 128)
        GT = min(4, NT)
        NG = NT // GT
        NTOK = GT * 128
        pieces = []
        f0 = 0
        while f0 < 22:
            fn = min(3, 22 - f0)
            pieces.append((f0, fn)); f0 += fn
        for pi, (f0, fn) in enumerate(pieces):
            bsel = pi % 2
            wupb = W[:, bsel * 9216:bsel * 9216 + 6144].rearrange("p (k n) -> p k n", k=8)
            wdnb = W[:, bsel * 9216 + 6144:bsel * 9216 + 9216].rearrange("p (f n) -> p f n", f=3)
            actb = W[:, 28672 + bsel * 1536:28672 + (bsel + 1) * 1536].rearrange("p (f n) -> p f n", f=3)
            ku, kd, ka = f"wup{bsel}", f"wdn{bsel}", f"actT{bsel}"
            for kc in range(8):
                k.dma("pool", wupb[:, kc, 0:fn * 128], D["w_ffn_up"][l, kc * 128:(kc + 1) * 128, f0 * 128:(f0 + fn) * 128], w=[ku])
                k.dma("pool", wupb[:, kc, 384:384 + fn * 128], D["w_ffn_up"][l, kc * 128:(kc + 1) * 128, FFH + f0 * 128:FFH + (f0 + fn) * 128], w=[ku])
            for fi in range(fn):
                k.dma("pool", wdnb[:, fi, :], D["w_ffn_down"][l, (f0 + fi) * 128:(f0 + fi + 1) * 128, :], w=[kd])
            if pi == 0:
                for kc in range(8):
                    k.dma("pool", wpg[:, kc, :], D["w_ple_gate"][l, kc * 128:(kc + 1) * 128, :], w=["wpg"])
                k.dma("pool", wpp[:], D["w_ple_proj"][l].rearrange("(kc q) n -> q kc n", q=128), w=["wpp"])
            for gi in range(NG):
                g0 = gi * NTOK
                for fi in range(fn):
                    bg_, bgk_ = bank(); bu_, buk_ = bank()
                    for kc in range(8):
                        k.mm(bg_[:, 0:NTOK], wupb[:, kc, fi * 128:(fi + 1) * 128], hn2T[:, kc, g0:g0 + NTOK], start=(kc == 0), stop=(kc == 7), r=[ku, "hn2T"], w=[bgk_])
                    for kc in range(8):
                        k.mm(bu_[:, 0:NTOK], wupb[:, kc, 384 + fi * 128:384 + (fi + 1) * 128], hn2T[:, kc, g0:g0 + NTOK], start=(kc == 0), stop=(kc == 7), r=[ku, "hn2T"], w=[buk_])
                    sgt = Bt[fi % 2]
                    k.act(sgt[:, 0:NTOK], bg_[:, 0:NTOK], AF.Silu, r=[bgk_], w=[f"B{fi % 2}"])
                    k.tt(actb[:, fi, 0:NTOK], sgt[:, 0:NTOK], bu_[:, 0:NTOK], ALU.mult, r=[f"B{fi % 2}", buk_], w=[ka])
                for tt_ in range(GT):
                    t = gi * GT + tt_
                    for nh in range(2):
                        bo, bok = bank()
                        for fi in range(fn):
                            k.mm(bo[:], actb[:, fi, tt_ * 128:(tt_ + 1) * 128], wdnb[:, fi, nh * 512:(nh + 1) * 512], start=(fi == 0), stop=(fi == fn - 1), r=[ka, kd], w=[bok])
                        k.tt(h[:, t, nh * 512:(nh + 1) * 512], h[:, t, nh * 512:(nh + 1) * 512], bo[:], ALU.add, r=["h", bok], w=["h"])
        ck("ffn")
        k.barrier()
        colload(gcol[:, 0:8], "gcol", D["norm_ple"][l].rearrange("(a b) -> a b", b=128), 8)
        for t in range(NT):
            norm_T(h[:, t, :], "gcol", hn2T, "hn2T", ntok_off=t * 128)
        ptmp = A2[:, 8192:8704]
        for t in range(NT):
            pin = Bt[t % 2]; pT = Bt[2 + t % 2]
            k.dma("pool", pin[:, 0:256], D["p"][l, t * 128:(t + 1) * 128, :], w=[f"B{t % 2}"])
            pb, pk = bbank()
            for i in range(2):
                k.tr(pb[:, i * 128:(i + 1) * 128], pin[:, i * 128:(i + 1) * 128], ident_b[:], r=[f"B{t % 2}", "ident_b"], w=[pk])
            k.cp(pT[:, 0:256], pb[:, 0:256], r=[pk], w=[f"B{2 + t % 2}"], e="act")
            for nh in range(2):
                bg_, bgk_ = bank(); bp_, bpk_ = bank()
                for kc in range(8):
                    k.mm(bg_[:], hn2T[:, kc, t * 128:(t + 1) * 128], wpg[:, kc, nh * 512:(nh + 1) * 512], start=(kc == 0), stop=(kc == 7), r=["hn2T", "wpg"], w=[bgk_])
                for kc in range(2):
                    k.mm(bp_[:], pT[:, kc * 128:(kc + 1) * 128], wpp[:, kc, nh * 512:(nh + 1) * 512], start=(kc == 0), stop=(kc == 1), r=[f"B{2 + t % 2}", "wpp"], w=[bpk_])
                k.act(ptmp, bg_[:], AF.Sigmoid, r=[bgk_], w=["ptmp"])
                k.tt(ptmp, ptmp, bp_[:], ALU.mult, r=["ptmp", bpk_], w=["ptmp"])
                k.tt(h[:, t, nh * 512:(nh + 1) * 512], h[:, t, nh * 512:(nh + 1) * 512], ptmp, ALU.add, r=["h", "ptmp"], w=["h"])

    k.barrier()
    ck("ple")
    k.dma("sp", F[2][:], D["norm_final"][0:1, 0:512].partition_broadcast(128), w=["F2"])
    k.dma("sp", F[3][:], D["norm_final"][0:1, 512:1024].partition_broadcast(128), w=["F3"])
    for t in range(NT):
        ss = small[:, 0:1]
        k.act(hn[:], h[:, t, :], AF.Square, accum_out=ss, r=["h"], w=["hn", "sm_ss"])
        rstd_from(small[:, 1:2], "sm_rs", ss, "sm_ss", 1.0 / DM)
        for nh in range(2):
            ob = F[4 + nh]
            k.stt(ob[:], h[:, t, nh * 512:(nh + 1) * 512], small[:, 1:2], F[2 + nh][:], ALU.mult, ALU.mult, r=["h", "sm_rs", f"F{2 + nh}"], w=[f"F{4 + nh}"])
            k.dma("sp", out[t * 128:(t + 1) * 128, nh * 512:(nh + 1) * 512], ob[:], r=[f"F{4 + nh}"], w=["out"])
    k.finish("sp")
    _K[0] = k
    print("built: insts", k.cnt, "waits", k.nwaits, "sbuf_left", nc.sbuf_bytes_remaining, flush=True)
    return nc


_CACHE = {}


def kernel(**inputs):
    L, depth = 2048, 2
    if "nc" not in _CACHE:
        _CACHE["nc"] = build(L, depth)
    nc = _CACHE["nc"]
    shared = {}
    for nm, _ in PSH:
        shared[nm] = np.ascontiguousarray(np.asarray(inputs[nm], dtype=np.float32))
    shared["norm_final"] = np.ascontiguousarray(np.asarray(inputs["norm_final"], dtype=np.float32).reshape(1, 1024))
    x = np.asarray(inputs["x"], dtype=np.float32)
    p = np.asarray(inputs["p"], dtype=np.float32)
    pos = np.asarray(inputs["positions"]).astype(np.int32)
    in_maps = []
    for b in range(8):
        m = dict(shared)
        m["x"] = np.ascontiguousarray(x[b])
        m["p"] = np.ascontiguousarray(p[:, b])
        m["positions"] = np.ascontiguousarray(pos[b:b + 1])
        in_maps.append(m)
    res = run_bass_kernel_spmd(nc, in_maps, core_ids=list(range(8)))
    return np.stack([np.asarray(r["out"], dtype=np.float32) for r in res.results], axis=0)
```

```python
import numpy as np
import concourse.bass as bass
import concourse.mybir as mybir
from contextlib import ExitStack

F32 = mybir.dt.float32
BF16 = mybir.dt.bfloat16
I32 = mybir.dt.int32
ALU = mybir.AluOpType
AF = mybir.ActivationFunctionType
AX = mybir.AxisListType

SEM_CH = 20000
N_DMA_SEMS = 24


_ESZ = {}


def _esize(dt):
    v = _ESZ.get(dt)
    if v is None:
        n = str(dt)
        v = 2 if ("bfloat16" in n or "float16" in n or "int16" in n) else (1 if "8" in n else 4)
        _ESZ[dt] = v
    return v


def _region(a):
    dims = a.ap
    es = _esize(a.dtype)
    ps, pc = dims[0]
    off = a.offset
    if ps > 0:
        p0 = off // ps
        fo = off % ps
        p1 = p0 + pc
    else:
        p0, p1, fo = 0, 1 << 30, off
    lo = hi = fo
    for st, c in dims[1:]:
        ext = st * (c - 1)
        if ext >= 0:
            hi += ext
        else:
            lo += ext
    if type(a.tensor).__name__ == "PSumTensorHandle":
        return (a.tensor.name, p0, p1, 0, 1 << 30)
    return (a.tensor.name, p0, p1, lo * es, (hi + 1) * es)


class _Rec:
    def __init__(self, eng):
        self._eng = eng
        self.reads = []
        self.writes = []

    def __getattr__(self, name):
        f = getattr(self._eng, name)

        def call(*a, **kw):
            for i, v in enumerate(a):
                if type(v).__name__ == "AP":
                    (self.writes if i == 0 else self.reads).append(v)
            for kn, v in kw.items():
                if type(v).__name__ == "AP":
                    (self.writes if kn in ("out", "accum_out", "ap", "out_ap") else self.reads).append(v)
            return f(*a, **kw)
        return call


class _Stub:
    def __init__(self):
        self.reads = []
        self.writes = []

    def __getattr__(self, name):
        def call(*a, **kw):
            for i, v in enumerate(a):
                if type(v).__name__ == "AP":
                    (self.writes if i == 0 else self.reads).append(v)
            for kn, v in kw.items():
                if type(v).__name__ == "AP":
                    (self.writes if kn in ("out", "accum_out", "ap", "out_ap") else self.reads).append(v)
            return None
        return call


class KB:
    def __init__(self, nc, est=None):
        self.nc = nc
        self.st = ExitStack()
        self.eng = {"pe": nc.tensor, "act": nc.scalar, "dve": nc.vector, "pool": nc.gpsimd, "sp": nc.sync}
        self.cnt = {e: 0 for e in self.eng}
        self.sems = {e: [] for e in self.eng}
        self.clock = {e: {} for e in self.eng}
        self.hist = {}
        self.last_w = {}
        self.readers = {}
        self.events = {}
        self.dsem = [self.st.enter_context(nc.semaphore(f"dma{i}")) for i in range(N_DMA_SEMS)]
        self.dcnt = [0] * N_DMA_SEMS
        self.dnext = 0
        self.dnext_pool = 0
        self.nwaits = 0
        self._uid = 0
        self.tag = 'start'
        self.names = {}

    def sb(self, name, shape, dt=F32):
        return self.st.enter_context(self.nc.sbuf_tensor(name, list(shape), dt))

    def ps(self, name, shape, dt=F32):
        return self.st.enter_context(self.nc.psum_tensor(name, list(shape), dt))

    def _sem_for(self, e, n):
        i = (n - 1) // SEM_CH
        while len(self.sems[e]) <= i:
            self.sems[e].append(self.st.enter_context(self.nc.semaphore(f"t_{e}_{len(self.sems[e])}")))
        return self.sems[e][i], ((n - 1) % SEM_CH) + 1

    def _wait(self, e, src, n):
        if self.clock[e].get(src, 0) >= n:
            return
        if isinstance(src, int):
            self.eng[e].wait_ge(self.dsem[src], 16 * n)
        else:
            sem, val = self._sem_for(src, n)
            self.eng[e].wait_ge(sem, val)
        self.nwaits += 1
        ck = self.clock[e]
        for s2, n2 in self.hist.get((src, n), {}).items():
            if ck.get(s2, 0) < n2:
                ck[s2] = n2
        if ck.get(src, 0) < n:
            ck[src] = n

    def _deps(self, e, r, w):
        need = {}

        def add(sn):
            if sn is None:
                return
            s, n = sn
            if s == "pe" and e == "pe":
                return
            if need.get(s, 0) < n:
                need[s] = n
        for key in r:
            add(self.last_w.get(key))
        for key in w:
            add(self.last_w.get(key))
            for s, n in self.readers.get(key, {}).items():
                add((s, n))
        for s, n in need.items():
            self._wait(e, s, n)

    def _rdeps(self, e, reads, writes):
        need = {}
        for kind, regs in (("r", reads), ("w", writes)):
            for (nm, p0, p1, lo, hi) in regs:
                for ev in self.events.get(nm, ()):
                    if kind == "r" and ev[4] != "w":
                        continue
                    if ev[0] < p1 and p0 < ev[1] and ev[2] < hi and lo < ev[3]:
                        s_ = ev[5]
                        if s_ == "pe" and e == "pe":
                            continue
                        if need.get(s_, 0) < ev[6]:
                            need[s_] = ev[6]
        for s_, n in need.items():
            self._wait(e, s_, n)

    def _rupdate(self, src, n, reads, writes):
        for (nm, p0, p1, lo, hi) in reads:
            lst = self.events.setdefault(nm, [])
            for ev in lst:
                if ev[4] == "r" and ev[5] == src and ev[0] == p0 and ev[1] == p1 and ev[2] == lo and ev[3] == hi:
                    ev[6] = n
                    break
            else:
                lst.append([p0, p1, lo, hi, "r", src, n])
        for (nm, p0, p1, lo, hi) in writes:
            lst = self.events.setdefault(nm, [])
            lst[:] = [ev for ev in lst if not (p0 <= ev[0] and ev[1] <= p1 and lo <= ev[2] and ev[3] <= hi)]
            lst.append([p0, p1, lo, hi, "w", src, n])

    def op(self, e, fn, r=(), w=()):
        rec = _Rec(self.eng[e])
        stub = _Stub()
        fn(stub)
        reads = [_region(a) for a in stub.reads]
        writes = [_region(a) for a in stub.writes]
        self._rdeps(e, reads, writes)
        inst = fn(self.eng[e])
        n = self.cnt[e] = self.cnt[e] + 1
        sem, _ = self._sem_for(e, n)
        inst.then_inc(sem, 1)
        try:
            self.names[inst.ins.name] = self.tag
        except Exception:
            pass
        h = dict(self.clock[e])
        h[e] = n
        self.hist[(e, n)] = h
        self._rupdate(e, n, reads, writes)
        return inst

    def dma(self, q, out, in_, r=(), w=(), **kw):
        reads = [_region(in_)]
        writes = [_region(out)]
        self._rdeps(q, reads, writes)
        half = N_DMA_SEMS // 2
        if q == "pool":
            slot = half + self.dnext_pool
            self.dnext_pool = (self.dnext_pool + 1) % half
        else:
            slot = self.dnext
            self.dnext = (self.dnext + 1) % half
        if self.dcnt[slot] > 0:
            self._wait(q, slot, self.dcnt[slot])
        inst = self.eng[q].dma_start(out=out, in_=in_, **kw)
        inst.then_inc(self.dsem[slot], 16)
        n = self.dcnt[slot] = self.dcnt[slot] + 1
        self.hist[(slot, n)] = dict(self.clock[q])
        self._rupdate(slot, n, reads, writes)
        return inst

    def finish(self, e="sp"):
        for s in list(self.eng):
            if s != e and self.cnt[s] > 0:
                self._wait(e, s, self.cnt[s])
        for slot in range(N_DMA_SEMS):
            if self.dcnt[slot] > 0:
                self._wait(e, slot, self.dcnt[slot])

    def mm(self, out, lhsT, rhs, start=True, stop=True, r=(), w=()):
        return self.op("pe", lambda e: e.matmul(out, lhsT=lhsT, rhs=rhs, start=start, stop=stop), r=r, w=w)

    def tr(self, out, in_, ident, r=(), w=()):
        return self.op("pe", lambda e: e.transpose(out, in_, ident), r=r, w=w)

    def act(self, out, in_, func, r=(), w=(), **kw):
        return self.op("act", lambda e: e.activation(out=out, in_=in_, func=func, **kw), r=r, w=w)

    def tt(self, out, in0, in1, op, r=(), w=(), e="dve"):
        return self.op(e, lambda g: g.tensor_tensor(out=out, in0=in0, in1=in1, op=op), r=r, w=w)

    def ts(self, out, in0, s1, s2=None, op0=ALU.mult, op1=None, r=(), w=(), e="dve", **kw):
        if op1 is None:
            return self.op(e, lambda g: g.tensor_scalar(out=out, in0=in0, scalar1=s1, scalar2=None, op0=op0, **kw), r=r, w=w)
        return self.op(e, lambda g: g.tensor_scalar(out=out, in0=in0, scalar1=s1, scalar2=s2, op0=op0, op1=op1, **kw), r=r, w=w)

    def stt(self, out, in0, scalar, in1, op0, op1, r=(), w=()):
        return self.op("dve", lambda g: g.scalar_tensor_tensor(out=out, in0=in0, scalar=scalar, in1=in1, op0=op0, op1=op1), r=r, w=w)

    def cp(self, out, in_, r=(), w=(), e="dve"):
        if e == "act":
            return self.op("act", lambda g: g.copy(out=out, in_=in_), r=r, w=w)
        return self.op(e, lambda g: g.tensor_copy(out=out, in_=in_), r=r, w=w)

    def barrier(self):
        snap = dict(self.cnt)
        dsn = list(self.dcnt)
        for e in self.eng:
            for s, n in snap.items():
                if n > 0:
                    self._wait(e, s, n)
            for slot, n in enumerate(dsn):
                if n > 0:
                    self._wait(e, slot, n)


import math
from concourse.bass_utils import run_bass_kernel_spmd

DM = 1024
INW = 3096
FFH = 2816
PI = math.pi
LN_G = [math.log1p(-(2.0 ** (-5.0 - h))) for h in range(4)]
EPS = 1e-6
PSH = [("norm_mix", [1024]), ("w_in", [1024, 3096]), ("w_out", [1024, 1024]), ("gla_w_a2", [16, 128]),
       ("gla_b_a", [128]), ("gla_norm", [64]), ("s5_lam_re", [16, 64]), ("s5_lam_im", [16, 64]),
       ("s5_log_dt", [16]), ("s5_b_re", [16, 64, 16]), ("s5_b_im", [16, 64, 16]), ("s5_c_re", [16, 16, 64]),
       ("s5_c_im", [16, 16, 64]), ("s5_d", [256]), ("s5_w_glu", [256, 256]), ("s5_b_glu", [256]),
       ("ret_norm", [256]), ("gdn_conv", [4, 768]), ("gdn_a_log", [4]), ("gdn_dt_bias", [4]),
       ("gdn_norm", [64]), ("norm_ffn", [1024]), ("w_ffn_up", [1024, 5632]), ("w_ffn_down", [2816, 1024]),
       ("norm_ple", [1024]), ("w_ple_gate", [1024, 1024]), ("w_ple_proj", [256, 1024])]
FPIECES = [(0, 6), (6, 6), (12, 5), (17, 5)]


_K = [None]


class _Stop(Exception):
    pass


def build(L=2048, depth=2, dbg=False, stop=None):
    try:
        return _build(L, depth, dbg, stop)
    except _Stop as e:
        return e.args[0]


def _build(L, depth, dbg, stop):
    NT = L // 128
    nc = bass.Bass("TRN2", target_bir_lowering=False)
    k = KB(nc)
    D = {}

    def ck(name):
        k.tag = name
        if stop == name:
            k.finish("sp")
            print("STOP at", name, k.cnt, flush=True)
            raise _Stop(nc)

    def din(name, shape, dt=F32):
        D[name] = nc.dram_tensor(name, list(shape), dt, kind="ExternalInput").ap()
    din("x", [L, DM]); din("p", [depth, L, 256]); din("positions", [1, L], I32)
    for nm, sh in PSH:
        din(nm, [depth] + sh)
    din("norm_final", [1, 1024])
    out = nc.dram_tensor("out", [L, DM], F32, kind="ExternalOutput").ap()
    dbgo = nc.dram_tensor("dbg", [128, 8, L], F32, kind="ExternalOutput").ap() if dbg else None

    h = k.sb("h", [128, NT, DM])
    W = k.sb("W", [128, 8 * 3096 + 8 * 1024], BF16)
    win = W[:, 0:8 * 3096].rearrange("p (k n) -> p k n", k=8)
    wout = W[:, 8 * 3096:8 * 3096 + 8192].rearrange("p (k n) -> p k n", k=8)
    wup = W[:, 0:8 * 1536].rearrange("p (k n) -> p k n", k=8)
    wdn = W[:, 12288:12288 + 6 * 1024].rearrange("p (f n) -> p f n", f=6)
    wpg = W[:, 18432:18432 + 8192].rearrange("p (k n) -> p k n", k=8)
    wpp = W[:, 26624:26624 + 2048].rearrange("p (k n) -> p k n", k=2)
    actT = W[:, 28672:28672 + 3072].rearrange("p (f n) -> p f n", f=6)
    NF, NB = 8, 12
    A2 = k.sb("A2", [128, 8704])
    F = [A2[:, i * 512:(i + 1) * 512] for i in range(NF)]
    s5cos = A2[:, 4096:5120]; s5sin = A2[:, 5120:6144]; rtab = A2[:, 6144:7168]
    cdiag = A2[:, 7168:8704].bitcast(BF16)
    hn2T = A2[:, 0:8192].bitcast(BF16).rearrange("p (k n) -> p k n", k=8)
    Bt = [k.sb(f"B{i}", [128, 512], BF16) for i in range(NB)]
    FI = F[7].bitcast(I32)
    Rm = k.sb("Rm", [128, 128], BF16)
    ropeCt = k.sb("ropeCt", [128, 128], BF16); ropeSt = k.sb("ropeSt", [128, 128], BF16)
    ident_f = k.sb("ident_f", [128, 128]); ident_b = k.sb("ident_b", [128, 128], BF16)
    ones_f = k.sb("ones_f", [128, 128]); triL = k.sb("triL", [128, 128]); Ust = k.sb("Ust", [128, 128])
    negU = k.sb("negU", [128, 128]); blk1 = k.sb("blk1", [128, 128])
    ones_lo = k.sb("ones_lo", [128, 128]); ones_hi = k.sb("ones_hi", [128, 128])
    iota1 = k.sb("iota1", [128, 128])
    cms = k.sb("cms", [128, 128])
    dmask = k.sb("dmask", [128, 512]); xi = k.sb("xi", [128, 4]); zeta = k.sb("zeta", [128, 4])
    g128 = k.sb("g128", [128, 2]); mlohi = k.sb("mlohi", [128, 2]); nmlohi = k.sb("nmlohi", [128, 2]); EO = k.sb("EO", [128, 2])
    pcol = k.sb("pcol", [128, 1]); hm = k.sb("hm", [128, 4])
    Brem = k.sb("Brem", [128, 1024], BF16); Bimm = k.sb("Bimm", [128, 1024], BF16)
    Cpr = k.sb("Cpr", [128, 1024], BF16); Cpi = k.sb("Cpi", [128, 1024], BF16); Dfull = k.sb("Dfull", [128, 256], BF16)
    ropeC = nc.dram_tensor("ropeC_d", [128, L], BF16).ap(); ropeS = nc.dram_tensor("ropeS_d", [128, L], BF16).ap()
    stage = k.sb("stage", [8, 128])
    gcol = k.sb("gcol", [128, 8])
    small = k.sb("small", [128, 64])
    wa2 = k.sb("wa2", [16, 128]); nba = k.sb("nba", [128, 1]); gn_gla = k.sb("gn_gla", [128, 64])
    gn_gdn = k.sb("gn_gdn", [128, 64]); gn_ret = k.sb("gn_ret", [128, 256])
    alog_b = k.sb("alog_b", [128, 4]); dtb_b = k.sb("dtb_b", [128, 4]); nexpA = k.sb("nexpA", [128, 4])
    cw = k.sb("cw", [128, 24])
    wglu = k.sb("wglu", [128, 2, 256], BF16); bglu = k.sb("bglu", [128, 2]); dcol = k.sb("dcol", [128, 2])
    s5c = k.sb("s5c", [128, 96])
    gmask = k.sb("gmask", [128, 9 * 128], BF16)
    hn = k.sb("hn", [128, DM], BF16); hnT = k.sb("hnT", [128, 8, 128], BF16); mixT = k.sb("mixT", [128, 8, 128], BF16)
    alow = k.sb("alow", [16, 128])
    Sgla = k.sb("Sgla", [128, 64]); Sgla_b = k.sb("Sgla_b", [128, 64], BF16)
    Sret = k.sb("Sret", [128, 128]); Sret_b = k.sb("Sret_b", [128, 128], BF16)
    Sgdn = k.sb("Sgdn", [128, 128]); Sgdn_b = k.sb("Sgdn_b", [128, 128], BF16)
    xst = k.sb("xst", [128, 16])
    xc = k.sb("xc", [128, 6, 132], BF16)
    PS = [k.ps(f"PS{i}", [128, 512]) for i in range(6)]
    PBF = [k.ps(f"PB{i}", [128, 1024], BF16) for i in range(2)]
    cnt = {"ps": 0, "pb": 0}

    pinned = set()

    def bank(pin=False):
        while True:
            i = cnt["ps"] % 6; cnt["ps"] += 1
            if i not in pinned:
                break
        if pin:
            pinned.add(i)
        return PS[i], f"PS{i}"

    def unpin(key):
        pinned.discard(int(key[2:]))

    def bbank():
        i = cnt["pb"] % 2; cnt["pb"] += 1
        return PBF[i], f"PB{i}"

    def v3(ap, a):
        return ap.rearrange("p (a b) -> p a b", a=a)

    def bc_mid(ap2, a):
        return ap2.unsqueeze(1).broadcast_to([ap2.shape[0], a, ap2.shape[1]])

    def v4(ap):
        return ap.rearrange("p (i q t) -> p i q t", i=2, q=2)

    def src4(ap256):
        return ap256.rearrange("p (i t) -> p i t", i=2).unsqueeze(2).broadcast_to([128, 2, 2, 128])

    def msk4(m2):
        return m2.unsqueeze(1).unsqueeze(3).broadcast_to([128, 2, 2, 128])

    def bc_last(ap2, n):
        return ap2.unsqueeze(2).broadcast_to([ap2.shape[0], ap2.shape[1], n])

    def colload(dst, dkey, src2d, n):
        k.dma("sp", stage[0:n, :], src2d, w=["stage"])
        b, bk = bank()
        k.tr(b[:, 0:n], stage[0:n, :], ident_f[0:n, 0:n], r=["stage", "ident_f"], w=[bk])
        k.cp(dst, b[:, 0:n], r=[bk], w=[dkey])

    def range_reduce(t, tkey, tf, fkey, ti, ikey):
        k.ts(tf, t, 1.0 / (2 * PI), r=[tkey], w=[fkey])
        k.cp(ti, tf, r=[fkey], w=[ikey])
        k.cp(tf, ti, r=[ikey], w=[fkey])
        k.stt(t, tf, -2.0 * PI, t, ALU.mult, ALU.add, r=[fkey, tkey], w=[tkey])
        k.ts(tf, t, PI, -2.0 * PI, op0=ALU.is_gt, op1=ALU.mult, r=[tkey], w=[fkey])
        k.tt(t, t, tf, ALU.add, r=[tkey, fkey], w=[tkey])
        k.ts(tf, t, -PI, 2.0 * PI, op0=ALU.is_lt, op1=ALU.mult, r=[tkey], w=[fkey])
        k.tt(t, t, tf, ALU.add, r=[tkey, fkey], w=[tkey])

    def rstd_from(dst, dkey, src, skey, scale, r_extra=()):
        k.act(dst, src, AF.Ln, scale=scale, bias=EPS, r=[skey] + list(r_extra), w=[dkey])
        k.act(dst, dst, AF.Exp, scale=-0.5, r=[dkey], w=[dkey])

    def norm_T(src_h, gkey_loaded, dstT, dkey, ntok_off=0):
        ss = small[:, 0:1]
        k.act(hn[:], src_h, AF.Square, accum_out=ss, r=["h"], w=["hn", "sm_ss"])
        rstd_from(small[:, 1:2], "sm_rs", ss, "sm_ss", 1.0 / DM)
        k.ts(hn[:], src_h, small[:, 1:2], r=["h", "sm_rs"], w=["hn"])
        pb, pk = bbank()
        for kc in range(8):
            k.tr(pb[:, kc * 128:(kc + 1) * 128], hn[:, kc * 128:(kc + 1) * 128], ident_b[:], r=["hn", "ident_b"], w=[pk])
        k.tt(dstT[:, :, ntok_off:ntok_off + 128], v3(pb[:, 0:1024], 8), bc_last(gcol[:, 0:8], 128), ALU.mult,
             r=[pk, gkey_loaded], w=[dkey])

    def head_norm_gate(o3, okeys, gn_ap, gnkey, sgate, sgkey, dst_bf, dkey, ft, ftkey, center=False):
        t0 = ft[:, 0:256]; t1 = ft[:, 256:512]
        if center:
            k.op("dve", lambda g: g.tensor_reduce(out=small[:, 8:12], in_=o3, axis=AX.X, op=ALU.add), r=okeys, w=["sm_m"])
            k.ts(small[:, 8:12], small[:, 8:12], 1.0 / 64, r=["sm_m"], w=["sm_m"])
            k.tt(v3(t0, 4), o3, bc_last(small[:, 8:12], 64), ALU.subtract, r=okeys + ["sm_m"], w=[ftkey])
            src3 = v3(t0, 4); skeys = [ftkey]
        else:
            src3 = o3; skeys = okeys
        k.act(v3(t1, 4), src3, AF.Square, r=skeys, w=[ftkey])
        k.op("dve", lambda g: g.tensor_reduce(out=small[:, 12:16], in_=v3(t1, 4), axis=AX.X, op=ALU.add), r=[ftkey], w=["sm_v"])
        rstd_from(small[:, 16:20], "sm_r4", small[:, 12:16], "sm_v", 1.0 / 64)
        k.tt(v3(t1, 4), src3, bc_last(small[:, 16:20], 64), ALU.mult, r=skeys + ["sm_r4"], w=[ftkey])
        k.tt(t1, t1, gn_ap, ALU.mult, r=[ftkey, gnkey], w=[ftkey])
        k.tt(dst_bf, t1, sgate, ALU.mult, r=[ftkey, sgkey], w=[dkey])

    def to_mixT(src_bf, skey, c0):
        pb, pk = bbank()
        for i in range(2):
            k.tr(pb[:, i * 128:(i + 1) * 128], src_bf[:, i * 128:(i + 1) * 128], ident_b[:], r=[skey, "ident_b"], w=[pk])
        k.cp(mixT[:, c0:c0 + 2, :], v3(pb[:, 0:256], 2), r=[pk], w=["mixT"], e="act")

    def proj_fm(b, bk, slot, c0, M=128):
        for kc in range(8):
            k.mm(b[0:M, slot * 128:(slot + 1) * 128], win[:, kc, c0:c0 + M], hnT[:, kc, :], start=(kc == 0), stop=(kc == 7),
                 r=["win", "hnT"], w=[bk])

    def proj_tm(b, bk, o0, c0, n):
        for kc in range(8):
            k.mm(b[:, o0:o0 + n], hnT[:, kc, :], win[:, kc, c0:c0 + n], start=(kc == 0), stop=(kc == 7),
                 r=["win", "hnT"], w=[bk])

    P = "pool"
    k.op(P, lambda e: e.memset(ones_f[:], 1.0), w=["ones_f"])
    k.op(P, lambda e: e.affine_select(out=ident_f[:], in_=ones_f[:], pattern=[[-1, 128]], compare_op=ALU.is_equal, fill=0.0, base=0, channel_multiplier=1), r=["ones_f"], w=["ident_f"])
    k.cp(ident_b[:], ident_f[:], r=["ident_f"], w=["ident_b"], e=P)
    k.op(P, lambda e: e.affine_select(out=triL[:], in_=ones_f[:], pattern=[[1, 128]], compare_op=ALU.is_ge, fill=0.0, base=0, channel_multiplier=-1), r=["ones_f"], w=["triL"])
    k.op(P, lambda e: e.affine_select(out=Ust[:], in_=ones_f[:], pattern=[[-1, 128]], compare_op=ALU.is_gt, fill=0.0, base=0, channel_multiplier=1), r=["ones_f"], w=["Ust"])
    k.op(P, lambda e: e.affine_select(out=negU[:], in_=ones_f[:], pattern=[[1, 128]], compare_op=ALU.is_gt, fill=0.0, base=0, channel_multiplier=-1), r=["ones_f"], w=["negU"])
    k.ts(negU[:], negU[:], -1.0, r=["negU"], w=["negU"], e=P)
    k.op(P, lambda e: e.memset(blk1[:], 0.0), w=["blk1"])
    k.op(P, lambda e: e.memset(blk1[0:64, 0:64], 1.0), w=["blk1"])
    k.op(P, lambda e: e.memset(blk1[64:128, 64:128], 1.0), w=["blk1"])
    k.op(P, lambda e: e.memset(ones_lo[:], 0.0), w=["ones_lo"])
    k.op(P, lambda e: e.memset(ones_lo[:, 0:64], 1.0), w=["ones_lo"])
    k.op(P, lambda e: e.memset(ones_hi[:], 0.0), w=["ones_hi"])
    k.op(P, lambda e: e.memset(ones_hi[:, 64:128], 1.0), w=["ones_hi"])
    k.op(P, lambda e: e.memset(mlohi[:], 0.0), w=["mlohi"])
    k.op(P, lambda e: e.memset(mlohi[0:64, 0:1], 1.0), w=["mlohi"])
    k.op(P, lambda e: e.memset(mlohi[64:128, 1:2], 1.0), w=["mlohi"])
    k.ts(nmlohi[:], mlohi[:], -1.0, e=P)
    for i in range(2):
        k.op(P, lambda e: e.memset(g128[0:64, i:i + 1], math.exp(128 * LN_G[2 * i])), w=["g128"])
        k.op(P, lambda e: e.memset(g128[64:128, i:i + 1], math.exp(128 * LN_G[2 * i + 1])), w=["g128"])
    k.op(P, lambda e: e.iota(iota1[:], pattern=[[1, 128]], base=1, channel_multiplier=0, allow_small_or_imprecise_dtypes=True), w=["iota1"])
    k.op(P, lambda e: e.iota(cms[:], pattern=[[1, 128]], base=0, channel_multiplier=-1, allow_small_or_imprecise_dtypes=True), w=["cms"])
    k.op(P, lambda e: e.iota(pcol[:], pattern=[[0, 1]], base=0, channel_multiplier=1, allow_small_or_imprecise_dtypes=True), w=["pcol"])
    k.op("dve", lambda g: g.tensor_reduce(out=F[0][:, 0:8], in_=ident_f[:].rearrange("p (a b c) -> p a b c", a=4, b=2), axis=AX.X, op=ALU.add), r=["ident_f"], w=["F0"])
    k.op("dve", lambda g: g.tensor_reduce(out=EO[:], in_=F[0][:, 0:8].rearrange("p (a b) -> p b a", a=4), axis=AX.X, op=ALU.add), r=["F0"], w=["EO"])
    k.op("dve", lambda g: g.tensor_reduce(out=hm[:], in_=v3(ident_f[:], 4), axis=AX.X, op=ALU.add), r=["ident_f"], w=["hm"])
    rv4 = lambda ap: ap.rearrange("p (b h d) -> p b h d", b=2, h=2)
    k.op(P, lambda e: e.affine_select(out=F[0][:, 0:128], in_=ones_f[:], pattern=[[-1, 128]], compare_op=ALU.is_equal, fill=0.0, base=-32, channel_multiplier=1), r=["ones_f"], w=["F0"])
    k.op(P, lambda e: e.memset(rv4(F[0][:, 0:128])[:, :, 1, :], 0.0), w=["F0"])
    k.op(P, lambda e: e.affine_select(out=F[1][:, 0:128], in_=ones_f[:], pattern=[[-1, 128]], compare_op=ALU.is_equal, fill=0.0, base=32, channel_multiplier=1), r=["ones_f"], w=["F1"])
    k.op(P, lambda e: e.memset(rv4(F[1][:, 0:128])[:, :, 0, :], 0.0), w=["F1"])
    k.tt(Rm[:], F[1][:, 0:128], F[0][:, 0:128], ALU.subtract, r=["F0", "F1"], w=["Rm"], e=P)
    def bdmask(dst, m):
        nb_ = 128 // m
        k.op(P, lambda e: e.affine_select(out=dst, in_=ones_f[:], pattern=[[-m, nb_], [0, m]], compare_op=ALU.is_ge, fill=0.0, base=0, channel_multiplier=1), r=["ones_f"], w=["F2"])
        k.op(P, lambda e: e.affine_select(out=dst, in_=dst, pattern=[[m, nb_], [0, m]], compare_op=ALU.is_ge, fill=0.0, base=m - 1, channel_multiplier=-1), r=["F2"], w=["F2"])
    bds = {8: F[2][:, 0:128], 16: F[2][:, 128:256], 32: F[2][:, 256:384], 64: F[2][:, 384:512], 128: ones_f[:]}
    for m_ in (8, 16, 32, 64):
        bdmask(bds[m_], m_)
    k.cp(gmask[:, 0:128], bds[8], r=["F2"], w=["gmask"])
    for li_, m_ in enumerate((8, 16, 32, 64)):
        k.tt(F[3][:, 0:128], bds[2 * m_], bds[m_], ALU.subtract, r=["F2", "ones_f"], w=["F3"])
        k.tt(gmask[:, (1 + li_) * 128:(2 + li_) * 128], F[3][:, 0:128], Ust[:], ALU.mult, r=["F3", "Ust"], w=["gmask"])
        k.stt(gmask[:, (5 + li_) * 128:(6 + li_) * 128], F[3][:, 0:128], -1.0, negU[:], ALU.mult, ALU.mult, r=["F3", "negU"], w=["gmask"])
    for hh in range(4):
        k.act(dmask[:, hh * 128:(hh + 1) * 128], cms[:], AF.Exp, scale=LN_G[hh], r=["cms"], w=["dmask"])
        k.act(xi[:, hh:hh + 1], pcol[:], AF.Exp, scale=LN_G[hh], bias=LN_G[hh], r=["pcol"], w=["xi"])
        k.act(zeta[:, hh:hh + 1], pcol[:], AF.Exp, scale=-LN_G[hh], bias=127.0 * LN_G[hh], r=["pcol"], w=["zeta"])
    k.tt(v3(dmask[:], 4), v3(dmask[:], 4), bc_mid(triL[:], 4), ALU.mult, r=["dmask", "triL"], w=["dmask"])
    fidx = small[:, 32:33]; invf = small[:, 33:34]
    for b4 in range(4):
        k.op(P, lambda e: e.iota(fidx[32 * b4:32 * b4 + 32, :], pattern=[[0, 1]], base=0, channel_multiplier=1, allow_small_or_imprecise_dtypes=True), w=["fidx"])
    k.act(invf, fidx, AF.Exp, scale=-math.log(10000.0) / 31.0, r=["fidx"], w=["invf"])
    for c in range(0, L, 512):
        n = min(512, L - c)
        k.dma("sp", FI[:, 0:n], D["positions"][0:1, c:c + n].partition_broadcast(128), w=["F7"])
        k.cp(F[1][:, 0:n], FI[:, 0:n], r=["F7"], w=["F1"])
        k.ts(F[0][:, 0:n], F[1][:, 0:n], invf, r=["F1", "invf"], w=["F0"])
        k.ts(F[2][:, 0:n], F[0][:, 0:n], PI / 2, op0=ALU.add, r=["F0"], w=["F2"])
        range_reduce(F[0][:, 0:n], "F0", F[3][:, 0:n], "F3", FI[:, 0:n], "F7")
        k.act(Bt[0][:, 0:n], F[0][:, 0:n], AF.Sin, r=["F0"], w=["B0"])
        k.dma("sp", ropeS[:, c:c + n], Bt[0][:, 0:n], r=["B0"], w=["ropeS"])
        range_reduce(F[2][:, 0:n], "F2", F[3][:, 0:n], "F3", FI[:, 0:n], "F7")
        k.act(Bt[1][:, 0:n], F[2][:, 0:n], AF.Sin, r=["F2"], w=["B1"])
        k.dma("sp", ropeC[:, c:c + n], Bt[1][:, 0:n], r=["B1"], w=["ropeC"])
    for t in range(NT):
        k.dma("sp", h[:, t, :], D["x"][t * 128:(t + 1) * 128, :], w=["h"])

    ck("const")
    for l in range(depth):
        k.barrier()
        for kc in range(8):
            k.dma("pool", win[:, kc, 0:INW], D["w_in"][l, kc * 128:(kc + 1) * 128, :], w=["win"])
        for kc in range(8):
            k.dma("pool", wout[:, kc, :], D["w_out"][l, kc * 128:(kc + 1) * 128, :], w=["wout"])
        k.dma("pool", wglu[:], D["s5_w_glu"][l].rearrange("(kc q) n -> q kc n", q=128), w=["wglu"])
        ck("wload")
        colload(gcol[:, 0:8], "gcol", D["norm_mix"][l].rearrange("(a b) -> a b", b=128), 8)
        k.dma("sp", wa2[:], D["gla_w_a2"][l], w=["wa2"])
        colload(nba[:, 0:1], "nba", D["gla_b_a"][l].rearrange("(a b) -> a b", b=128), 1)
        k.ts(nba[:], nba[:], -1.0, r=["nba"], w=["nba"])
        k.dma("sp", gn_gla[:], D["gla_norm"][l:l + 1, :].partition_broadcast(128), w=["gn_gla"])
        k.dma("sp", gn_gdn[:], D["gdn_norm"][l:l + 1, :].partition_broadcast(128), w=["gn_gdn"])
        k.dma("sp", gn_ret[:], D["ret_norm"][l:l + 1, :].partition_broadcast(128), w=["gn_ret"])
        k.dma("sp", alog_b[:], D["gdn_a_log"][l:l + 1, :].partition_broadcast(128), w=["alog_b"])
        k.dma("sp", dtb_b[:], D["gdn_dt_bias"][l:l + 1, :].partition_broadcast(128), w=["dtb_b"])
        k.act(nexpA[:], alog_b[:], AF.Exp, r=["alog_b"], w=["nexpA"])
        k.ts(nexpA[:], nexpA[:], -1.0, r=["nexpA"], w=["nexpA"])
        for i in range(6):
            colload(cw[:, i * 4:(i + 1) * 4], "cw", D["gdn_conv"][l, :, i * 128:(i + 1) * 128], 4)
        for i in range(6):
            for j in range(4):
                k.ts(cdiag[:, (i * 4 + j) * 128:(i * 4 + j + 1) * 128], ident_f[:], cw[:, i * 4 + j:i * 4 + j + 1], r=["ident_f", "cw"], w=["cdiag"])
        colload(bglu[:, 0:2], "bglu", D["s5_b_glu"][l].rearrange("(a b) -> a b", b=128), 2)
        colload(dcol[:, 0:2], "dcol", D["s5_d"][l].rearrange("(a b) -> a b", b=128), 2)
        ck("params")
        s8 = F[4]
        k.dma("sp", s8[0:8, 0:128], D["s5_lam_re"][l].rearrange("(j g) p -> j (g p)", g=2), w=["F4"])
        k.dma("sp", s8[0:8, 128:256], D["s5_lam_im"][l].rearrange("(j g) p -> j (g p)", g=2), w=["F4"])
        k.dma("sp", s8[0:8, 256:258], D["s5_log_dt"][l].rearrange("(j g) -> j g", g=2), w=["F4"])
        k.act(s8[0:8, 256:258], s8[0:8, 256:258], AF.Exp, r=["F4"], w=["F4"])
        k.ts(s8[0:8, 0:128], s8[0:8, 0:128], -1e-4, op0=ALU.min, r=["F4"], w=["F4"])
        dt3 = s8[0:8, 256:258].unsqueeze(2).broadcast_to([8, 2, 64])
        k.tt(v3(s8[0:8, 260:388], 2), v3(s8[0:8, 0:128], 2), dt3, ALU.mult, r=["F4"], w=["F4"])
        k.tt(v3(F[5][0:8, 0:128], 2), v3(s8[0:8, 128:256], 2), dt3, ALU.mult, r=["F4"], w=["F5"])
        b, bk = bank()
        k.tr(b[:, 0:8], s8[0:8, 0:128], ident_f[0:8, 0:8], r=["F4", "ident_f"], w=[bk])
        k.tr(b[:, 8:16], s8[0:8, 128:256], ident_f[0:8, 0:8], r=["F4", "ident_f"], w=[bk])
        k.tr(b[:, 16:24], s8[0:8, 260:388], ident_f[0:8, 0:8], r=["F4", "ident_f"], w=[bk])
        k.tr(b[:, 24:32], F[5][0:8, 0:128], ident_f[0:8, 0:8], r=["F5", "ident_f"], w=[bk])
        k.cp(s5c[:, 0:32], b[:, 0:32], r=[bk], w=["s5c"])
        lr = s5c[:, 0:8]; li = s5c[:, 8:16]; lrd = s5c[:, 16:24]; th = s5c[:, 24:32]
        rr_ = s5c[:, 32:40]
        k.act(rr_, lrd, AF.Exp, r=["s5c"], w=["s5c"])
        k.cp(v3(rtab[:], 8), bc_last(rr_, 128), r=["s5c"], w=["rtab"])
        k.op("dve", lambda g: g.memset(v3(rtab[:], 8)[:, :, 0], 0.0), w=["rtab"])
        for half in range(2):
            sl = slice(half * 512, (half + 1) * 512)
            k.tt(v3(F[0][:], 4), bc_mid(iota1[:], 4), bc_last(th[:, half * 4:half * 4 + 4], 128), ALU.mult, r=["iota1", "s5c"], w=["F0"])
            k.ts(F[2][:], F[0][:], PI / 2, op0=ALU.add, r=["F0"], w=["F2"])
            range_reduce(F[0][:], "F0", F[3][:], "F3", FI[:], "F7")
            k.act(s5sin[:, sl], F[0][:], AF.Sin, r=["F0"], w=["s5sin"])
            range_reduce(F[2][:], "F2", F[3][:], "F3", FI[:], "F7")
            k.act(s5cos[:, sl], F[2][:], AF.Sin, r=["F2"], w=["s5cos"])
        k.cp(F[0][:, 0:8], th, r=["s5c"], w=["F0"])
        k.ts(F[2][:, 0:8], th, PI / 2, op0=ALU.add, r=["s5c"], w=["F2"])
        range_reduce(F[0][:, 0:8], "F0", F[3][:, 0:8], "F3", FI[:, 0:8], "F7")
        range_reduce(F[2][:, 0:8], "F2", F[3][:, 0:8], "F3", FI[:, 0:8], "F7")
        sth = s5c[:, 40:48]; cth = s5c[:, 48:56]
        k.act(sth, F[0][:, 0:8], AF.Sin, r=["F0"], w=["s5c"])
        k.act(cth, F[2][:, 0:8], AF.Sin, r=["F2"], w=["s5c"])
        am1 = s5c[:, 56:64]; ai = s5c[:, 64:72]; c1r = s5c[:, 72:80]; c1i = s5c[:, 80:88]; den = s5c[:, 88:96]
        k.tt(am1, rr_, cth, ALU.mult, r=["s5c"], w=["s5c"])
        k.ts(am1, am1, -1.0, op0=ALU.add, r=["s5c"], w=["s5c"])
        k.tt(ai, rr_, sth, ALU.mult, r=["s5c"], w=["s5c"])
        t8a = F[0][:, 16:24]; t8b = F[0][:, 24:32]
        k.tt(den, lr, lr, ALU.mult, r=["s5c"], w=["s5c"])
        k.tt(t8a, li, li, ALU.mult, r=["s5c"], w=["F0"])
        k.tt(den, den, t8a, ALU.add, r=["s5c", "F0"], w=["s5c"])
        k.op("dve", lambda g: g.reciprocal(out=den, in_=den), r=["s5c"], w=["s5c"])
        k.tt(t8a, am1, lr, ALU.mult, r=["s5c"], w=["F0"])
        k.tt(t8b, ai, li, ALU.mult, r=["s5c"], w=["F0"])
        k.tt(t8a, t8a, t8b, ALU.add, r=["F0"], w=["F0"])
        k.tt(c1r, t8a, den, ALU.mult, r=["F0", "s5c"], w=["s5c"])
        k.tt(t8a, ai, lr, ALU.mult, r=["s5c"], w=["F0"])
        k.tt(t8b, am1, li, ALU.mult, r=["s5c"], w=["F0"])
        k.tt(t8a, t8a, t8b, ALU.subtract, r=["F0"], w=["F0"])
        k.tt(c1i, t8a, den, ALU.mult, r=["F0", "s5c"], w=["s5c"])
        braw = F[5];
        k.dma("sp", v3(braw[:, 0:128], 8), D["s5_b_re"][l].rearrange("(j g) p h -> (g p) j h", g=2), w=["F5"])
        k.dma("sp", v3(braw[:, 128:256], 8), D["s5_b_im"][l].rearrange("(j g) p h -> (g p) j h", g=2), w=["F5"])
        bre3 = v3(braw[:, 0:128], 8); bim3 = v3(braw[:, 128:256], 8)
        tA = v3(F[6][:, 0:128], 8); tB = v3(F[6][:, 128:256], 8); bbr = v3(F[6][:, 256:384], 8); bbi = v3(F[6][:, 384:512], 8)
        k.tt(tA, bre3, bc_last(c1r, 16), ALU.mult, r=["F5", "s5c"], w=["F6"])
        k.tt(tB, bim3, bc_last(c1i, 16), ALU.mult, r=["F5", "s5c"], w=["F6"])
        k.tt(bbr, tA, tB, ALU.subtract, r=["F6", "F6"], w=["F6"])
        k.tt(tA, bim3, bc_last(c1r, 16), ALU.mult, r=["F5", "s5c"], w=["F6"])
        k.tt(tB, bre3, bc_last(c1i, 16), ALU.mult, r=["F5", "s5c"], w=["F6"])
        k.tt(bbi, tA, tB, ALU.add, r=["F6", "F6"], w=["F6"])
        Bre = Bt[2][:, 0:256]; Bim = Bt[2][:, 256:512]; CreT = Bt[3][:, 0:256]; nCimT = Bt[3][:, 256:512]
        for (src3, skey, dstB, dkey) in ((bbr, "F6", Bre, "B2"), (bbi, "F6", Bim, "B2")):
            X = Bt[0][:, 0:256].rearrange("p (j c) -> p j c", j=8)
            k.ts(X[:, :, 0:16], src3, mlohi[:, 0:1], r=[skey, "mlohi"], w=["B0"])
            k.ts(X[:, :, 16:32], src3, mlohi[:, 1:2], r=[skey, "mlohi"], w=["B0"])
            pb, pk = bbank()
            for T in range(2):
                k.tr(pb[:, T * 128:(T + 1) * 128], Bt[0][:, T * 128:(T + 1) * 128], ident_b[:], r=["B0", "ident_b"], w=[pk])
            k.cp(dstB[:], pb[:, 0:256], r=[pk], w=[dkey])
            dstM, dmk = (Brem, "Brem") if dstB is Bre else (Bimm, "Bimm")
            for T in range(2):
                k.tt(v3(dstM[:, T * 512:(T + 1) * 512], 4), bc_mid(dstB[:, T * 128:(T + 1) * 128], 4), bc_last(hm[:], 128), ALU.mult, r=[dkey, "hm"], w=[dmk])
        craw = F[5]
        k.dma("sp", v3(craw[:, 0:128], 2), D["s5_c_re"][l].rearrange("(t g) h p -> (g h) t p", t=2), w=["F5"])
        k.dma("sp", v3(craw[:, 128:256], 2), D["s5_c_im"][l].rearrange("(t g) h p -> (g h) t p", t=2), w=["F5"])
        for (c0, sgn, dstC, dkey) in ((0, 1.0, CreT, "B3"), (128, -1.0, nCimT, "B3")):
            Y = Bt[0][:, 0:256].rearrange("p (t c) -> p t c", t=2)
            cs3 = v3(craw[:, c0:c0 + 128], 2)
            k.ts(Y[:, :, 0:64], cs3, EO[:, 0:1], sgn, op0=ALU.mult, op1=ALU.mult, r=["F5", "EO"], w=["B0"])
            k.ts(Y[:, :, 64:128], cs3, EO[:, 1:2], sgn, op0=ALU.mult, op1=ALU.mult, r=["F5", "EO"], w=["B0"])
            pb, pk = bbank()
            for T in range(2):
                k.tr(pb[:, T * 128:(T + 1) * 128], Bt[0][:, T * 128:(T + 1) * 128], ident_b[:], r=["B0", "ident_b"], w=[pk])
            k.cp(dstC[:], pb[:, 0:256], r=[pk], w=[dkey])
            dstP, dpk = (Cpr, "Cpr") if dstC is CreT else (Cpi, "Cpi")
            k.op("dve", lambda g: g.memset(dstP[:], 0.0), w=[dpk])
            for T in range(2):
                for jj in range(4):
                    k.cp(dstP[:, T * 512 + jj * 128 + 32 * jj:T * 512 + jj * 128 + 32 * jj + 32], dstC[:, T * 128 + 32 * jj:T * 128 + 32 * jj + 32], r=[dkey], w=[dpk])
        for T in range(2):
            k.ts(Dfull[:, T * 128:(T + 1) * 128], ident_f[:], dcol[:, T:T + 1], r=["ident_f", "dcol"], w=["Dfull"])
        ck("s5setup")
        for (tn, key) in ((Sgla, "Sgla"), (Sret, "Sret"), (Sgdn, "Sgdn"), (xst, "xst")):
            k.op("dve", lambda g: g.memset(tn[:], 0.0), w=[key])
        for (tn, key) in ((Sgla_b, "Sgla_b"), (Sret_b, "Sret_b"), (Sgdn_b, "Sgdn_b"), (xc, "xc")):
            k.op("dve", lambda g: g.memset(tn[:], 0.0), w=[key])

        def gla_gen(t):
            k.tag = 'gla0'
            b, bk = bank(pin=True)
            proj_fm(b, bk, 0, 0); proj_fm(b, bk, 1, 128); proj_fm(b, bk, 2, 512, M=16)
            b2, b2k = bank(pin=True)
            proj_tm(b2, b2k, 0, 256, 256); proj_tm(b2, b2k, 256, 528, 256)
            yield
            k.tag = 'gla0'
            k.cp(alow[:], b[0:16, 256:384], r=[bk], w=["alow"])
            k.mm(b[:, 384:512], wa2[:], alow[:], r=["wa2", "alow"], w=[bk])
            e1 = F[0][:, 0:128]; sp_ = F[0][:, 128:256]; cs = F[0][:, 256:384]
            k.act(e1, b[:, 384:512], AF.Exp, scale=-1.0, bias=nba[:, 0:1], r=[bk, "nba"], w=["F0"])
            k.act(sp_, e1, AF.Ln, bias=1.0, r=["F0"], w=["F0"])
            k.op("dve", lambda g: g.tensor_tensor_scan(out=cs, data0=ones_f[:], data1=sp_, initial=0.0, op0=ALU.mult, op1=ALU.add), r=["ones_f", "F0"], w=["F0"])
            eb = F[1][:, 0:128]; enb = F[1][:, 128:256]; eke = F[1][:, 256:384]
            nbl = small[:, 2:3]; ebl = small[:, 3:4]
            k.ts(nbl, cs[:, 127:128], -1.0 / 16, r=["F0"], w=["sm_nbl"])
            k.act(eb, cs, AF.Exp, scale=-1.0 / 16, r=["F0"], w=["F1"])
            k.act(enb, cs, AF.Exp, scale=1.0 / 16, r=["F0"], w=["F1"])
            k.act(eke, cs, AF.Exp, scale=1.0 / 16, bias=nbl, r=["F0", "sm_nbl"], w=["F1"])
            k.act(ebl, nbl, AF.Exp, r=["sm_nbl"], w=["sm_ebl"])
            qtT = Bt[0][:, 0:128]; ktT = Bt[0][:, 128:256]; keT = Bt[0][:, 256:384]
            k.stt(qtT, b[:, 0:128], 32.0 ** -0.5, eb, ALU.mult, ALU.mult, r=[bk, "F1"], w=["B0"])
            k.tt(ktT, b[:, 128:256], enb, ALU.mult, r=[bk, "F1"], w=["B0"])
            k.tt(keT, b[:, 128:256], eke, ALU.mult, r=[bk, "F1"], w=["B0"])
            pb, pk = bbank()
            k.tr(pb[:, 0:128], keT, ident_b[:], r=["B0", "ident_b"], w=[pk])
            kend = Bt[0][:, 384:512]
            k.cp(kend, pb[:, 0:128], r=[pk], w=["B0"], e="act")
            v_bf = Bt[1][:, 0:256]; sr = F[2][:, 0:256]
            k.cp(v_bf, b2[:, 0:256], r=[b2k], w=["B1"], e="act")
            k.act(sr, b2[:, 256:512], AF.Silu, r=[b2k], w=["F2"])
            qtTm = Bt[3]; ktTm = Bt[4]
            k.tt(v3(qtTm[:], 4), bc_mid(qtT, 4), bc_last(hm[:], 128), ALU.mult, r=["B0", "hm"], w=["B3"])
            k.tt(v3(ktTm[:], 4), bc_mid(ktT, 4), bc_last(hm[:], 128), ALU.mult, r=["B0", "hm"], w=["B4"])
            b3, b3k = bank()
            for hh in range(4):
                k.mm(b3[:, hh * 128:(hh + 1) * 128], ktTm[:, hh * 128:(hh + 1) * 128], qtT, r=["B0", "B4"], w=[b3k])
            attn = Bt[2]
            k.tt(v3(attn[:], 4), v3(b3[:], 4), bc_mid(triL[:], 4), ALU.mult, r=[b3k, "triL"], w=["B2"])
            b4, b4k = bank(pin=True)
            for hh in range(4):
                ps_ = slice(32 * hh, 32 * hh + 32); vs = slice(64 * hh, 64 * hh + 64)
                k.mm(b4[:, vs], attn[:, hh * 128:(hh + 1) * 128], v_bf[:, vs], start=True, stop=False, r=["B2", "B1"], w=[b4k])
                k.mm(b4[:, vs], qtTm[:, hh * 128:(hh + 1) * 128], Sgla_b[:, :], start=False, stop=True, r=["B3", "Sgla_b"], w=[b4k])
            unpin(bk); unpin(b2k)
            yield
            k.tag = 'gla0'
            mixa = Bt[1][:, 256:512]
            head_norm_gate(v3(b4[:, 0:256], 4), [b4k], bc_mid(gn_gla[:], 4).rearrange("p a b -> p (a b)") if False else gn_gla[:].unsqueeze(1).broadcast_to([128, 4, 64]),
                           "gn_gla", sr, "F2", mixa, "B1", F[3], "F3", center=False) if False else None
            t1 = F[3][:, 256:512]
            k.act(v3(t1, 4), v3(b4[:, 0:256], 4), AF.Square, r=[b4k], w=["F3"])
            k.op("dve", lambda g: g.tensor_reduce(out=small[:, 12:16], in_=v3(t1, 4), axis=AX.X, op=ALU.add), r=["F3"], w=["sm_v"])
            rstd_from(small[:, 16:20], "sm_r4", small[:, 12:16], "sm_v", 1.0 / 64)
            k.tt(v3(t1, 4), v3(b4[:, 0:256], 4), bc_last(small[:, 16:20], 64), ALU.mult, r=[b4k, "sm_r4"], w=["F3"])
            k.tt(v3(t1, 4), v3(t1, 4), bc_mid(gn_gla[:], 4), ALU.mult, r=["F3", "gn_gla"], w=["F3"])
            k.tt(mixa, t1, sr, ALU.mult, r=["F3", "F2"], w=["B1"])
            to_mixT(mixa, "B1", 0)
            b5, b5k = bank()
            k.mm(b5[:, 0:256], kend, v_bf, r=["B0", "B1"], w=[b5k])
            k.tt(v3(F[1][:, 0:256], 4), v3(b5[:, 0:256], 4), bc_last(hm[:], 64), ALU.mult, r=[b5k, "hm"], w=["F1"])
            k.op("dve", lambda g: g.tensor_reduce(out=F[1][:, 256:320], in_=F[1][:, 0:256].rearrange("p (h v) -> p v h", h=4), axis=AX.X, op=ALU.add), r=["F1"], w=["F1"])
            k.stt(Sgla[:], Sgla[:], ebl, F[1][:, 256:320], ALU.mult, ALU.add, r=["Sgla", "sm_ebl", "F1"], w=["Sgla"])
            k.cp(Sgla_b[:], Sgla[:], r=["Sgla"], w=["Sgla_b"], e="act")

            unpin(b4k)

        def s5_gen(t):
            ck("gla")
            bu, buk = bank()
            proj_fm(bu, buk, 0, 784); proj_fm(bu, buk, 1, 912)
            uT = Bt[3][:, 0:256]
            k.cp(uT, bu[:, 0:256], r=[buk], w=["B3"], e="act")
            yield
            k.tag = 'gla'
            bY, bYk = bank(pin=True)
            for half in range(2):
                bR, bRk = bank(); bI, bIk = bank()
                T = half
                for jj in range(4):
                    k.mm(bR[:, jj * 128:(jj + 1) * 128], Brem[:, T * 512 + jj * 128:T * 512 + (jj + 1) * 128], uT[:, T * 128:(T + 1) * 128], r=["Brem", "B3"], w=[bRk])
                    k.mm(bI[:, jj * 128:(jj + 1) * 128], Bimm[:, T * 512 + jj * 128:T * 512 + (jj + 1) * 128], uT[:, T * 128:(T + 1) * 128], r=["Bimm", "B3"], w=[bIk])
                csl = s5cos[:, half * 512:(half + 1) * 512]; ssl = s5sin[:, half * 512:(half + 1) * 512]
                Vre = F[4]; Vim = F[5]
                (t1_, k1), (t2_, k2) = ((F[6], "F6"), (F[7], "F7")) if half == 0 else ((F[0], "F0"), (F[1], "F1"))
                k.tt(t1_[:], bR[:], csl, ALU.mult, r=[bRk, "s5cos"], w=[k1])
                k.tt(t2_[:], bI[:], ssl, ALU.mult, r=[bIk, "s5sin"], w=[k2])
                k.tt(Vre[:], t1_[:], t2_[:], ALU.add, r=[k1, k2], w=["F4"])
                k.tt(t1_[:], bI[:], csl, ALU.mult, r=[bIk, "s5cos"], w=[k1])
                k.tt(t2_[:], bR[:], ssl, ALU.mult, r=[bRk, "s5sin"], w=[k2])
                k.tt(Vim[:], t1_[:], t2_[:], ALU.subtract, r=[k1, k2], w=["F5"])
                hs = slice(half * 4, half * 4 + 4)
                k.tt(small[:, 56:60], xst[:, hs], s5c[:, 32 + half * 4:36 + half * 4], ALU.mult, r=["xst", "s5c"], w=["sm_xi"])
                k.tt(small[:, 60:64], xst[:, 8 + half * 4:12 + half * 4], s5c[:, 32 + half * 4:36 + half * 4], ALU.mult, r=["xst", "s5c"], w=["sm_xi"])
                k.tt(v3(Vre[:], 4)[:, :, 0], v3(Vre[:], 4)[:, :, 0], small[:, 56:60], ALU.add, r=["F4", "sm_xi"], w=["F4"])
                k.tt(v3(Vim[:], 4)[:, :, 0], v3(Vim[:], 4)[:, :, 0], small[:, 60:64], ALU.add, r=["F5", "sm_xi"], w=["F5"])
                Wre, Wim = t1_, t2_
                rt = rtab[:, half * 512:(half + 1) * 512]
                k.op("dve", lambda g: g.tensor_tensor_scan(out=Wre[:], data0=rt, data1=Vre[:], initial=0.0, op0=ALU.mult, op1=ALU.add), r=["rtab", "F4"], w=[k1])
                k.op("dve", lambda g: g.tensor_tensor_scan(out=Wim[:], data0=rt, data1=Vim[:], initial=0.0, op0=ALU.mult, op1=ALU.add), r=["rtab", "F5"], w=[k2])
                t3_ = F[2]; t4_ = F[3]; Xrb = Bt[4]; Xib = Bt[5]
                PE_ = "dve"
                k.tt(t3_[:], Wre[:], csl, ALU.mult, r=[k1, "s5cos"], w=["F2"], e=PE_)
                k.tt(t4_[:], Wim[:], ssl, ALU.mult, r=[k2, "s5sin"], w=["F3"], e=PE_)
                k.tt(Xrb[:], t3_[:], t4_[:], ALU.subtract, r=["F2", "F3"], w=["B4"], e=PE_)
                k.cp(xst[:, hs], v3(t3_[:], 4)[:, :, 127], r=["F2"], w=["xst"], e=PE_)
                k.tt(xst[:, hs], xst[:, hs], v3(t4_[:], 4)[:, :, 127], ALU.subtract, r=["xst", "F3"], w=["xst"], e=PE_)
                k.tt(t3_[:], Wre[:], ssl, ALU.mult, r=[k1, "s5sin"], w=["F2"], e=PE_)
                k.tt(t4_[:], Wim[:], csl, ALU.mult, r=[k2, "s5cos"], w=["F3"], e=PE_)
                k.tt(Xib[:], t3_[:], t4_[:], ALU.add, r=["F2", "F3"], w=["B5"], e=PE_)
                k.cp(xst[:, 8 + half * 4:12 + half * 4], v3(t3_[:], 4)[:, :, 127], r=["F2"], w=["xst"], e=PE_)
                k.tt(xst[:, 8 + half * 4:12 + half * 4], xst[:, 8 + half * 4:12 + half * 4], v3(t4_[:], 4)[:, :, 127], ALU.add, r=["xst", "F3"], w=["xst"], e=PE_)
                o_ = bY[:, T * 128:(T + 1) * 128]
                for jj in range(4):
                    c = slice(jj * 128, (jj + 1) * 128); pc = slice(T * 512 + jj * 128, T * 512 + (jj + 1) * 128)
                    k.mm(o_, Cpr[:, pc], Xrb[:, c], start=(jj == 0), stop=False, r=["Cpr", "B4"], w=[bYk])
                    k.mm(o_, Cpi[:, pc], Xib[:, c], start=False, stop=False, r=["Cpi", "B5"], w=[bYk])
                k.mm(o_, Dfull[:, T * 128:(T + 1) * 128], uT[:, T * 128:(T + 1) * 128], start=False, stop=True, r=["Dfull", "B3"], w=[bYk])
            yield
            k.tag = 'gla'
            y = bY[:, 0:256]; y2 = F[2][:, 0:256]; ge = F[2][:, 256:512]; geb = Bt[3][:, 256:512]
            k.act(y2, y, AF.Square, r=[bYk], w=["F2"])
            k.ts(y2, y2, 0.044715, 1.0, op0=ALU.mult, op1=ALU.add, r=["F2"], w=["F2"])
            k.tt(y2, y2, y, ALU.mult, r=["F2", bYk], w=["F2"])
            k.act(y2, y2, AF.Sigmoid, scale=1.5957691216057308, r=["F2"], w=["F2"])
            k.tt(ge, y2, y, ALU.mult, r=["F2", bYk], w=["F2"])
            k.cp(geb, ge, r=["F2"], w=["B3"], e="act")
            bz, bzk = bank()
            for oi in range(2):
                for kc in range(2):
                    k.mm(bz[:, oi * 128:(oi + 1) * 128], wglu[:, kc, oi * 128:(oi + 1) * 128], geb[:, kc * 128:(kc + 1) * 128],
                         start=(kc == 0), stop=(kc == 1), r=["wglu", "B3"], w=[bzk])
            sgl = F[3][:, 0:256]
            for oi in range(2):
                k.act(sgl[:, oi * 128:(oi + 1) * 128], bz[:, oi * 128:(oi + 1) * 128], AF.Sigmoid, bias=bglu[:, oi:oi + 1], r=[bzk, "bglu"], w=["F3"])
            k.tt(mixT[:, 2:4, :], v3(ge, 2), v3(sgl, 2), ALU.mult, r=["F2", "F3"], w=["mixT"])

            unpin(bYk)

        def ret_gen(t):
            tsl = slice(t * 128, (t + 1) * 128)
            ck("s5")
            bq, bqk = bank(pin=True); bkk_, bkkk = bank(pin=True)
            for i in range(2):
                proj_fm(bq, bqk, i, 1040 + 128 * i)
                proj_fm(bkk_, bkkk, i, 1296 + 128 * i)
            xq = Bt[5][:, 0:256]; xk = Bt[5][:, 256:512]
            k.cp(xq, bq[:, 0:256], r=[bqk], w=["B5"], e="act")
            k.cp(xk, bkk_[:, 0:256], r=[bkkk], w=["B5"], e="act")
            for i in range(2):
                k.mm(bq[:, 256 + i * 128:256 + (i + 1) * 128], Rm[:], xq[:, i * 128:(i + 1) * 128], r=["Rm", "B5"], w=[bqk])
                k.mm(bkk_[:, 256 + i * 128:256 + (i + 1) * 128], Rm[:], xk[:, i * 128:(i + 1) * 128], r=["Rm", "B5"], w=[bkkk])
            k.dma("sp", ropeCt[:], ropeC[:, tsl], r=["ropeC"], w=["ropeCt"])
            k.dma("sp", ropeSt[:], ropeS[:, tsl], r=["ropeS"], w=["ropeSt"])
            yield
            ck("r0")
            cosb = bc_mid(ropeCt[:], 2); sinb = bc_mid(ropeSt[:], 2)
            qT = Bt[0][:, 0:256]; kT = Bt[0][:, 256:512]
            ta = F[0][:, 0:256]; tb = F[0][:, 256:512]
            k.tt(v3(ta, 2), v3(bq[:, 0:256], 2), cosb, ALU.mult, r=[bqk, "ropeCt"], w=["F0"])
            k.tt(v3(tb, 2), v3(bq[:, 256:512], 2), sinb, ALU.mult, r=[bqk, "ropeSt"], w=["F0"])
            k.tt(qT, ta, tb, ALU.add, r=["F0", "F0"], w=["B0"])
            k.stt(v3(ta, 2), v3(bkk_[:, 0:256], 2), 0.125, cosb, ALU.mult, ALU.mult, r=[bkkk, "ropeCt"], w=["F0"])
            k.stt(v3(tb, 2), v3(bkk_[:, 256:512], 2), 0.125, sinb, ALU.mult, ALU.mult, r=[bkkk, "ropeSt"], w=["F0"])
            k.tt(kT, ta, tb, ALU.add, r=["F0", "F0"], w=["B0"])
            unpin(bqk); unpin(bkkk)
            pb, pk = bbank()
            for i in range(2):
                k.tr(pb[:, i * 128:(i + 1) * 128], kT[:, i * 128:(i + 1) * 128], ident_b[:], r=["B0", "ident_b"], w=[pk])
            ck("r1")
            Kz = Bt[1][:, 0:256]
            k.tt(v3(Kz, 4), v3(pb[:, 0:256], 4), bc_last(zeta[:], 64), ALU.mult, r=[pk, "zeta"], w=["B1"])
            b2, b2k = bank()
            proj_tm(b2, b2k, 0, 1552, 512)
            v_bf = Bt[1][:, 256:512]; sg = F[1][:, 0:256]
            k.cp(v_bf, b2[:, 0:256], r=[b2k], w=["B1"], e="act")
            k.act(sg, b2[:, 256:512], AF.Silu, r=[b2k], w=["F1"])
            qTm = Bt[6]; kTm = Bt[7]
            k.tt(v4(qTm[:]), src4(qT), msk4(mlohi[:, 0:2]), ALU.mult)
            k.tt(v4(kTm[:]), src4(kT), msk4(mlohi[:, 0:2]), ALU.mult)
            b3, b3k = bank()
            for hh in range(4):
                i = hh // 2
                k.mm(b3[:, hh * 128:(hh + 1) * 128], kTm[:, hh * 128:(hh + 1) * 128], qT[:, i * 128:(i + 1) * 128], r=["B0", "B7"], w=[b3k])
            ck("r2")
            attn = Bt[2]
            k.tt(attn[:], b3[:], dmask[:], ALU.mult, r=[b3k, "dmask"], w=["B2"])
            b4, b4k = bank(pin=True)
            for hh in range(4):
                i = hh // 2; ps_ = slice(64 * (hh % 2), 64 * (hh % 2) + 64); vs = slice(64 * hh, 64 * hh + 64)
                k.mm(b4[:, vs], attn[:, hh * 128:(hh + 1) * 128], v_bf[:, vs], r=["B2", "B1"], w=[b4k])
                k.mm(b4[:, 256 + 64 * hh:256 + 64 * hh + 64], qTm[:, hh * 128:(hh + 1) * 128], Sret_b[:, i * 64:(i + 1) * 64], r=["B6", "Sret_b"], w=[b4k])
            yield
            k.tag = 'r3'
            o_ = F[3][:, 0:256]; t1 = F[3][:, 256:512]
            k.tt(v3(o_, 4), v3(b4[:, 256:512], 4), bc_last(xi[:], 64), ALU.mult, r=[b4k, "xi"], w=["F3"])
            k.tt(o_, o_, b4[:, 0:256], ALU.add, r=["F3", b4k], w=["F3"])
            k.op("dve", lambda g: g.tensor_reduce(out=small[:, 8:12], in_=v3(o_, 4), axis=AX.X, op=ALU.add), r=["F3"], w=["sm_m"])
            k.ts(small[:, 8:12], small[:, 8:12], 1.0 / 64, r=["sm_m"], w=["sm_m"])
            k.tt(v3(o_, 4), v3(o_, 4), bc_last(small[:, 8:12], 64), ALU.subtract, r=["F3", "sm_m"], w=["F3"])
            k.act(t1, o_, AF.Square, r=["F3"], w=["F3"])
            k.op("dve", lambda g: g.tensor_reduce(out=small[:, 12:16], in_=v3(t1, 4), axis=AX.X, op=ALU.add), r=["F3"], w=["sm_v"])
            rstd_from(small[:, 16:20], "sm_r4", small[:, 12:16], "sm_v", 1.0 / 64)
            k.tt(v3(t1, 4), v3(o_, 4), bc_last(small[:, 16:20], 64), ALU.mult, r=["F3", "sm_r4"], w=["F3"])
            k.tt(t1, t1, gn_ret[:], ALU.mult, r=["F3", "gn_ret"], w=["F3"])
            ck("r3")
            mixc = Bt[3][:, 0:256]
            k.tt(mixc, t1, sg, ALU.mult, r=["F3", "F1"], w=["B3"])
            to_mixT(mixc, "B3", 4)
            b5, b5k = bank()
            for i in range(2):
                k.mm(b5[:, i * 128:(i + 1) * 128], Kz[:, i * 128:(i + 1) * 128], v_bf[:, i * 128:(i + 1) * 128], r=["B1", "B1"], w=[b5k])
            for i in range(2):
                dsl = F[0][:, i * 64:(i + 1) * 64]
                k.ts(dsl, b5[:, i * 128:i * 128 + 64], mlohi[:, 0:1], r=[b5k, "mlohi"], w=["F0"])
                k.stt(dsl, b5[:, i * 128 + 64:(i + 1) * 128], mlohi[:, 1:2], dsl, ALU.mult, ALU.add, r=[b5k, "mlohi", "F0"], w=["F0"])
                k.stt(Sret[:, i * 64:(i + 1) * 64], Sret[:, i * 64:(i + 1) * 64], g128[:, i:i + 1], dsl, ALU.mult, ALU.add,
                      r=["Sret", "g128", "F0"], w=["Sret"])
            k.cp(Sret_b[:], Sret[:], r=["Sret"], w=["Sret_b"], e="act")

            unpin(b4k)

        def gdn_gen(t):
            ck("ret")
            bA, bAk = bank(); bB, bBk = bank()
            for i in range(4):
                proj_fm(bA, bAk, i, 2064 + 128 * i)
            for i in range(2):
                proj_fm(bB, bBk, i, 2064 + 128 * (4 + i))
            k.cp(xc[:, 0:4, 3:131], v3(bA[:], 4), r=[bAk], w=["xc"], e="act")
            k.cp(xc[:, 4:6, 3:131], v3(bB[:, 0:256], 2), r=[bBk], w=["xc"], e="act")
            yield
            k.tag = 'gdn.conv'
            bC, bCk = bank(); bD, bDk = bank()
            for i in range(6):
                ob = bC[:, i * 128:(i + 1) * 128] if i < 4 else bD[:, (i - 4) * 128:(i - 3) * 128]
                obk = bCk if i < 4 else bDk
                for j in range(4):
                    k.mm(ob, cdiag[:, (i * 4 + j) * 128:(i * 4 + j + 1) * 128], xc[:, i, j:j + 128], start=(j == 0), stop=(j == 3), r=["cdiag", "xc"], w=[obk])
            k.cp(xc[:, :, 0:3], xc[:, :, 128:131], r=["xc"], w=["xc"])
            qk = F[0]
            vTb = Bt[0][:, 0:256]
            k.act(qk[:], bC[:], AF.Silu, r=[bCk], w=["F0"])
            k.act(vTb, bD[:, 0:256], AF.Silu, r=[bDk], w=["B0"])
            sq = F[1]
            k.act(sq[:], qk[:], AF.Square, r=["F0"], w=["F1"])
            bn, bnk = bank()
            for i in range(4):
                k.mm(bn[:, i * 128:(i + 1) * 128], blk1[:], sq[:, i * 128:(i + 1) * 128], r=["blk1", "F1"], w=[bnk])
            rinv = F[1]
            k.act(rinv[:], bn[:], AF.Ln, bias=EPS, r=[bnk], w=["F1"])
            k.act(rinv[:], rinv[:], AF.Exp, scale=-0.5, r=["F1"], w=["F1"])
            qkT = Bt[1]
            k.stt(qkT[:, 0:256], qk[:, 0:256], 0.125, rinv[:, 0:256], ALU.mult, ALU.mult, r=["F0", "F1"], w=["B1"])
            k.tt(qkT[:, 256:512], qk[:, 256:512], rinv[:, 256:512], ALU.mult, r=["F0", "F1"], w=["B1"])
            pb, pk = bbank()
            for i in range(2):
                k.tr(pb[:, i * 128:(i + 1) * 128], qkT[:, 256 + i * 128:256 + (i + 1) * 128], ident_b[:], r=["B1", "ident_b"], w=[pk])
                k.tr(pb[:, 256 + i * 128:256 + (i + 1) * 128], vTb[:, i * 128:(i + 1) * 128], ident_b[:], r=["B0", "ident_b"], w=[pk])
            k.tag = 'gdn.gates'
            b2, b2k = bank()
            proj_tm(b2, b2k, 0, 2832, 264)
            beta = small[:, 20:24]; gg = small[:, 24:28]; lnb = small[:, 28:32]
            sg = F[2][:, 0:256]
            k.act(beta, b2[:, 0:4], AF.Sigmoid, r=[b2k], w=["sm_beta"])
            k.act(sg, b2[:, 8:264], AF.Silu, r=[b2k], w=["F2"])
            k.tt(gg, b2[:, 4:8], dtb_b[:], ALU.add, r=[b2k, "dtb_b"], w=["sm_g"])
            k.act(gg, gg, AF.Exp, r=["sm_g"], w=["sm_g"])
            k.act(gg, gg, AF.Ln, bias=1.0, r=["sm_g"], w=["sm_g"])
            k.tt(gg, gg, nexpA[:], ALU.mult, r=["sm_g", "nexpA"], w=["sm_g"])
            k.act(lnb, beta, AF.Ln, r=["sm_beta"], w=["sm_lnb"])
            bg, bgk = bank()
            k.mm(bg[:, 0:4], triL[:], gg, r=["triL", "sm_g"], w=[bgk])
            k.mm(bg[:, 4:8], ones_f[:], gg, r=["ones_f", "sm_g"], w=[bgk])
            gg2 = gg.rearrange("p (i two) -> p two i", two=2)
            k.mm(bg[:, 8:10], ones_lo[:], gg2[:, 0, :], start=True, stop=False, r=["ones_lo", "sm_g"], w=[bgk])
            k.mm(bg[:, 8:10], ones_hi[:], gg2[:, 1, :], start=False, stop=True, r=["ones_hi", "sm_g"], w=[bgk])
            gcum = small[:, 36:40]; eg = small[:, 40:44]; eke4 = small[:, 44:48]; dlS = small[:, 48:50]; bexp = small[:, 52:56]
            k.cp(gcum, bg[:, 0:4], r=[bgk], w=["sm_gc"])
            k.act(eg, gcum, AF.Exp, r=["sm_gc"], w=["sm_eg"])
            k.tt(eke4, bg[:, 4:8], gcum, ALU.subtract, r=[bgk, "sm_gc"], w=["sm_eke"])
            k.act(eke4, eke4, AF.Exp, r=["sm_eke"], w=["sm_eke"])
            k.act(dlS, bg[:, 8:10], AF.Exp, r=[bgk], w=["sm_dls"])
            k.tt(bexp, beta, eg, ALU.mult, r=["sm_beta", "sm_eg"], w=["sm_bexp"])
            Ru = Bt[3][:, 0:256]; Rw = Bt[3][:, 256:512]; Kend = Bt[4][:, 0:256]
            k.tt(v3(Ru, 4), v3(pb[:, 256:512], 4), bc_last(beta, 64), ALU.mult, r=[pk, "sm_beta"], w=["B3"])
            k.tt(v3(Rw, 4), v3(pb[:, 0:256], 4), bc_last(bexp, 64), ALU.mult, r=[pk, "sm_bexp"], w=["B3"])
            k.tt(v3(Kend, 4), v3(pb[:, 0:256], 4), bc_last(eke4, 64), ALU.mult, r=[pk, "sm_eke"], w=["B4"])
            k.tag = 'gdn.decay'
            rhsA = F[3]; rhsB = F[4]
            k.tt(v3(rhsA[:], 4), bc_mid(triL[:], 4), bc_last(gg, 128), ALU.mult, r=["triL", "sm_g"], w=["F3"])
            k.tt(v3(rhsB[:], 4), bc_mid(ident_f[:], 4), bc_last(lnb, 128), ALU.mult, r=["ident_f", "sm_lnb"], w=["F4"])
            k.tt(rhsB[:], rhsB[:], rhsA[:], ALU.add, r=["F3", "F4"], w=["F4"])
            bD1, bD1k = bank(); bD2, bD2k = bank()
            for hh in range(4):
                c = slice(hh * 128, (hh + 1) * 128)
                k.mm(bD1[:, c], Ust[:], rhsA[:, c], r=["Ust", "F3"], w=[bD1k])
                k.mm(bD2[:, c], Ust[:], rhsB[:, c], r=["Ust", "F4"], w=[bD2k])
            E1 = F[5]; E2 = F[6]
            k.act(E1[:], bD1[:], AF.Exp, r=[bD1k], w=["F5"])
            k.act(E2[:], bD2[:], AF.Exp, r=[bD2k], w=["F6"])
            bK, bKk = bank(); bQ, bQk = bank()
            kTm = Bt[7]
            k.tt(v4(kTm[:]), src4(qkT[:, 256:512]), msk4(mlohi[:, 0:2]), ALU.mult)
            for hh in range(4):
                i = hh // 2; c = slice(hh * 128, (hh + 1) * 128)
                k.mm(bK[:, c], kTm[:, c], qkT[:, 256 + i * 128:256 + (i + 1) * 128], r=["B1", "B7"], w=[bKk])
                k.mm(bQ[:, c], kTm[:, c], qkT[:, i * 128:(i + 1) * 128], r=["B1", "B7"], w=[bQk])
            aqk = Bt[2]; Mb = Bt[5]; Nb = Bt[10]
            k.tt(E1[:], E1[:], bQ[:], ALU.mult, r=["F5", bQk], w=["F5"])
            k.tt(v3(aqk[:], 4), v3(E1[:], 4), bc_mid(triL[:], 4), ALU.mult, r=["F5", "triL"], w=["B2"])
            k.tt(E2[:], E2[:], bK[:], ALU.mult, r=["F6", bKk], w=["F6"])
            k.tt(v3(Mb[:], 4), v3(E2[:], 4), bc_mid(negU[:], 4), ALU.mult, r=["F6", "negU"], w=["B5"])
            pb2, pk2 = bbank()
            for hh in range(4):
                c = slice(hh * 128, (hh + 1) * 128)
                k.tr(pb2[:, c], Mb[:, c], ident_b[:], r=["B5", "ident_b"], w=[pk2])
            k.cp(Nb[:], pb2[:, 0:512], r=[pk2], w=["B10"], e="act")
            k.tag = 'gdn.solve'
            Ttb_t = Bt[6]; Qb = Bt[0]; Db = Bt[9]; Pb2 = Bt[11]
            M0 = Bt[7]; N0 = Bt[8]
            k.tt(v3(M0[:], 4), v3(Mb[:], 4), bc_mid(gmask[:, 0:128], 4), ALU.mult, r=["B5", "gmask"], w=["B7"])
            k.tt(v3(N0[:], 4), v3(Nb[:], 4), bc_mid(gmask[:, 0:128], 4), ALU.mult, r=["B10", "gmask"], w=["B8"])
            k.tt(v3(Ttb_t[:], 4), v3(M0[:], 4), bc_mid(ident_b[:], 4), ALU.add, r=["B7", "ident_b"], w=["B6"])
            bM, bMk = bank(); bN, bNk = bank()
            for hh in range(4):
                c = slice(hh * 128, (hh + 1) * 128)
                k.mm(bN[:, c], M0[:, c], N0[:, c], r=["B7", "B8"], w=[bNk])
                k.mm(bM[:, c], N0[:, c], M0[:, c], r=["B7", "B8"], w=[bMk])
            k.cp(Qb[:], bN[:], r=[bNk], w=["B0"], e="act")
            k.cp(Db[:], bM[:], r=[bMk], w=["B9"])
            bT, bTk = bank()
            for hh in range(4):
                c = slice(hh * 128, (hh + 1) * 128)
                k.mm(bT[:, c], ident_b[:], Ttb_t[:, c], start=True, stop=False, r=["ident_b", "B6"], w=[bTk])
                k.mm(bT[:, c], Qb[:, c], Ttb_t[:, c], start=False, stop=True, r=["B0", "B6"], w=[bTk])
            k.cp(Ttb_t[:], bT[:], r=[bTk], w=["B6"], e="act")
            bN, bNk = bank()
            for hh in range(4):
                c = slice(hh * 128, (hh + 1) * 128)
                k.mm(bN[:, c], Db[:, c], Qb[:, c], r=["B9", "B0"], w=[bNk])
            k.cp(N0[:], bN[:], r=[bNk], w=["B8"])
            bT, bTk = bank()
            for hh in range(4):
                c = slice(hh * 128, (hh + 1) * 128)
                k.mm(bT[:, c], ident_b[:], Ttb_t[:, c], start=True, stop=False, r=["ident_b", "B6"], w=[bTk])
                k.mm(bT[:, c], N0[:, c], Ttb_t[:, c], start=False, stop=True, r=["B8", "B6"], w=[bTk])
            k.cp(Ttb_t[:], bT[:], r=[bTk], w=["B6"], e="act")
            pb3, pk3 = bbank()
            for hh in range(4):
                c = slice(hh * 128, (hh + 1) * 128)
                k.tr(pb3[:, c], Ttb_t[:, c], ident_b[:], r=["B6", "ident_b"], w=[pk3])
            k.cp(Db[:], pb3[:, 0:512], r=[pk3], w=["B9"])
            Nm = Bt[8]; Mm = Bt[7]
            for li_, m_ in enumerate((8, 16, 32, 64)):
                last = (m_ == 64)
                k.tt(v3(Nm[:], 4), v3(Nb[:], 4), bc_mid(gmask[:, (1 + li_) * 128:(2 + li_) * 128], 4), ALU.mult, r=["B10", "gmask"], w=["B8"])
                if not last:
                    k.tt(v3(Mm[:], 4), v3(Mb[:], 4), bc_mid(gmask[:, (5 + li_) * 128:(6 + li_) * 128], 4), ALU.mult, r=["B5", "gmask"], w=["B7"])
                bP, bPk = bank()
                for hh in range(4):
                    c = slice(hh * 128, (hh + 1) * 128)
                    k.mm(bP[:, c], Nm[:, c], Ttb_t[:, c], r=["B8", "B6"], w=[bPk])
                if not last:
                    bQ2, bQ2k = bank()
                    for hh in range(4):
                        c = slice(hh * 128, (hh + 1) * 128)
                        k.mm(bQ2[:, c], Mm[:, c], Db[:, c], r=["B7", "B9"], w=[bQ2k])
                k.cp(Qb[:], bP[:], r=[bPk], w=["B0"], e="act")
                if not last:
                    k.cp(Pb2[:], bQ2[:], r=[bQ2k], w=["B11"])
                bT, bTk = bank()
                for hh in range(4):
                    c = slice(hh * 128, (hh + 1) * 128)
                    k.mm(bT[:, c], ident_b[:], Ttb_t[:, c], start=True, stop=False, r=["ident_b", "B6"], w=[bTk])
                    k.mm(bT[:, c], Db[:, c], Qb[:, c], start=False, stop=True, r=["B9", "B0"], w=[bTk])
                if not last:
                    bD_, bD_k = bank()
                    for hh in range(4):
                        c = slice(hh * 128, (hh + 1) * 128)
                        k.mm(bD_[:, c], ident_b[:], Db[:, c], start=True, stop=False, r=["ident_b", "B9"], w=[bD_k])
                        k.mm(bD_[:, c], Ttb_t[:, c], Pb2[:, c], start=False, stop=True, r=["B6", "B11"], w=[bD_k])
                k.cp(Ttb_t[:], bT[:], r=[bTk], w=["B6"], e="act")
                if not last:
                    k.cp(Db[:], bD_[:], r=[bD_k], w=["B9"])
            k.tag = 'gdn.state'
            bW, bWk = bank()
            nwT = Bt[9]
            for hh in range(4):
                i = hh // 2; c = slice(hh * 128, (hh + 1) * 128)
                k.mm(bW[:, c], Rw[:, i * 128:(i + 1) * 128], Ttb_t[:, c], r=["B3", "B6"], w=[bWk])
            k.tt(v4(nwT[:]), v4(bW[:]), msk4(nmlohi[:, 0:2]), ALU.mult)
            bV, bVk = bank()
            for hh in range(4):
                i = hh // 2; ps_ = slice(64 * (hh % 2), 64 * (hh % 2) + 64); c = slice(hh * 128, (hh + 1) * 128); vs = slice(64 * hh, 64 * hh + 64)
                k.mm(bV[:, vs], Ttb_t[:, c], Ru[:, vs], start=True, stop=False, r=["B6", "B3"], w=[bVk])
                k.mm(bV[:, vs], nwT[:, c], Sgdn_b[:, i * 64:(i + 1) * 64], start=False, stop=True, r=["B9", "Sgdn_b"], w=[bVk])
            vnew = Bt[0][:, 256:512]
            k.cp(vnew, bV[:, 0:256], r=[bVk], w=["B0"], e="act")
            qTm = Bt[8]
            k.tt(v4(qTm[:]), src4(qkT[:, 0:256]), msk4(mlohi[:, 0:2]), ALU.mult)
            bO, bOk = bank(pin=True)
            for hh in range(4):
                i = hh // 2; ps_ = slice(64 * (hh % 2), 64 * (hh % 2) + 64); c = slice(hh * 128, (hh + 1) * 128); vs = slice(64 * hh, 64 * hh + 64)
                k.mm(bO[:, vs], aqk[:, c], vnew[:, vs], r=["B2", "B0"], w=[bOk])
                k.mm(bO[:, 256 + 64 * hh:256 + 64 * hh + 64], qTm[:, c], Sgdn_b[:, i * 64:(i + 1) * 64], r=["B8", "Sgdn_b"], w=[bOk])
            yield
            k.tag = 'gdn.state'
            o_ = F[3][:, 0:256]; t1 = F[3][:, 256:512]
            k.tt(v3(o_, 4), v3(bO[:, 256:512], 4), bc_last(eg, 64), ALU.mult, r=[bOk, "sm_eg"], w=["F3"])
            k.tt(o_, o_, bO[:, 0:256], ALU.add, r=["F3", bOk], w=["F3"])
            k.act(t1, o_, AF.Square, r=["F3"], w=["F3"])
            k.op("dve", lambda g: g.tensor_reduce(out=small[:, 12:16], in_=v3(t1, 4), axis=AX.X, op=ALU.add), r=["F3"], w=["sm_v"])
            rstd_from(small[:, 16:20], "sm_r4", small[:, 12:16], "sm_v", 1.0 / 64)
            k.tt(v3(t1, 4), v3(o_, 4), bc_last(small[:, 16:20], 64), ALU.mult, r=["F3", "sm_r4"], w=["F3"])
            k.tt(v3(t1, 4), v3(t1, 4), bc_mid(gn_gdn[:], 4), ALU.mult, r=["F3", "gn_gdn"], w=["F3"])
            mixd = Bt[2][:, 0:256]
            k.tt(mixd, t1, sg, ALU.mult, r=["F3", "F2", "B2"], w=["B2"])
            to_mixT(mixd, "B2", 6)
            bS, bSk = bank()
            for i in range(2):
                k.mm(bS[:, i * 128:(i + 1) * 128], Kend[:, i * 128:(i + 1) * 128], vnew[:, i * 128:(i + 1) * 128], r=["B4", "B0"], w=[bSk])
            for i in range(2):
                dsl = F[0][:, i * 64:(i + 1) * 64]
                k.ts(dsl, bS[:, i * 128:i * 128 + 64], mlohi[:, 0:1], r=[bSk, "mlohi"], w=["F0"])
                k.stt(dsl, bS[:, i * 128 + 64:(i + 1) * 128], mlohi[:, 1:2], dsl, ALU.mult, ALU.add, r=[bSk, "mlohi", "F0"], w=["F0"])
                k.stt(Sgdn[:, i * 64:(i + 1) * 64], Sgdn[:, i * 64:(i + 1) * 64], dlS[:, i:i + 1], dsl, ALU.mult, ALU.add,
                      r=["Sgdn", "sm_dls", "F0"], w=["Sgdn"])
            k.cp(Sgdn_b[:], Sgdn[:], r=["Sgdn"], w=["Sgdn_b"], e="act")

            unpin(bOk)

        def outproj(t):
            tsl = slice(t * 128, (t + 1) * 128)
            ck("gdn")
            if dbg and l == 0:
                k.cp(F[0][:, 0:512], mixT[:, 0:4, :].rearrange("p a b -> p (a b)"), r=["mixT"], w=["F0"])
                k.cp(F[1][:, 0:512], mixT[:, 4:8, :].rearrange("p a b -> p (a b)"), r=["mixT"], w=["F1"])
                k.dma("sp", dbgo[:, 0:4, tsl], v3(F[0][:], 4), r=["F0"], w=["dbgo"])
                k.dma("sp", dbgo[:, 4:8, tsl], v3(F[1][:], 4), r=["F1"], w=["dbgo"])
            k.tag = 'outproj'
            for nh in range(2):
                bo, bok = bank()
                for kc in range(8):
                    k.mm(bo[:], mixT[:, kc, :], wout[:, kc, nh * 512:(nh + 1) * 512], start=(kc == 0), stop=(kc == 7), r=["mixT", "wout"], w=[bok])
                k.tt(h[:, t, nh * 512:(nh + 1) * 512], h[:, t, nh * 512:(nh + 1) * 512], bo[:], ALU.add, r=["h", bok], w=["h"])

        def step(g):
            try:
                next(g)
            except StopIteration:
                pass

        k.tag = 'normT'
        norm_T(h[:, 0, :], "gcol", hnT, "hnT")
        gcur = gla_gen(0); step(gcur)
        for t in range(NT):
            step(gcur)
            s5g = s5_gen(t); step(s5g)
            step(gcur)
            step(s5g)
            rtg = ret_gen(t); step(rtg)
            step(s5g)
            step(rtg)
            gdg = gdn_gen(t); step(gdg)
            step(rtg)
            step(gdg)
            if t + 1 < NT:
                k.tag = 'normT'
                norm_T(h[:, t + 1, :], "gcol", hnT, "hnT")
                gcur = gla_gen(t + 1); step(gcur)
            step(gdg)
            outproj(t)

        ck("phaseA")
        k.barrier()
        colload(gcol[:, 0:8], "gcol", D["norm_ffn"][l].rearrange("(a b) -> a b", b=128), 8)
        for t in range(NT):
            norm_T(h[:, t, :], "gcol", hn2T, "hn2T", ntok_off=t * 128)
        GT = min(4, NT)
        NG = NT // GT
        NTOK = GT * 128
        pieces = []
        f0 = 0
        while f0 < 22:
            fn = min(3, 22 - f0)
            pieces.append((f0, fn)); f0 += fn
        for pi, (f0, fn) in enumerate(pieces):
            bsel = pi % 2
            wupb = W[:, bsel * 9216:bsel * 9216 + 6144].rearrange("p (k n) -> p k n", k=8)
            wdnb = W[:, bsel * 9216 + 6144:bsel * 9216 + 9216].rearrange("p (f n) -> p f n", f=3)
            actb = W[:, 28672 + bsel * 1536:28672 + (bsel + 1) * 1536].rearrange("p (f n) -> p f n", f=3)
            ku, kd, ka = f"wup{bsel}", f"wdn{bsel}", f"actT{bsel}"
            for kc in range(8):
                k.dma("pool", wupb[:, kc, 0:fn * 128], D["w_ffn_up"][l, kc * 128:(kc + 1) * 128, f0 * 128:(f0 + fn) * 128], w=[ku])
                k.dma("pool", wupb[:, kc, 384:384 + fn * 128], D["w_ffn_up"][l, kc * 128:(kc + 1) * 128, FFH + f0 * 128:FFH + (f0 + fn) * 128], w=[ku])
            for fi in range(fn):
                k.dma("pool", wdnb[:, fi, :], D["w_ffn_down"][l, (f0 + fi) * 128:(f0 + fi + 1) * 128, :], w=[kd])
            if pi == 0:
                for kc in range(8):
                    k.dma("pool", wpg[:, kc, :], D["w_ple_gate"][l, kc * 128:(kc + 1) * 128, :], w=["wpg"])
                k.dma("pool", wpp[:], D["w_ple_proj"][l].rearrange("(kc q) n -> q kc n", q=128), w=["wpp"])
            for gi in range(NG):
                g0 = gi * NTOK
                for fi in range(fn):
                    bg_, bgk_ = bank(); bu_, buk_ = bank()
                    for kc in range(8):
                        k.mm(bg_[:, 0:NTOK], wupb[:, kc, fi * 128:(fi + 1) * 128], hn2T[:, kc, g0:g0 + NTOK], start=(kc == 0), stop=(kc == 7), r=[ku, "hn2T"], w=[bgk_])
                    for kc in range(8):
                        k.mm(bu_[:, 0:NTOK], wupb[:, kc, 384 + fi * 128:384 + (fi + 1) * 128], hn2T[:, kc, g0:g0 + NTOK], start=(kc == 0), stop=(kc == 7), r=[ku, "hn2T"], w=[buk_])
                    sgt = Bt[fi % 2]
                    k.act(sgt[:, 0:NTOK], bg_[:, 0:NTOK], AF.Silu, r=[bgk_], w=[f"B{fi % 2}"])
                    k.tt(actb[:, fi, 0:NTOK], sgt[:, 0:NTOK], bu_[:, 0:NTOK], ALU.mult, r=[f"B{fi % 2}", buk_], w=[ka])
                for tt_ in range(GT):
                    t = gi * GT + tt_
                    for nh in range(2):
                        bo, bok = bank()
                        for fi in range(fn):
                            k.mm(bo[:], actb[:, fi, tt_ * 128:(tt_ + 1) * 128], wdnb[:, fi, nh * 512:(nh + 1) * 512], start=(fi == 0), stop=(fi == fn - 1), r=[ka, kd], w=[bok])
                        k.tt(h[:, t, nh * 512:(nh + 1) * 512], h[:, t, nh * 512:(nh + 1) * 512], bo[:], ALU.add, r=["h", bok], w=["h"])
        ck("ffn")
        k.barrier()
        colload(gcol[:, 0:8], "gcol", D["norm_ple"][l].rearrange("(a b) -> a b", b=128), 8)
        for t in range(NT):
            norm_T(h[:, t, :], "gcol", hn2T, "hn2T", ntok_off=t * 128)
        ptmp = A2[:, 8192:8704]
        for t in range(NT):
            pin = Bt[t % 2]; pT = Bt[2 + t % 2]
            k.dma("pool", pin[:, 0:256], D["p"][l, t * 128:(t + 1) * 128, :], w=[f"B{t % 2}"])
            pb, pk = bbank()
            for i in range(2):
                k.tr(pb[:, i * 128:(i + 1) * 128], pin[:, i * 128:(i + 1) * 128], ident_b[:], r=[f"B{t % 2}", "ident_b"], w=[pk])
            k.cp(pT[:, 0:256], pb[:, 0:256], r=[pk], w=[f"B{2 + t % 2}"], e="act")
            for nh in range(2):
                bg_, bgk_ = bank(); bp_, bpk_ = bank()
                for kc in range(8):
                    k.mm(bg_[:], hn2T[:, kc, t * 128:(t + 1) * 128], wpg[:, kc, nh * 512:(nh + 1) * 512], start=(kc == 0), stop=(kc == 7), r=["hn2T", "wpg"], w=[bgk_])
                for kc in range(2):
                    k.mm(bp_[:], pT[:, kc * 128:(kc + 1) * 128], wpp[:, kc, nh * 512:(nh + 1) * 512], start=(kc == 0), stop=(kc == 1), r=[f"B{2 + t % 2}", "wpp"], w=[bpk_])
                k.act(ptmp, bg_[:], AF.Sigmoid, r=[bgk_], w=["ptmp"])
                k.tt(ptmp, ptmp, bp_[:], ALU.mult, r=["ptmp", bpk_], w=["ptmp"])
                k.tt(h[:, t, nh * 512:(nh + 1) * 512], h[:, t, nh * 512:(nh + 1) * 512], ptmp, ALU.add, r=["h", "ptmp"], w=["h"])

    k.barrier()
    ck("ple")
    k.dma("sp", F[2][:], D["norm_final"][0:1, 0:512].partition_broadcast(128), w=["F2"])
    k.dma("sp", F[3][:], D["norm_final"][0:1, 512:1024].partition_broadcast(128), w=["F3"])
    for t in range(NT):
        ss = small[:, 0:1]
        k.act(hn[:], h[:, t, :], AF.Square, accum_out=ss, r=["h"], w=["hn", "sm_ss"])
        rstd_from(small[:, 1:2], "sm_rs", ss, "sm_ss", 1.0 / DM)
        for nh in range(2):
            ob = F[4 + nh]
            k.stt(ob[:], h[:, t, nh * 512:(nh + 1) * 512], small[:, 1:2], F[2 + nh][:], ALU.mult, ALU.mult, r=["h", "sm_rs", f"F{2 + nh}"], w=[f"F{4 + nh}"])
            k.dma("sp", out[t * 128:(t + 1) * 128, nh * 512:(nh + 1) * 512], ob[:], r=[f"F{4 + nh}"], w=["out"])
    k.finish("sp")
    _K[0] = k
    print("built: insts", k.cnt, "waits", k.nwaits, "sbuf_left", nc.sbuf_bytes_remaining, flush=True)
    return nc


_CACHE = {}


def kernel(**inputs):
    L, depth = 2048, 2
    if "nc" not in _CACHE:
        _CACHE["nc"] = build(L, depth)
    nc = _CACHE["nc"]
    shared = {}
    for nm, _ in PSH:
        shared[nm] = np.ascontiguousarray(np.asarray(inputs[nm], dtype=np.float32))
    shared["norm_final"] = np.ascontiguousarray(np.asarray(inputs["norm_final"], dtype=np.float32).reshape(1, 1024))
    x = np.asarray(inputs["x"], dtype=np.float32)
    p = np.asarray(inputs["p"], dtype=np.float32)
    pos = np.asarray(inputs["positions"]).astype(np.int32)
    in_maps = []
    for b in range(8):
        m = dict(shared)
        m["x"] = np.ascontiguousarray(x[b])
        m["p"] = np.ascontiguousarray(p[:, b])
        m["positions"] = np.ascontiguousarray(pos[b:b + 1])
        in_maps.append(m)
    res = run_bass_kernel_spmd(nc, in_maps, core_ids=list(range(8)))
    return np.stack([np.asarray(r["out"], dtype=np.float32) for r in res.results], axis=0)
```
